# Optimizing a Trainium2 kernel written in Bass

```python
import jax, jax.numpy as jnp
from jax import lax
import numpy as np

D_MODEL = 1024
BATCH = 8
SEQ = 8192
DEPTH = 4

CTX_LEN = 256
GRID_W = 64

N_ADA = 9
D_FF = 2816
NORM_EPS = 1e-6
MIX_WIDTH = 512

ATTN_HEADS = 8
ATTN_KV_HEADS = 2
HEAD_DIM = 64
WINDOW = 128
ATTN_BLOCK = 128
ROPE_BASE = 10000.0
NEG_INF = -1e30

LRU_WIDTH = MIX_WIDTH
LRU_BLOCKS = 8
LRU_CONV = 4
LRU_C = 8.0

RWKV_HEADS = 8
RWKV_HEAD = 64
RWKV_WIDTH = RWKV_HEADS * RWKV_HEAD
RWKV_DECAY_RANK = 64
RWKV_A_RANK = 64
RWKV_G_RANK = 128
RWKV_LN_EPS = 64e-5

N_BRANCH = 3
ATTN_Q_COLS = ATTN_HEADS * HEAD_DIM
ATTN_KV_COLS = ATTN_KV_HEADS * HEAD_DIM
ATTN_COLS = ATTN_Q_COLS + 2 * ATTN_KV_COLS
LRU_COLS = 2 * LRU_WIDTH
RWKV_COLS = 3 * RWKV_WIDTH + 2 * RWKV_DECAY_RANK + 2 * RWKV_A_RANK + RWKV_G_RANK
GATE_COLS = N_BRANCH * D_MODEL
IN_COLS = ATTN_COLS + LRU_COLS + RWKV_COLS + GATE_COLS

kernel_name = 'hybrid_dit_gqa_rglru_rwkv7_macaron'


def _rms(x, g):
    xf = x.astype(jnp.float32)
    y = xf * lax.rsqrt(jnp.mean(xf * xf, axis=-1, keepdims=True) + NORM_EPS)
    return (y * g.astype(jnp.float32)).astype(x.dtype)


def _modulate(x, g, shift, scale):
    return _rms(x, g) * (1 + scale) + shift


def _swiglu(h, w_gu, w_d):
    gt, up = jnp.split(h @ w_gu, 2, axis=-1)
    return (jax.nn.silu(gt) * up) @ w_d


def _shift(u, off):
    if off == 0:
        return u
    T = u.shape[1]
    pad = [(0, 0)] * u.ndim
    if off > 0:
        pad[1] = (0, off)
        return jnp.pad(u, pad)[:, off:off + T]
    pad[1] = (-off, 0)
    return jnp.pad(u, pad)[:, :T]


def _depthwise_conv(u, w, b):
    K = w.shape[0]
    out = b + w[0] * _shift(u, -(K // 2))
    for j in range(1, K):
        out = out + w[j] * _shift(u, j - K // 2)
    return out


def _token_shift(u, mu):
    return u + mu[0] * (_shift(u, -1) - u) + mu[1] * (_shift(u, 1) - u)


def _axial_rope_tables(n_tokens):
    n_rows = n_tokens // GRID_W
    row = jnp.repeat(jnp.arange(n_rows), GRID_W).astype(jnp.float32)
    col = jnp.tile(jnp.arange(GRID_W), n_rows).astype(jnp.float32)
    n_freq = HEAD_DIM // 4
    inv = ROPE_BASE ** (-jnp.arange(n_freq, dtype=jnp.float32) / n_freq)
    ang = jnp.concatenate([row[:, None] * inv, col[:, None] * inv], axis=-1)
    return jnp.cos(ang), jnp.sin(ang)


def _apply_rope(x, cos, sin):
    B, T, Hh, Dh = x.shape
    nf = Dh // 4
    xr = x.reshape(B, T, Hh, 2, 2, nf)
    x1, x2 = xr[..., 0, :], xr[..., 1, :]
    cs = cos.reshape(1, T, 1, 2, nf)
    sn = sin.reshape(1, T, 1, 2, nf)
    out = jnp.stack([x1 * cs - x2 * sn, x2 * cs + x1 * sn], axis=-2)
    return out.reshape(B, T, Hh, Dh).astype(x.dtype)


def _windowed_attention(q, k, v, kc, vc, sink):
    B, T, H, Dh = q.shape
    G = k.shape[2]
    R = H // G
    C = kc.shape[1]
    n_blk = T // ATTN_BLOCK
    span = ATTN_BLOCK + 2 * WINDOW
    pad = ((0, 0), (WINDOW, WINDOW), (0, 0), (0, 0))
    kp = jnp.pad(k, pad)
    vp = jnp.pad(v, pad)
    qg = q.reshape(B, T, G, R, Dh)
    scale = Dh ** -0.5
    sink_col = jnp.broadcast_to(sink.astype(jnp.float32).reshape(1, G, R, 1, 1), (B, G, R, ATTN_BLOCK, 1))

    def one_block(i):
        start = i * ATTN_BLOCK
        qb = lax.dynamic_slice_in_dim(qg, start, ATTN_BLOCK, axis=1)
        kb = lax.dynamic_slice_in_dim(kp, start, span, axis=1)
        vb = lax.dynamic_slice_in_dim(vp, start, span, axis=1)
        q_pos = start + jnp.arange(ATTN_BLOCK)
        k_pos = start - WINDOW + jnp.arange(span)
        valid = (jnp.abs(q_pos[:, None] - k_pos[None, :]) <= WINDOW) & (k_pos[None, :] >= 0) & (k_pos[None, :] < T)
        s_loc = jnp.einsum('bqgrd,bkgd->bgrqk', qb, kb).astype(jnp.float32) * scale
        s_loc = jnp.where(valid, s_loc, NEG_INF)
        s_ctx = jnp.einsum('bqgrd,bcgd->bgrqc', qb, kc).astype(jnp.float32) * scale
        p = jax.nn.softmax(jnp.concatenate([s_loc, s_ctx, sink_col], axis=-1), axis=-1).astype(vb.dtype)
        o = jnp.einsum('bgrqk,bkgd->bqgrd', p[..., :span], vb)
        o = o + jnp.einsum('bgrqc,bcgd->bqgrd', p[..., span:span + C], vc)
        return o

    o = lax.map(one_block, jnp.arange(n_blk))
    return jnp.moveaxis(o, 0, 1).reshape(B, T, H * Dh)


def _context_attention(q, k, v, sink):
    B, C, H, Dh = q.shape
    G = k.shape[2]
    R = H // G
    qg = q.reshape(B, C, G, R, Dh)
    s = jnp.einsum('bqgrd,bkgd->bgrqk', qg, k).astype(jnp.float32) * Dh ** -0.5
    sink_col = jnp.broadcast_to(sink.astype(jnp.float32).reshape(1, G, R, 1, 1), (B, G, R, C, 1))
    p = jax.nn.softmax(jnp.concatenate([s, sink_col], axis=-1), axis=-1)[..., :C].astype(v.dtype)
    return jnp.einsum('bgrqk,bkgd->bqgrd', p, v).reshape(B, C, H * Dh)


def _split_qkv(za):
    B, T, _ = za.shape
    q, k, v = jnp.split(za, [ATTN_Q_COLS, ATTN_Q_COLS + ATTN_KV_COLS], axis=-1)
    return (q.reshape(B, T, ATTN_HEADS, HEAD_DIM),
            k.reshape(B, T, ATTN_KV_HEADS, HEAD_DIM),
            v.reshape(B, T, ATTN_KV_HEADS, HEAD_DIM))


def _attention_mixer(za, zac, cos, sin, q_gain, k_gain, sink, need_ctx):
    q, k, v = _split_qkv(za)
    q = _apply_rope(_rms(q, q_gain), cos, sin)
    k = _apply_rope(_rms(k, k_gain), cos, sin)
    qc, kc, vc = _split_qkv(zac)
    kc = _rms(kc, k_gain)
    y = _windowed_attention(q, k, v, kc, vc, sink)
    yc = _context_attention(_rms(qc, q_gain), kc, vc, sink) if need_ctx else None
    return y, yc


def _linear_combine(left, right):
    return left[0] * right[0], right[0] * left[1] + right[1]


def _rglru_direction(u, w_gate, b_gate, lam, h0, reverse):
    B, T, W = u.shape
    ub = u.reshape(B, T, LRU_BLOCKS, W // LRU_BLOCKS)
    gates = jnp.einsum('btnd,gnde->gbtne', ub, w_gate).reshape(2, B, T, W) + b_gate[:, None, None, :]
    r = jax.nn.sigmoid(gates[0].astype(jnp.float32))
    i = jax.nn.sigmoid(gates[1].astype(jnp.float32))
    log_a = -LRU_C * jax.nn.softplus(-lam.astype(jnp.float32)) * r
    a = jnp.exp(log_a)
    b = jnp.sqrt(-jnp.expm1(2.0 * log_a)) * (i * u.astype(jnp.float32))
    A, Hs = lax.associative_scan(_linear_combine, (a, b), axis=1, reverse=reverse)
    return A * h0[:, None, :] + Hs


def _lru_mixer(zl, zlc, conv_w, conv_b, gate_w, gate_b, lam, need_ctx):
    ux, ug = jnp.split(zl, 2, axis=-1)
    uxc, ugc = jnp.split(zlc, 2, axis=-1)
    xl = _depthwise_conv(ux, conv_w, conv_b)
    xc = _depthwise_conv(uxc, conv_w, conv_b)
    h0 = jnp.zeros((xc.shape[0], LRU_WIDTH), jnp.float32)
    h_lat = None
    h_ctx = None
    for d in range(2):
        rev = d == 1
        hc = _rglru_direction(xc, gate_w[d], gate_b[d], lam[d], h0, rev)
        h_fin = hc[:, 0] if rev else hc[:, -1]
        hl = _rglru_direction(xl, gate_w[d], gate_b[d], lam[d], h_fin, rev)
        h_lat = hl if h_lat is None else h_lat + hl
        if need_ctx:
            h_ctx = hc if h_ctx is None else h_ctx + hc
    y = h_lat.astype(ug.dtype) * jax.nn.gelu(ug)
    yc = h_ctx.astype(ugc.dtype) * jax.nn.gelu(ugc) if need_ctx else None
    return y, yc


def _rwkv_heads(t):
    return t.astype(jnp.float32).reshape(t.shape[0], t.shape[1], RWKV_HEADS, RWKV_HEAD)


def _rwkv_prepare(z, mu, w_up, w0, a_up, a0, g_up, k_k, k_a):
    Rw, Ra = RWKV_DECAY_RANK, RWKV_A_RANK
    cuts = [RWKV_WIDTH, 2 * RWKV_WIDTH, 3 * RWKV_WIDTH, 3 * RWKV_WIDTH + 2 * Rw, 3 * RWKV_WIDTH + 2 * Rw + 2 * Ra]
    r, k, v, wd, ad, gd = jnp.split(_token_shift(z, mu), cuts, axis=-1)
    kk = _rwkv_heads(k * k_k)
    kk = kk / jnp.maximum(jnp.linalg.norm(kk, axis=-1, keepdims=True), 1e-12)
    per_dir = []
    for d in range(2):
        logw = (w0[d] + jnp.tanh(wd[..., d * Rw:(d + 1) * Rw]) @ w_up[d]).astype(jnp.float32)
        decay = jnp.exp(-jnp.exp(-jax.nn.softplus(-logw) - 0.5))
        a = jax.nn.sigmoid(a0[d] + ad[..., d * Ra:(d + 1) * Ra] @ a_up[d])
        k_d = k * (1 + (a - 1) * k_a)
        per_dir.append((_rwkv_heads(decay), _rwkv_heads(k_d), _rwkv_heads(a)))
    gate = jax.nn.sigmoid(gd) @ g_up
    return _rwkv_heads(r), _rwkv_heads(v), kk, per_dir, gate


def _rwkv_scan(s0, r, w, k, v, kk, a, reverse):
    xs = tuple(jnp.moveaxis(t, 1, 0) for t in (r, w, k, v, kk, a))

    def step(s, inp):
        r_t, w_t, k_t, v_t, kk_t, a_t = inp
        sa = jnp.einsum('bhvk,bhk->bhv', s, -kk_t)
        s = s * w_t[:, :, None, :] + sa[..., None] * (kk_t * a_t)[:, :, None, :] + v_t[..., None] * k_t[:, :, None, :]
        return s, jnp.einsum('bhvk,bhk->bhv', s, r_t)

    s_fin, o = lax.scan(step, s0, xs, reverse=reverse)
    return jnp.moveaxis(o, 0, 1), s_fin


def _rwkv_bonus(r, k, v, r_k):
    return jnp.sum(r * k * r_k.astype(jnp.float32).reshape(RWKV_HEADS, RWKV_HEAD), axis=-1, keepdims=True) * v


def _head_groupnorm(o, g, b):
    mean = jnp.mean(o, axis=-1, keepdims=True)
    var = jnp.mean(jnp.square(o - mean), axis=-1, keepdims=True)
    y = (o - mean) * lax.rsqrt(var + RWKV_LN_EPS)
    return y.reshape(o.shape[0], o.shape[1], RWKV_WIDTH) * g + b


def _rwkv_mixer(zr, zrc, mu, w_up, w0, a_up, a0, g_up, k_k, k_a, r_k, ln_g, ln_b, need_ctx):
    r, v, kk, dirs, gate = _rwkv_prepare(zr, mu, w_up, w0, a_up, a0, g_up, k_k, k_a)
    rc, vc, kkc, dirs_c, gate_c = _rwkv_prepare(zrc, mu, w_up, w0, a_up, a0, g_up, k_k, k_a)
    s0 = jnp.zeros((rc.shape[0], RWKV_HEADS, RWKV_HEAD, RWKV_HEAD), jnp.float32)
    o = None
    oc = None
    for d in range(2):
        rev = d == 1
        w_c, k_c, a_c = dirs_c[d]
        o_c_d, s_c = _rwkv_scan(s0, rc, w_c, k_c, vc, kkc, a_c, rev)
        w_l, k_l, a_l = dirs[d]
        o_l_d, _ = _rwkv_scan(s_c, r, w_l, k_l, v, kk, a_l, rev)
        term = o_l_d + _rwkv_bonus(r, k_l, v, r_k)
        o = term if o is None else o + term
        if need_ctx:
            term_c = o_c_d + _rwkv_bonus(rc, k_c, vc, r_k)
            oc = term_c if oc is None else oc + term_c
    y = (_head_groupnorm(o, ln_g, ln_b) * gate).astype(zr.dtype)
    yc = (_head_groupnorm(oc, ln_g, ln_b) * gate_c).astype(zrc.dtype) if need_ctx else None
    return y, yc


def _gated_merge(outs, zg, proj, w_o):
    g = jax.nn.sigmoid(zg.reshape(zg.shape[0], zg.shape[1], N_BRANCH, D_MODEL))
    acc = g[..., 0, :] * (outs[0] @ proj[0])
    for b in range(1, N_BRANCH):
        acc = acc + g[..., b, :] * (outs[b] @ proj[b])
    return acc @ w_o


def _token_mixers(z, zc, cos, sin, q_gain, k_gain, sink, conv_w, conv_b, gate_w, gate_b, lam,
                  mu, w_up, w0, a_up, a0, g_up, k_k, k_a, r_k, ln_g, ln_b, proj, w_o, need_ctx):
    cuts = [ATTN_COLS, ATTN_COLS + LRU_COLS, ATTN_COLS + LRU_COLS + RWKV_COLS]
    za, zl, zr, zg = jnp.split(z, cuts, axis=-1)
    zac, zlc, zrc, zgc = jnp.split(zc, cuts, axis=-1)
    ya, yac = _attention_mixer(za, zac, cos, sin, q_gain, k_gain, sink, need_ctx)
    yl, ylc = _lru_mixer(zl, zlc, conv_w, conv_b, gate_w, gate_b, lam, need_ctx)
    yr, yrc = _rwkv_mixer(zr, zrc, mu, w_up, w0, a_up, a0, g_up, k_k, k_a, r_k, ln_g, ln_b, need_ctx)
    out = _gated_merge((ya, yl, yr), zg, proj, w_o)
    out_c = _gated_merge((yac, ylc, yrc), zgc, proj, w_o) if need_ctx else None
    return out, out_c


def setup_inputs(seed: int = 0) -> dict:
    key = jax.random.key(seed)
    ks = list(jax.random.split(key, 40))
    L, D = DEPTH, D_MODEL
    bw = LRU_WIDTH // LRU_BLOCKS

    def nrm(i, shape, scale):
        return jax.random.normal(ks[i], shape, jnp.float32) * scale

    a_init = jax.random.uniform(ks[30], (L, 2, LRU_WIDTH), jnp.float32, minval=0.9, maxval=0.999)
    return {
        'x': nrm(0, (BATCH, SEQ, D), 1.0),
        'c': nrm(1, (BATCH, D), 1.0),
        'ctx': nrm(2, (BATCH, CTX_LEN, D), 1.0),
        'c_ctx': nrm(3, (D,), 1.0),
        'ada_w': nrm(4, (L, D, N_ADA * D), 0.5 * D ** -0.5),
        'ada_b': nrm(5, (L, N_ADA * D), 0.02),
        'norm_g': 1.0 + nrm(6, (L, 3, D), 0.02),
        'ffn_w_gu': nrm(7, (L, 2, D, 2 * D_FF), D ** -0.5),
        'ffn_w_d': nrm(8, (L, 2, D_FF, D), D_FF ** -0.5),
        'w_in': nrm(9, (L, D, IN_COLS), D ** -0.5),
        'attn_q_gain': 1.0 + nrm(10, (L, HEAD_DIM), 0.02),
        'attn_k_gain': 1.0 + nrm(11, (L, HEAD_DIM), 0.02),
        'attn_sink': nrm(12, (L, ATTN_HEADS), 0.5),
        'lru_conv_w': nrm(13, (L, LRU_CONV, LRU_WIDTH), LRU_CONV ** -0.5),
        'lru_conv_b': nrm(14, (L, LRU_WIDTH), 0.02),
        'lru_gate_w': nrm(15, (L, 2, 2, LRU_BLOCKS, bw, bw), bw ** -0.5),
        'lru_gate_b': nrm(16, (L, 2, 2, LRU_WIDTH), 0.02),
        'lru_lambda': jnp.log(a_init) - jnp.log1p(-a_init),
        'rwkv_mu': jax.random.uniform(ks[17], (L, 2, RWKV_COLS), jnp.float32, minval=0.0, maxval=0.5),
        'rwkv_w_up': nrm(18, (L, 2, RWKV_DECAY_RANK, RWKV_WIDTH), 0.5 * RWKV_DECAY_RANK ** -0.5),
        'rwkv_w0': nrm(19, (L, 2, RWKV_WIDTH), 1.0) - 0.5,
        'rwkv_a_up': nrm(20, (L, 2, RWKV_A_RANK, RWKV_WIDTH), 0.5 * RWKV_A_RANK ** -0.5),
        'rwkv_a0': nrm(21, (L, 2, RWKV_WIDTH), 0.5),
        'rwkv_g_up': nrm(22, (L, RWKV_G_RANK, RWKV_WIDTH), RWKV_G_RANK ** -0.5),
        'rwkv_k_k': 0.85 + nrm(23, (L, RWKV_WIDTH), 0.02),
        'rwkv_k_a': 1.0 + nrm(24, (L, RWKV_WIDTH), 0.02),
        'rwkv_r_k': nrm(25, (L, RWKV_WIDTH), 0.1),
        'rwkv_ln_g': 1.0 + nrm(26, (L, RWKV_WIDTH), 0.02),
        'rwkv_ln_b': nrm(27, (L, RWKV_WIDTH), 0.02),
        'branch_proj': nrm(28, (L, N_BRANCH, MIX_WIDTH, D), MIX_WIDTH ** -0.5),
        'w_out': nrm(29, (L, D, D), D ** -0.5),
    }


def reference(x, c, ctx, c_ctx, ada_w, ada_b, norm_g, ffn_w_gu, ffn_w_d, w_in,
              attn_q_gain, attn_k_gain, attn_sink, lru_conv_w, lru_conv_b, lru_gate_w, lru_gate_b, lru_lambda,
              rwkv_mu, rwkv_w_up, rwkv_w0, rwkv_a_up, rwkv_a0, rwkv_g_up, rwkv_k_k, rwkv_k_a, rwkv_r_k,
              rwkv_ln_g, rwkv_ln_b, branch_proj, w_out):
    n_tok = x.shape[1]
    cos, sin = _axial_rope_tables(n_tok)
    s_lat = jax.nn.silu(c)[:, None, :]
    s_ctx = jax.nn.silu(c_ctx)[None, None, :]
    xc = ctx
    for l in range(DEPTH):
        need_ctx = l < DEPTH - 1
        m = jnp.split(s_lat @ ada_w[l] + ada_b[l], N_ADA, axis=-1)
        mc = jnp.split(s_ctx @ ada_w[l] + ada_b[l], N_ADA, axis=-1)
        x = x + 0.5 * m[2] * _swiglu(_modulate(x, norm_g[l, 0], m[0], m[1]), ffn_w_gu[l, 0], ffn_w_d[l, 0])
        xc = xc + 0.5 * mc[2] * _swiglu(_modulate(xc, norm_g[l, 0], mc[0], mc[1]), ffn_w_gu[l, 0], ffn_w_d[l, 0])
        z = _modulate(x, norm_g[l, 1], m[3], m[4]) @ w_in[l]
        zc = _modulate(xc, norm_g[l, 1], mc[3], mc[4]) @ w_in[l]
        mix, mix_c = _token_mixers(
            z, zc, cos, sin, attn_q_gain[l], attn_k_gain[l], attn_sink[l],
            lru_conv_w[l], lru_conv_b[l], lru_gate_w[l], lru_gate_b[l], lru_lambda[l],
            rwkv_mu[l], rwkv_w_up[l], rwkv_w0[l], rwkv_a_up[l], rwkv_a0[l], rwkv_g_up[l],
            rwkv_k_k[l], rwkv_k_a[l], rwkv_r_k[l], rwkv_ln_g[l], rwkv_ln_b[l],
            branch_proj[l], w_out[l], need_ctx)
        x = x + m[5] * mix
        x = x + 0.5 * m[8] * _swiglu(_modulate(x, norm_g[l, 2], m[6], m[7]), ffn_w_gu[l, 1], ffn_w_d[l, 1])
        if need_ctx:
            xc = xc + mc[5] * mix_c
            xc = xc + 0.5 * mc[8] * _swiglu(_modulate(xc, norm_g[l, 2], mc[6], mc[7]), ffn_w_gu[l, 1], ffn_w_d[l, 1])
    return x
```

```python
import numpy as np
import ml_dtypes
from contextlib import ExitStack
import concourse.bass as bass
import concourse.mybir as mybir
from concourse.bass_utils import run_bass_kernel_spmd

F32 = mybir.dt.float32
BF16 = mybir.dt.bfloat16
AF = mybir.ActivationFunctionType
ALU = mybir.AluOpType
AX = mybir.AxisListType

D = 1024
CTX = 256
DFF = 2816
NADA = 9
EPS = 1e-6
ENGS = ('pe', 'act', 'dve', 'pool', 'sp')
ND = 12
SB_WORDS = 52736
RK_STOP = 99
CG = 256


class Buf:
    __slots__ = ('w', 'r', 'name')

    def __init__(self, name=''):
        self.w = {}
        self.r = {}
        self.name = name


class Tile:
    __slots__ = ('ap', 'buf')

    def __init__(self, ap, name=''):
        self.ap = ap
        self.buf = Buf(name)


class Prog:
    def __init__(self, nc, es):
        self.nc = nc
        self.sem = {e: es.enter_context(nc.semaphore('s_' + e)) for e in ENGS}
        self.dsem = {q: [es.enter_context(nc.semaphore('d_%s%d' % (q, i))) for i in range(ND)]
                     for q in ('sp', 'pool', 'act')}
        self.cnt = {e: 0 for e in ENGS}
        self.dcnt = {q: 0 for q in self.dsem}
        self.seen = {e: {} for e in ENGS}
        self.q = {e: [] for e in ENGS}
        self.big = es.enter_context(nc.sbuf_tensor('big', [128, SB_WORDS], F32))
        self.off = 0
        self.psum = []
        for i in range(8):
            t = es.enter_context(nc.psum_tensor('ps%d' % i, [128, 512], F32))
            self.psum.append(Tile(t[:, :], 'ps%d' % i))
        self.psi = 0
        self.ninstr = 0

    def tile(self, nfree, dtype=F32, name=''):
        nw = nfree if dtype == F32 else (nfree + 1) // 2
        nw = (nw + 7) // 8 * 8
        assert self.off + nw <= SB_WORDS, ('SBUF overflow', name, self.off, nw)
        ap = self.big[:, self.off:self.off + nw]
        self.off += nw
        if dtype != F32:
            ap = ap.bitcast(dtype)
        ap = ap[:, 0:nfree]
        return Tile(ap, name)

    def ps(self):
        t = self.psum[self.psi]
        self.psi = (self.psi + 1) % 8
        return t

    def handle(self, k):
        if isinstance(k, tuple):
            return self.dsem[k[1]][k[2]]
        return self.sem[k]

    def _deps(self, eng, reads, writes):
        need = {}
        for b in reads:
            for k, v in b.w.items():
                if need.get(k, 0) < v:
                    need[k] = v
        for b in writes:
            for k, v in b.w.items():
                if need.get(k, 0) < v:
                    need[k] = v
            for k, v in b.r.items():
                if need.get(k, 0) < v:
                    need[k] = v
        waits = []
        seen = self.seen[eng]
        for k, v in need.items():
            if k == 'pe' and eng == 'pe':
                continue
            if seen.get(k, 0) >= v:
                continue
            seen[k] = v
            waits.append((k, v))
        return waits

    @staticmethod
    def _bufs(xs):
        return [x.buf if isinstance(x, Tile) else x for x in xs]

    def op(self, eng, fn, reads=(), writes=()):
        reads = self._bufs(reads)
        writes = self._bufs(writes)
        waits = self._deps(eng, reads, writes)
        self.cnt[eng] += 1
        c = self.cnt[eng]
        for b in reads:
            if b.r.get(eng, 0) < c:
                b.r[eng] = c
        for b in writes:
            b.w[eng] = c
            b.r = {}
        self.q[eng].append((waits, fn, (eng, 1)))
        self.ninstr += 1 + len(waits)

    def dma(self, q, out, in_, reads=(), writes=(), **kw):
        reads = self._bufs(reads)
        writes = self._bufs(writes)
        waits = self._deps(q, reads, writes)
        i = self.dcnt[q]
        self.dcnt[q] += 1
        slot = i % ND
        val = 16 * (i // ND + 1)
        key = ('d', q, slot)
        if i >= ND and self.seen[q].get(key, 0) < val - 16:
            self.seen[q][key] = val - 16
            waits.append((key, val - 16))
        for b in reads:
            if b.r.get(key, 0) < val:
                b.r[key] = val
        for b in writes:
            b.w[key] = val
            b.r = {}
        self.q[q].append((waits, lambda e: e.dma_start(out=out, in_=in_, **kw), (key, 16)))
        self.ninstr += 1 + len(waits)

    def barrier(self):
        toks = [(e, self.cnt[e]) for e in ENGS if self.cnt[e] > 0]
        for q in self.dsem:
            n = self.dcnt[q]
            for slot in range(min(n, ND)):
                last = ((n - 1 - slot) // ND) + 1
                toks.append((('d', q, slot), 16 * last))
        for e in ENGS:
            waits = []
            for k, v in toks:
                if k == e and e == 'pe':
                    continue
                if self.seen[e].get(k, 0) >= v:
                    continue
                self.seen[e][k] = v
                waits.append((k, v))
            if waits:
                self.q[e].append((waits, None, None))
                self.ninstr += len(waits)

    def emit(self):
        self.barrier()
        with self.nc.Block() as block:
            def mk(e):
                def run(eng):
                    for waits, fn, inc in self.q[e]:
                        for k, v in waits:
                            eng.wait_ge(self.handle(k), v)
                        if fn is not None:
                            fn(eng).then_inc(self.handle(inc[0]), inc[1])
                return run
            block.tensor(mk('pe'))
            block.scalar(mk('act'))
            block.vector(mk('dve'))
            block.gpsimd(mk('pool'))
            block.sync(mk('sp'))


def bc(ap, shape, axis):
    return ap.unsqueeze(axis).broadcast_to(list(shape))


def v3(ap, k):
    return ap.rearrange("p (k n) -> p k n", k=k)


class Builder:
    def __init__(self, T, depth, stop=None):
        self.T = T
        self.S = CTX + T
        self.depth = depth
        self.stop = stop
        self.debug = False
        self.dbg_names = []
        self.chunks = [(CG * i, CG, 1 if i == 0 else 0) for i in range(self.S // CG)]

    def build(self):
        nc = bass.Bass("TRN2", target_bir_lowering=False)
        self.nc = nc
        T, S = self.T, self.S
        L = 4
        dr = {}

        def inp(name, shape, dt=F32):
            dr[name] = nc.dram_tensor(name, list(shape), dt, kind="ExternalInput").ap()

        inp('x', [T, D]); inp('ctx', [CTX, D]); inp('cvec', [2, D])
        inp('ada_w', [L, D, NADA * D]); inp('ada_b', [L, NADA * D]); inp('norm_g', [L, 3, D])
        inp('ffn_w_gu', [L, 2, D, 2 * DFF]); inp('ffn_w_d', [L, 2, DFF, D])
        inp('ident', [128, 128])
        inp('w_in', [L, D, 6784]); inp('attn_q_gain', [L, 64]); inp('attn_k_gain', [L, 64]); inp('attn_sink', [L, 8])
        inp('lru_conv_w', [L, 4, 512]); inp('lru_conv_b', [L, 512]); inp('lru_gate_w', [L, 2, 2, 8, 64, 64])
        inp('lru_gate_b', [L, 2, 2, 512]); inp('lru_lambda', [L, 2, 512])
        inp('branch_proj', [L, 3, 512, D]); inp('w_out', [L, D, D])
        inp('rope_cos', [128, T]); inp('rope_sin', [128, T]); inp('rotm', [128, 128]); inp('blk64', [128, 128])
        inp('mlo', [128, 512]); inp('mhi', [128, 512])
        inp('rwkv_mu', [L, 2, 1920]); inp('rwkv_w_up', [L, 2, 64, 512]); inp('rwkv_w0', [L, 2, 512]); inp('rwkv_a_up', [L, 2, 64, 512])
        inp('rwkv_a0', [L, 2, 512]); inp('rwkv_g_up', [L, 128, 512]); inp('rwkv_k_k', [L, 512]); inp('rwkv_k_a', [L, 512])
        inp('rwkv_r_k', [L, 512]); inp('rwkv_ln_g', [L, 512]); inp('rwkv_ln_b', [L, 512])
        inp('sel', [128, 2])
        for d_ in range(2):
            inp('mask4_%d' % d_, [128, 512]); inp('maskN_%d' % d_, [128, 512]); inp('rst_%d' % d_, [128, 512])
        for nm, shp, dt in (('RS', [512, S], F32), ('KS', [512, S], F32), ('KK', [512, S], F32), ('VT', [S, 512], BF16),
                            ('TW', [128, S], BF16), ('AD', [128, S], BF16), ('SG', [128, S], BF16), ('OS', [S, 512], F32)):
            dr[nm] = nc.dram_tensor(nm, shp, dt, kind="Internal").ap()
        for nm, shp, dt in (('QT', [512, S], BF16), ('KT', [128, S], BF16), ('VA', [S, 128], BF16), ('ZL', [1024, S], F32),
                            ('ZR', [1920, S], F32), ('YA', [512, S], BF16), ('YL', [512, S], BF16), ('YR', [512, S], BF16)):
            dr[nm] = nc.dram_tensor(nm, shp, dt, kind="Internal").ap()
        self.sb = {nm: Buf(nm) for nm in ('QT', 'KT', 'VA', 'ZL', 'ZR', 'YA', 'YL', 'YR', 'RS', 'KS', 'KK', 'VT', 'TW', 'AD', 'SG', 'OS')}
        dr['out'] = nc.dram_tensor('out', [S, D], F32, kind="ExternalOutput").ap()
        dr['XT'] = nc.dram_tensor('XT', [D, S], F32, kind="Internal").ap()
        self.dr = dr
        with ExitStack() as es:
            P = Prog(nc, es)
            self.P = P
            self.xt_bufs = [Buf('xt%d' % i) for i in range(self.S // CG)]
            self.consts()
            self.phase_in()
            for l in range(self.depth):
                self.layer(l)
            self.phase_out()
            P.emit()
        return nc

    def consts(self):
        P, dr = self.P, self.dr
        self.ident = P.tile(128, F32, 'ident')
        P.dma('sp', self.ident.ap, dr['ident'][:, :], writes=[self.ident])
        self.identb = P.tile(128, BF16, 'identb')
        P.op('dve', lambda e: e.tensor_copy(out=self.identb.ap, in_=self.ident.ap), [self.ident], [self.identb])
        self.onesb = P.tile(128, BF16, 'onesb')
        P.op('dve', lambda e: e.memset(self.onesb.ap, 1.0), [], [self.onesb])
        st = P.tile(128, F32, 'cv_stage')
        P.dma('sp', st.ap[0:16, :], dr['cvec'].rearrange("v (k p) -> (v k) p", p=128), writes=[st])
        ps = P.ps()
        P.op('pe', lambda e: e.transpose(ps.ap[:, 0:16], st.ap[0:16, :], self.ident.ap[0:16, 0:16]), [st, self.ident], [ps])
        self.sc = P.tile(16, F32, 'silu_c')
        sc3 = self.sc.ap.rearrange("p (k v) -> p k v", v=2)
        P.op('act', lambda e: e.activation(out=sc3, in_=ps.ap[:, 0:16].rearrange("p (v k) -> p k v", v=2), func=AF.Silu), [ps], [self.sc])
        def cload(name, n, q='pool'):
            t = P.tile(n, BF16, name)
            P.dma(q, t.ap, dr[name][:, :], writes=[t])
            return t
        self.rotm = cload('rotm', 128)
        self.blk64 = cload('blk64', 128)
        self.mlo = cload('mlo', 512)
        self.mhi = cload('mhi', 512)
        self.mask4 = [cload('mask4_%d' % d_, 512) for d_ in range(2)]
        self.maskN = [cload('maskN_%d' % d_, 512) for d_ in range(2)]
        self.rst = []
        for d_ in range(2):
            t = P.tile(512, F32, 'rst%d' % d_)
            P.dma('sp', t.ap, dr['rst_%d' % d_][:, :], writes=[t])
            self.rst.append(t)
        self.sel = P.tile(2, F32, 'sel')
        P.dma('sp', self.sel.ap, dr['sel'][:, :], writes=[self.sel])
        self.eps_t = P.tile(1, F32, 'eps')
        self.eps_ap = self.eps_t.ap[:, 0:1]
        P.op('dve', lambda e: e.memset(self.eps_t.ap, EPS), [], [self.eps_t])
        self.lneps_t = P.tile(1, F32, 'lneps')
        self.lneps_ap = self.lneps_t.ap[:, 0:1]
        P.op('dve', lambda e: e.memset(self.lneps_t.ap, 64e-5), [], [self.lneps_t])
        self.one_t = P.tile(1, F32, 'one')
        self.one_ap = self.one_t.ap[:, 0:1]
        P.op('dve', lambda e: e.memset(self.one_t.ap, 1.0), [], [self.one_t])
        self.base_off = P.off

    def phase_in(self):
        P, dr = self.P, self.dr
        S = self.S
        stg = [P.tile(D, F32, 'in_stg%d' % i) for i in range(2)]
        outt = [P.tile(D, F32, 'in_out%d' % i) for i in range(2)]
        ntile = S // 128
        for it in range(ntile):
            t0 = it * 128
            a = stg[it % 2]
            o = outt[it % 2]
            src = dr['ctx'][t0:t0 + 128, :] if t0 < CTX else dr['x'][t0 - CTX:t0 - CTX + 128, :]
            P.dma('sp', a.ap, src, writes=[a])
            for hf in range(2):
                ps = P.ps()
                for j in range(4):
                    k = hf * 4 + j
                    P.op('pe', lambda e, ps=ps, j=j, k=k, a=a: e.transpose(ps.ap[:, j * 128:(j + 1) * 128], a.ap[:, k * 128:(k + 1) * 128], self.ident.ap), [a, self.ident], [ps])
                eng = 'act' if hf == 0 else 'dve'
                if eng == 'act':
                    P.op('act', lambda e, ps=ps, o=o, hf=hf: e.copy(out=o.ap[:, hf * 512:(hf + 1) * 512], in_=ps.ap), [ps], [o])
                else:
                    P.op('dve', lambda e, ps=ps, o=o, hf=hf: e.tensor_copy(out=o.ap[:, hf * 512:(hf + 1) * 512], in_=ps.ap), [ps], [o])
            cb = self.chunk_buf(t0)
            P.dma('sp', dr['XT'][:, t0:t0 + 128].rearrange("(k p) t -> p k t", p=128), v3(o.ap, 8), reads=[o], writes=[cb])
        P.barrier()
        P.off = self.base_off

    def dbg(self, name, tile, ap=None):
        if not self.debug:
            return
        ap = tile.ap if ap is None else ap
        shape = list(ap.shape)
        d = self.nc.dram_tensor('dbg_' + name, shape, ap.dtype, kind="ExternalOutput").ap()
        self.dbg_names.append('dbg_' + name)
        self.P.dma('sp', d, ap, reads=[tile])

    def dump_scratch(self, names=('YA', 'YL', 'YR')):
        P, dr, sb = self.P, self.dr, self.sb
        for nm in names:
            d = self.nc.dram_tensor('dbg_' + nm, list(dr[nm].shape), dr[nm].dtype, kind="ExternalOutput").ap()
            P.dma('sp', d, dr[nm], reads=[sb[nm]])

    def chunk_buf(self, t0):
        return self.xt_bufs[t0 // CG]

    def phase_out(self):
        P, dr = self.P, self.dr
        S = self.S
        stg = [P.tile(D, F32, 'o_stg%d' % i) for i in range(2)]
        outt = [P.tile(D, F32, 'o_out%d' % i) for i in range(2)]
        for it in range(S // 128):
            t0 = it * 128
            a = stg[it % 2]
            o = outt[it % 2]
            cb = self.chunk_buf(t0)
            P.dma('sp', v3(a.ap, 8), dr['XT'][:, t0:t0 + 128].rearrange("(k p) t -> p k t", p=128), reads=[cb], writes=[a])
            for hf in range(2):
                ps = P.ps()
                for j in range(4):
                    k = hf * 4 + j
                    P.op('pe', lambda e, ps=ps, j=j, k=k, a=a: e.transpose(ps.ap[:, j * 128:(j + 1) * 128], a.ap[:, k * 128:(k + 1) * 128], self.ident.ap), [a, self.ident], [ps])
                if hf == 0:
                    P.op('act', lambda e, ps=ps, o=o, hf=hf: e.copy(out=o.ap[:, hf * 512:(hf + 1) * 512], in_=ps.ap), [ps], [o])
                else:
                    P.op('dve', lambda e, ps=ps, o=o, hf=hf: e.tensor_copy(out=o.ap[:, hf * 512:(hf + 1) * 512], in_=ps.ap), [ps], [o])
            P.dma('sp', dr['out'][t0:t0 + 128, :], o.ap, reads=[o])
        P.barrier()
        P.off = self.base_off

    def adaln(self, l):
        P, dr = self.P, self.dr
        nt = NADA * 8
        vec = P.tile(96, F32, 'vecT')
        M = [P.tile(72, F32, 'M%d' % v) for v in range(2)]
        Gs = [P.tile(24, F32, 'G%d' % v) for v in range(2)]
        GTs = [P.tile(24, F32, 'GT%d' % v) for v in range(2)]
        mark = P.off
        CB = 1152
        wbuf = [P.tile(8 * CB, F32, 'adaw%d' % i) for i in range(2)]
        ps = P.ps()
        for cb in range(8):
            w = wbuf[cb % 2]
            w3 = v3(w.ap, 8)
            for k in range(8):
                P.dma('sp' if k % 2 == 0 else 'act', w3[:, k, :], dr['ada_w'][l, k * 128:(k + 1) * 128, cb * CB:(cb + 1) * CB], writes=[w])
            for f in range(9):
                ft = cb * 9 + f
                for k in range(8):
                    P.op('pe', lambda e, ps=ps, w3=w3, ft=ft, f=f, k=k: e.matmul(ps.ap[:, 2 * ft:2 * ft + 2], w3[:, k, f * 128:(f + 1) * 128], self.sc.ap[:, 2 * k:2 * k + 2], start=(k == 0), stop=(k == 7)), [w, self.sc], [ps])
        st = P.tile(128, F32, 'vec_stage')
        P.dma('sp', st.ap[0:72, :], dr['ada_b'][l].rearrange("(r p) -> r p", p=128), writes=[st])
        P.dma('sp', st.ap[72:96, :], dr['norm_g'][l].rearrange("i (k p) -> (i k) p", p=128), writes=[st])
        ps2 = P.ps()
        P.op('pe', lambda e: e.transpose(ps2.ap[:, 0:96], st.ap[0:96, :], self.ident.ap[0:96, 0:96]), [st, self.ident], [ps2])
        P.op('dve', lambda e: e.tensor_copy(out=vec.ap, in_=ps2.ap[:, 0:96]), [ps2], [vec])
        psv = ps.ap[:, 0:144].rearrange("p (t v) -> p t v", v=2)
        for v in range(2):
            P.op('dve', lambda e, v=v: e.tensor_tensor(out=M[v].ap, in0=psv[:, :, v], in1=vec.ap[:, 0:72], op=ALU.add), [ps, vec], [M[v]])
        P.barrier()
        P.off = mark
        mod = {}
        for v in range(2):
            G = Gs[v]
            GT = GTs[v]
            for i in range(3):
                P.op('dve', lambda e, v=v, i=i, G=G: e.scalar_tensor_tensor(out=G.ap[:, i * 8:(i + 1) * 8], in0=M[v].ap[:, (3 * i + 1) * 8:(3 * i + 2) * 8], scalar=1.0, in1=vec.ap[:, 72 + i * 8:72 + (i + 1) * 8], op0=ALU.add, op1=ALU.mult), [M[v], vec], [G])
                sc = 1.0 if i == 1 else 0.5
                P.op('dve', lambda e, v=v, i=i, GT=GT, sc=sc: e.tensor_scalar(out=GT.ap[:, i * 8:(i + 1) * 8], in0=M[v].ap[:, (3 * i + 2) * 8:(3 * i + 3) * 8], scalar1=sc, scalar2=None, op0=ALU.mult), [M[v]], [GT])
            for i in range(3):
                mod[(i, v)] = dict(G=G.ap[:, i * 8:(i + 1) * 8], SH=M[v].ap[:, 3 * i * 8:(3 * i + 1) * 8], GT=GT.ap[:, i * 8:(i + 1) * 8], bufs=[G, GT, M[v]])
        return mod

    def load_norm(self, c0, n, v, i, mod, xt, xm, sq, rs, xn):
        P, dr = self.P, self.dr
        m = mod[(i, v)]
        x3 = v3(xt.ap, 8)
        P.dma('sp', x3, dr['XT'][:, c0:c0 + n].rearrange("(k p) t -> p k t", p=128), reads=[self.chunk_buf(c0)], writes=[xt])
        P.op('act', lambda e: e.activation(out=sq.ap, in_=xt.ap, func=AF.Square), [xt], [sq])
        ps = P.ps()
        for k in range(8):
            P.op('pe', lambda e, k=k: e.matmul(ps.ap[:, 0:n], self.onesb.ap, v3(sq.ap, 8)[:, k, :], start=(k == 0), stop=(k == 7)), [sq, self.onesb], [ps])
        P.op('act', lambda e: e.activation(out=rs.ap, in_=ps.ap[:, 0:n], func=AF.Sqrt, bias=self.eps_ap, scale=1.0 / D), [ps, self.eps_t], [rs])
        P.op('dve', lambda e: e.reciprocal(out=rs.ap, in_=rs.ap), [rs], [rs])
        P.op('dve', lambda e: e.tensor_tensor(out=v3(xn.ap, 8), in0=x3, in1=bc(rs.ap, [128, 8, n], 1), op=ALU.mult), [xt, rs], [xn])
        for k in range(8):
            P.op('act', lambda e, k=k: e.activation(out=v3(xm.ap, 8)[:, k, :], in_=v3(xn.ap, 8)[:, k, :], func=AF.Identity, bias=m['SH'][:, k:k + 1], scale=m['G'][:, k:k + 1]), [xn] + m['bufs'], [xm])

    def ffn(self, l, which, mod):
        P, dr = self.P, self.dr
        i = 0 if which == 0 else 2
        mark = P.off
        n = CG
        wgu = P.tile(8 * 2 * DFF, BF16, 'wgu')
        wd = P.tile(22 * D, BF16, 'wd')
        wgu3 = v3(wgu.ap, 8)
        wd3 = v3(wd.ap, 22)
        for k in range(8):
            P.dma('pool', wgu3[:, k, :], dr['ffn_w_gu'][l, which, k * 128:(k + 1) * 128, :], writes=[wgu])
        for j in range(22):
            P.dma('pool', wd3[:, j, :], dr['ffn_w_d'][l, which, j * 128:(j + 1) * 128, :], writes=[wd])
        xts = [P.tile(8 * n, F32, 'xt%d' % b) for b in range(2)]
        xm = P.tile(8 * n, BF16, 'xm')
        xn = P.tile(8 * n, F32, 'xn')
        sq = P.tile(8 * n, BF16, 'sq')
        rs = P.tile(n, F32, 'rs')
        h = P.tile(22 * n, BF16, 'h')
        sg = [P.tile(n, F32, 'sg%d' % b) for b in range(2)]
        xm3 = v3(xm.ap, 8)
        h3 = v3(h.ap, 22)
        chunks = self.chunks
        self.load_norm(chunks[0][0], n, chunks[0][2], i, mod, xts[0], xm, sq, rs, xn)
        if which == 0 and l == 0:
            self.dbg('xm', xm); self.dbg('rs', rs); self.dbg('xn', xn); self.dbg('xt', xts[0])
            mm = mod[(0, 1)]
            self.dbg('G', mm['bufs'][0]); self.dbg('GT', mm['bufs'][1]); self.dbg('M', mm['bufs'][2])
        for ci, (c0, _, v) in enumerate(chunks):
            xt = xts[ci % 2]
            m = mod[(i, v)]
            for j in range(22):
                pg = P.ps()
                pu = P.ps()
                for k in range(8):
                    P.op('pe', lambda e, k=k, j=j, pg=pg: e.matmul(pg.ap[:, 0:n], wgu3[:, k, j * 128:(j + 1) * 128], xm3[:, k, :], start=(k == 0), stop=(k == 7)), [wgu, xm], [pg])
                for k in range(8):
                    P.op('pe', lambda e, k=k, j=j, pu=pu: e.matmul(pu.ap[:, 0:n], wgu3[:, k, DFF + j * 128:DFF + (j + 1) * 128], xm3[:, k, :], start=(k == 0), stop=(k == 7)), [wgu, xm], [pu])
                s_ = sg[j % 2]
                P.op('act', lambda e, pg=pg, s_=s_: e.activation(out=s_.ap, in_=pg.ap[:, 0:n], func=AF.Silu), [pg], [s_])
                P.op('dve', lambda e, pu=pu, s_=s_, j=j: e.tensor_tensor(out=h3[:, j, :], in0=pu.ap[:, 0:n], in1=s_.ap, op=ALU.mult), [pu, s_], [h])
            if which == 0 and l == 0 and ci == 0:
                self.dbg('h', h)
            if ci + 1 < len(chunks):
                self.load_norm(chunks[ci + 1][0], n, chunks[ci + 1][2], i, mod, xts[(ci + 1) % 2], xm, sq, rs, xn)
            xt3 = v3(xt.ap, 8)
            for dt in range(8):
                po = P.ps()
                for j in range(22):
                    P.op('pe', lambda e, j=j, dt=dt, po=po: e.matmul(po.ap[:, 0:n], wd3[:, j, dt * 128:(dt + 1) * 128], h3[:, j, :], start=(j == 0), stop=(j == 21)), [wd, h], [po])
                P.op('dve', lambda e, dt=dt, po=po, xt3=xt3, m=m: e.scalar_tensor_tensor(out=xt3[:, dt, :], in0=po.ap[:, 0:n], scalar=m['GT'][:, dt:dt + 1], in1=xt3[:, dt, :], op0=ALU.mult, op1=ALU.add), [po, xt] + m['bufs'], [xt])
            P.dma('sp', dr['XT'][:, c0:c0 + n].rearrange("(k p) t -> p k t", p=128), xt3, reads=[xt], writes=[self.chunk_buf(c0)])
        P.barrier()
        P.off = mark

    def mix_in(self, l, mod):
        P, dr, sb = self.P, self.dr, self.sb
        mark = P.off
        n = CG
        NZ = 3712
        win = P.tile(8 * NZ, BF16, 'win')
        win3 = v3(win.ap, 8)
        for k in range(8):
            P.dma('pool', win3[:, k, :], dr['w_in'][l, k * 128:(k + 1) * 128, 0:NZ], writes=[win])
        gq = P.tile(1, F32, 'gq'); gk = P.tile(1, F32, 'gk')
        for hh in range(2):
            P.dma('sp', gq.ap[hh * 64:(hh + 1) * 64, :], dr['attn_q_gain'][l].rearrange("(p o) -> p o", o=1), writes=[gq])
            P.dma('sp', gk.ap[hh * 64:(hh + 1) * 64, :], dr['attn_k_gain'][l].rearrange("(p o) -> p o", o=1), writes=[gk])
        xts = [P.tile(8 * n, F32, 'xt%d' % b) for b in range(2)]
        xm = P.tile(8 * n, BF16, 'xm'); xn = P.tile(8 * n, F32, 'xn'); sq = P.tile(8 * n, BF16, 'sq'); rs = P.tile(n, F32, 'rs')
        xm3 = v3(xm.ap, 8)
        zst = [P.tile(23 * n, F32, 'zst%d' % b) for b in range(2)]
        qst = [P.tile(5 * n, BF16, 'qst%d' % b) for b in range(2)]
        vst = [P.tile(128, BF16, 'vst%d' % b) for b in range(2)]
        sqz = [P.tile(n, BF16, 'sqz%d' % b) for b in range(2)]
        rq = [P.tile(n, F32, 'rq%d' % b) for b in range(2)]
        qn = [P.tile(n, BF16, 'qn%d' % b) for b in range(2)]
        t1 = [P.tile(n, F32, 't1%d' % b) for b in range(2)]
        t2 = [P.tile(n, F32, 't2%d' % b) for b in range(2)]
        cs = [P.tile(n, F32, 'cos%d' % b) for b in range(2)]
        sn = [P.tile(n, F32, 'sin%d' % b) for b in range(2)]
        chunks = self.chunks
        self.load_norm(chunks[0][0], n, chunks[0][2], 1, mod, xts[0], xm, sq, rs, xn)
        for ci, (c0, _, v) in enumerate(chunks):
            zs = zst[ci % 2]; qs = qst[ci % 2]
            zs3 = v3(zs.ap, 23); qs3 = v3(qs.ap, 5)
            if not v:
                P.dma('sp', cs[ci % 2].ap, dr['rope_cos'][:, c0 - CTX:c0 - CTX + n], writes=[cs[ci % 2]])
                P.dma('sp', sn[ci % 2].ap, dr['rope_sin'][:, c0 - CTX:c0 - CTX + n], writes=[sn[ci % 2]])
            for j in range(29):
                if j == 5:
                    continue
                pz = P.ps()
                for k in range(8):
                    P.op('pe', lambda e, k=k, j=j, pz=pz: e.matmul(pz.ap[:, 0:n], win3[:, k, j * 128:(j + 1) * 128], xm3[:, k, :], start=(k == 0), stop=(k == 7)), [win, xm], [pz])
                if j < 5:
                    b = j % 2
                    g_ = gq if j < 4 else gk
                    P.op('act', lambda e, pz=pz, b=b: e.activation(out=sqz[b].ap, in_=pz.ap[:, 0:n], func=AF.Square), [pz], [sqz[b]])
                    p2 = P.ps()
                    P.op('pe', lambda e, p2=p2, b=b: e.matmul(p2.ap[:, 0:n], self.blk64.ap, sqz[b].ap, start=True, stop=True), [self.blk64, sqz[b]], [p2])
                    P.op('act', lambda e, p2=p2, b=b: e.activation(out=rq[b].ap, in_=p2.ap[:, 0:n], func=AF.Sqrt, bias=self.eps_ap, scale=1.0 / 64), [p2, self.eps_t], [rq[b]])
                    P.op('dve', lambda e, b=b: e.reciprocal(out=rq[b].ap, in_=rq[b].ap), [rq[b]], [rq[b]])
                    if v:
                        P.op('dve', lambda e, pz=pz, b=b, g_=g_, j=j, qs3=qs3: e.scalar_tensor_tensor(out=qs3[:, j, :], in0=pz.ap[:, 0:n], scalar=g_.ap[:, 0:1], in1=rq[b].ap, op0=ALU.mult, op1=ALU.mult), [pz, g_, rq[b]], [qs])
                    else:
                        P.op('dve', lambda e, pz=pz, b=b, g_=g_: e.scalar_tensor_tensor(out=qn[b].ap, in0=pz.ap[:, 0:n], scalar=g_.ap[:, 0:1], in1=rq[b].ap, op0=ALU.mult, op1=ALU.mult), [pz, g_, rq[b]], [qn[b]])
                        p3 = P.ps()
                        P.op('pe', lambda e, p3=p3, b=b: e.matmul(p3.ap[:, 0:n], self.rotm.ap, qn[b].ap, start=True, stop=True), [self.rotm, qn[b]], [p3])
                        P.op('dve', lambda e, b=b, ci=ci: e.tensor_tensor(out=t1[b].ap, in0=qn[b].ap, in1=cs[ci % 2].ap, op=ALU.mult), [qn[b], cs[ci % 2]], [t1[b]])
                        P.op('dve', lambda e, b=b, ci=ci, p3=p3: e.tensor_tensor(out=t2[b].ap, in0=p3.ap[:, 0:n], in1=sn[ci % 2].ap, op=ALU.mult), [p3, sn[ci % 2]], [t2[b]])
                        P.op('pool', lambda e, b=b, j=j, qs3=qs3: e.tensor_tensor(out=qs3[:, j, :], in0=t1[b].ap, in1=t2[b].ap, op=ALU.add), [t1[b], t2[b]], [qs])
                else:
                    jj = j - 6
                    if jj % 2 == 0:
                        P.op('act', lambda e, pz=pz, jj=jj, zs3=zs3: e.copy(out=zs3[:, jj, :], in_=pz.ap[:, 0:n]), [pz], [zs])
                    else:
                        P.op('dve', lambda e, pz=pz, jj=jj, zs3=zs3: e.tensor_copy(out=zs3[:, jj, :], in_=pz.ap[:, 0:n]), [pz], [zs])
            for tt in range(n // 128):
                pv = P.ps()
                for k in range(8):
                    P.op('pe', lambda e, k=k, tt=tt, pv=pv: e.matmul(pv.ap[:, 0:128], xm3[:, k, tt * 128:(tt + 1) * 128], win3[:, k, 640:768], start=(k == 0), stop=(k == 7)), [win, xm], [pv])
                vs = vst[tt % 2]
                P.op('act', lambda e, pv=pv, vs=vs: e.copy(out=vs.ap, in_=pv.ap[:, 0:128]), [pv], [vs])
                P.dma('sp', dr['VA'][c0 + tt * 128:c0 + (tt + 1) * 128, :], vs.ap, reads=[vs], writes=[sb['VA']])
            P.dma('sp', dr['QT'][:, c0:c0 + n].rearrange("(j p) t -> p j t", p=128), qs3[:, 0:4, :], reads=[qs], writes=[sb['QT']])
            P.dma('sp', dr['KT'][:, c0:c0 + n], qs3[:, 4, :], reads=[qs], writes=[sb['KT']])
            P.dma('act', dr['ZL'][:, c0:c0 + n].rearrange("(j p) t -> p j t", p=128), zs3[:, 0:8, :], reads=[zs], writes=[sb['ZL']])
            P.dma('act', dr['ZR'][:, c0:c0 + n].rearrange("(j p) t -> p j t", p=128), zs3[:, 8:23, :], reads=[zs], writes=[sb['ZR']])
            if ci + 1 < len(chunks):
                self.load_norm(chunks[ci + 1][0], n, chunks[ci + 1][2], 1, mod, xts[(ci + 1) % 2], xm, sq, rs, xn)
        P.barrier()
        P.off = mark

    def attn(self, l):
        P, dr, sb = self.P, self.dr, self.sb
        S, T = self.S, self.T
        mark = P.off
        kt = P.tile(2 * S, BF16, 'kt')
        kt3 = v3(kt.ap, 2)
        P.dma('sp', kt3[0:64, :, :], dr['KT'].rearrange("(g d) t -> d g t", d=64), reads=[sb['KT']], writes=[kt])
        va = P.tile(S, BF16, 'va')
        va3 = va.ap.rearrange("p (n c) -> p n c", c=128)
        P.dma('sp', va3, dr['VA'].rearrange("(n p) c -> p n c", p=128), reads=[sb['VA']], writes=[va])
        sk = P.tile(8, F32, 'sk')
        P.dma('sp', sk.ap[0:64, :], dr['attn_sink'][l].partition_broadcast(64), writes=[sk])
        P.op('act', lambda e: e.activation(out=sk.ap[0:64, :], in_=sk.ap[0:64, :], func=AF.Exp), [sk], [sk])
        se = [P.tile(512, F32, 'se%d' % g) for g in range(2)]
        for g in range(2):
            P.op('dve', lambda e, g=g: e.tensor_copy(out=v3(se[g].ap, 4)[0:64], in_=bc(sk.ap[0:64, 4 * g:4 * g + 4], [64, 4, 128], 2)), [sk], [se[g]])
        qts = [P.tile(8 * 128, BF16, 'qt%d' % b) for b in range(2)]
        pts = [P.tile(512, BF16, 'pt%d' % b) for b in range(6)]
        dens = [P.tile(512, F32, 'den%d' % b) for b in range(2)]
        yos = [P.tile(512, BF16, 'yo%d' % b) for b in range(2)]
        nb = T // 128
        pti = 0
        for qb in range(S // 128):
            t0 = qb * 128
            qt = qts[qb % 2]
            qt3 = v3(qt.ap, 8)
            P.dma('sp', qt3[0:64], dr['QT'].rearrange("(h d) t -> d h t", d=64)[:, :, t0:t0 + 128], reads=[sb['QT']], writes=[qt])
            keys = [(0, None), (1, None)]
            if t0 >= CTX:
                i = qb - 2
                if i > 0:
                    keys.append((qb - 1, self.mlo))
                keys.append((qb, None))
                if i < nb - 1:
                    keys.append((qb + 1, self.mhi))
            for g in range(2):
                ptl = []
                for (kb, msk) in keys:
                    ps = P.ps()
                    P.op('pe', lambda e, ps=ps, kb=kb, g=g, qt3=qt3: e.matmul(ps.ap, kt3[0:64, g, kb * 128:(kb + 1) * 128], qt3[0:64, 4 * g:4 * g + 4, :], start=True, stop=True), [kt, qt], [ps])
                    pt = pts[pti % 6]; pti += 1
                    P.op('act', lambda e, ps=ps, pt=pt: e.activation(out=pt.ap, in_=ps.ap, func=AF.Exp, scale=0.125), [ps], [pt])
                    if msk is not None:
                        P.op('pool', lambda e, pt=pt, msk=msk: e.tensor_tensor(out=pt.ap, in0=pt.ap, in1=msk.ap, op=ALU.mult), [pt, msk], [pt])
                    ptl.append((kb, pt))
                po = P.ps(); pd = P.ps()
                for idx, (kb, pt) in enumerate(ptl):
                    P.op('pe', lambda e, po=po, kb=kb, pt=pt, g=g, idx=idx, nk=len(ptl): e.matmul(po.ap[0:64, :], va3[:, kb, g * 64:(g + 1) * 64], pt.ap, start=(idx == 0), stop=(idx == nk - 1)), [va, pt], [po])
                for idx, (kb, pt) in enumerate(ptl):
                    P.op('pe', lambda e, pd=pd, pt=pt, idx=idx, nk=len(ptl): e.matmul(pd.ap[0:64, :], self.onesb.ap[:, 0:64], pt.ap, start=(idx == 0), stop=(idx == nk - 1)), [self.onesb, pt], [pd])
                den = dens[g]; yo = yos[g]
                P.op('dve', lambda e, pd=pd, den=den, g=g: e.tensor_tensor(out=den.ap[0:64], in0=pd.ap[0:64, :], in1=se[g].ap[0:64], op=ALU.add), [pd, se[g]], [den])
                P.op('dve', lambda e, den=den: e.reciprocal(out=den.ap[0:64], in_=den.ap[0:64]), [den], [den])
                P.op('dve', lambda e, po=po, den=den, yo=yo: e.tensor_tensor(out=yo.ap[0:64], in0=po.ap[0:64, :], in1=den.ap[0:64], op=ALU.mult), [po, den], [yo])
                P.dma('sp', dr['YA'].rearrange("(h d) t -> d h t", d=64)[:, 4 * g:4 * g + 4, t0:t0 + 128], v3(yo.ap, 4)[0:64], reads=[yo], writes=[sb['YA']])
        P.barrier()
        P.off = mark

    def lru(self, l):
        P, dr, sb = self.P, self.dr, self.sb
        S, T = self.S, self.T
        mark = P.off
        CH = 512
        st = P.tile(128, F32, 'lst')
        P.dma('sp', st.ap[0:16, :], dr['lru_conv_w'][l].rearrange("c (j p) -> (c j) p", p=128), writes=[st])
        P.dma('sp', st.ap[16:20, :], dr['lru_conv_b'][l].rearrange("(j p) -> j p", p=128), writes=[st])
        P.dma('sp', st.ap[20:36, :], dr['lru_gate_b'][l].rearrange("d g (j p) -> (d g j) p", p=128), writes=[st])
        P.dma('sp', st.ap[36:44, :], dr['lru_lambda'][l].rearrange("d (j p) -> (d j) p", p=128), writes=[st])
        ps = P.ps()
        P.op('pe', lambda e: e.transpose(ps.ap[:, 0:44], st.ap[0:44, :], self.ident.ap[0:44, 0:44]), [st, self.ident], [ps])
        lv = P.tile(44, F32, 'lv')
        P.op('dve', lambda e: e.tensor_copy(out=lv.ap, in_=ps.ap[:, 0:44]), [ps], [lv])
        cc = P.tile(8, F32, 'lcc'); cc2 = P.tile(8, F32, 'lcc2')
        P.op('act', lambda e: e.activation(out=cc.ap, in_=lv.ap[:, 36:44], func=AF.Exp, scale=-1.0), [lv], [cc])
        P.op('act', lambda e: e.activation(out=cc.ap, in_=cc.ap, func=AF.Ln, bias=self.one_ap, scale=1.0), [cc, self.one_t], [cc])
        P.op('dve', lambda e: e.tensor_scalar(out=cc2.ap, in0=cc.ap, scalar1=-16.0, scalar2=None, op0=ALU.mult), [cc], [cc2])
        P.op('dve', lambda e: e.tensor_scalar(out=cc.ap, in0=cc.ap, scalar1=-8.0, scalar2=None, op0=ALU.mult), [cc], [cc])
        gw = P.tile(16 * 128, BF16, 'lgw')
        gw3 = v3(gw.ap, 16)
        P.op('dve', lambda e: e.memset(gw.ap, 0.0), [], [gw])
        for j in range(4):
            for d in range(2):
                for gt in range(2):
                    idx = (j * 2 + d) * 2 + gt
                    for hb in range(2):
                        P.dma('pool', gw3[hb * 64:(hb + 1) * 64, idx, hb * 64:(hb + 1) * 64], dr['lru_gate_w'][l, d, gt, 2 * j + hb], writes=[gw])
        ux = P.tile(S, F32, 'ux')
        ug = P.tile(S, F32, 'ug')
        xl = P.tile(S, F32, 'xl')
        xlb = P.tile(S, BF16, 'xlb')
        NB = 2
        rr = [P.tile(CH, F32, 'lr%d' % b) for b in range(NB)]
        ii = [P.tile(CH, F32, 'li%d' % b) for b in range(NB)]
        aa = [P.tile(CH, F32, 'la%d' % b) for b in range(NB)]
        a2 = [P.tile(CH, F32, 'la2%d' % b) for b in range(NB)]
        hh = [P.tile(CH, F32, 'lh%d' % b) for b in range(NB)]
        yb = P.tile(S, BF16, 'lyb')
        segs = [(0, CTX), (CTX, S)]
        itc = [0]

        def do_tile(j):
            P.dma('sp', ux.ap, dr['ZL'][j * 128:(j + 1) * 128, :], reads=[sb['ZL']], writes=[ux])
            P.dma('act', ug.ap, dr['ZL'][512 + j * 128:512 + (j + 1) * 128, :], reads=[sb['ZL']], writes=[ug])
            w = lambda c, j=j: lv.ap[:, c * 4 + j:c * 4 + j + 1]
            for (a, b) in segs:
                P.op('dve', lambda e, a=a, b=b: e.tensor_scalar(out=xl.ap[:, a:b], in0=ux.ap[:, a:b], scalar1=w(2), scalar2=lv.ap[:, 16 + j:17 + j], op0=ALU.mult, op1=ALU.add), [ux, lv], [xl])
                P.op('dve', lambda e, a=a, b=b: e.scalar_tensor_tensor(out=xl.ap[:, a + 1:b], in0=ux.ap[:, a:b - 1], scalar=w(1), in1=xl.ap[:, a + 1:b], op0=ALU.mult, op1=ALU.add), [ux, lv, xl], [xl])
                P.op('dve', lambda e, a=a, b=b: e.scalar_tensor_tensor(out=xl.ap[:, a + 2:b], in0=ux.ap[:, a:b - 2], scalar=w(0), in1=xl.ap[:, a + 2:b], op0=ALU.mult, op1=ALU.add), [ux, lv, xl], [xl])
                P.op('dve', lambda e, a=a, b=b: e.scalar_tensor_tensor(out=xl.ap[:, a:b - 1], in0=ux.ap[:, a + 1:b], scalar=w(3), in1=xl.ap[:, a:b - 1], op0=ALU.mult, op1=ALU.add), [ux, lv, xl], [xl])
            P.op('act', lambda e: e.copy(out=xlb.ap, in_=xl.ap), [xl], [xlb])
            hs = ux
            for d in range(2):
                chs = [(c, min(CH, (CTX if c < CTX else S) - c)) for c in list(range(0, CTX, CH)) + list(range(CTX, S, CH))]
                if d == 1:
                    chs = [c for c in chs if c[0] < CTX][::-1] + [c for c in chs if c[0] >= CTX][::-1]
                prev = None
                for (c0, n) in chs:
                    b = itc[0] % NB; itc[0] += 1
                    pr = P.ps(); pi = P.ps()
                    ir = (j * 2 + d) * 2
                    P.op('pe', lambda e, pr=pr, ir=ir, c0=c0, n=n: e.matmul(pr.ap[:, 0:n], gw3[:, ir, :], xlb.ap[:, c0:c0 + n], start=True, stop=True), [gw, xlb], [pr])
                    P.op('pe', lambda e, pi=pi, ir=ir, c0=c0, n=n: e.matmul(pi.ap[:, 0:n], gw3[:, ir + 1, :], xlb.ap[:, c0:c0 + n], start=True, stop=True), [gw, xlb], [pi])
                    cb = 20 + (d * 2) * 4 + j
                    P.op('act', lambda e, pr=pr, b=b, n=n, cb=cb: e.activation(out=rr[b].ap[:, 0:n], in_=pr.ap[:, 0:n], func=AF.Sigmoid, bias=lv.ap[:, cb:cb + 1], scale=1.0), [pr, lv], [rr[b]])
                    P.op('act', lambda e, pi=pi, b=b, n=n, cb=cb: e.activation(out=ii[b].ap[:, 0:n], in_=pi.ap[:, 0:n], func=AF.Sigmoid, bias=lv.ap[:, cb + 4:cb + 5], scale=1.0), [pi, lv], [ii[b]])
                    P.op('act', lambda e, b=b, n=n, d=d: e.activation(out=aa[b].ap[:, 0:n], in_=rr[b].ap[:, 0:n], func=AF.Exp, scale=cc.ap[:, d * 4 + j:d * 4 + j + 1]), [rr[b], cc], [aa[b]])
                    P.op('act', lambda e, b=b, n=n, d=d: e.activation(out=a2[b].ap[:, 0:n], in_=rr[b].ap[:, 0:n], func=AF.Exp, scale=cc2.ap[:, d * 4 + j:d * 4 + j + 1]), [rr[b], cc2], [a2[b]])
                    P.op('act', lambda e, b=b, n=n: e.activation(out=a2[b].ap[:, 0:n], in_=a2[b].ap[:, 0:n], func=AF.Sqrt, bias=self.one_ap, scale=-1.0), [a2[b], self.one_t], [a2[b]])
                    P.op('dve', lambda e, b=b, n=n, c0=c0: e.tensor_tensor(out=ii[b].ap[:, 0:n], in0=ii[b].ap[:, 0:n], in1=xl.ap[:, c0:c0 + n], op=ALU.mult), [ii[b], xl], [ii[b]])
                    P.op('dve', lambda e, b=b, n=n: e.tensor_tensor(out=ii[b].ap[:, 0:n], in0=ii[b].ap[:, 0:n], in1=a2[b].ap[:, 0:n], op=ALU.mult), [ii[b], a2[b]], [ii[b]])
                    if d == 0:
                        init = 0.0 if prev is None else hs.ap[:, c0 - 1:c0]
                        P.op('dve', lambda e, b=b, n=n, c0=c0, init=init: e.tensor_tensor_scan(hs.ap[:, c0:c0 + n], aa[b].ap[:, 0:n], ii[b].ap[:, 0:n], init, ALU.mult, ALU.add), [aa[b], ii[b], hs], [hs])
                    else:
                        if prev is None:
                            init = 0.0
                        elif c0 + n == S:
                            init = hh[(itc[0] - 2) % NB].ap[:, 0:1]
                        else:
                            init = hh[(itc[0] - 2) % NB].ap[:, 0:1]
                        rv = lambda ap, n=n: bass.AP(ap.tensor, ap.offset + n - 1, [list(ap.ap[0]), [-1, n]])
                        P.op('dve', lambda e, b=b, n=n, init=init, rv=rv: e.tensor_tensor_scan(rv(hh[b].ap[:, 0:n]), rv(aa[b].ap[:, 0:n]), rv(ii[b].ap[:, 0:n]), init, ALU.mult, ALU.add), [aa[b], ii[b], hh[(itc[0] - 2) % NB]], [hh[b]])
                        P.op('pool', lambda e, b=b, n=n, c0=c0: e.tensor_tensor(out=hs.ap[:, c0:c0 + n], in0=hs.ap[:, c0:c0 + n], in1=hh[b].ap[:, 0:n], op=ALU.add), [hs, hh[b]], [hs])
                    prev = (c0, n)
            P.op('act', lambda e: e.activation(out=xl.ap, in_=ug.ap, func=AF.Square), [ug], [xl])
            P.op('dve', lambda e: e.tensor_scalar(out=xl.ap, in0=xl.ap, scalar1=0.044715, scalar2=1.0, op0=ALU.mult, op1=ALU.add), [xl], [xl])
            P.op('dve', lambda e: e.tensor_tensor(out=xl.ap, in0=xl.ap, in1=ug.ap, op=ALU.mult), [xl, ug], [xl])
            P.op('act', lambda e: e.activation(out=xl.ap, in_=xl.ap, func=AF.Sigmoid, scale=1.5957691216057308), [xl], [xl])
            P.op('dve', lambda e: e.tensor_tensor(out=xl.ap, in0=xl.ap, in1=ug.ap, op=ALU.mult), [xl, ug], [xl])
            P.op('dve', lambda e: e.tensor_tensor(out=yb.ap, in0=xl.ap, in1=hs.ap, op=ALU.mult), [xl, hs], [yb])
            P.dma('sp', dr['YL'][j * 128:(j + 1) * 128, :], yb.ap, reads=[yb], writes=[sb['YL']])
        for j in range(4):
            do_tile(j)
        P.barrier()
        P.off = mark

    def rwkv_prep(self, l):
        P, dr, sb = self.P, self.dr, self.sb
        S = self.S
        mark = P.off
        st = P.tile(128, F32, 'rp_stage')
        P.dma('sp', st.ap[0:30, :], dr['rwkv_mu'][l].rearrange("m (j p) -> (m j) p", p=128), writes=[st])
        P.dma('sp', st.ap[30:34, :], dr['rwkv_k_k'][l].rearrange("(j p) -> j p", p=128), writes=[st])
        ps = P.ps()
        P.op('pe', lambda e: e.transpose(ps.ap[:, 0:34], st.ap[0:34, :], self.ident.ap[0:34, 0:34]), [st, self.ident], [ps])
        rv = P.tile(64, F32, 'rp_vec')
        P.op('dve', lambda e: e.tensor_copy(out=rv.ap[:, 0:34], in_=ps.ap[:, 0:34]), [ps], [rv])
        P.op('dve', lambda e: e.tensor_tensor(out=rv.ap[:, 34:49], in0=rv.ap[:, 0:15], in1=rv.ap[:, 15:30], op=ALU.add), [rv], [rv])
        P.op('dve', lambda e: e.tensor_scalar(out=rv.ap[:, 34:49], in0=rv.ap[:, 34:49], scalar1=-1.0, scalar2=1.0, op0=ALU.mult, op1=ALU.add), [rv], [rv])
        zin = [P.tile(S, F32, 'rp_zin%d' % i) for i in range(2)]
        zs = P.tile(S, F32, 'rp_zs')
        tmp = P.tile(S, F32, 'rp_tmp')
        zbs = [P.tile(S, BF16, 'rp_zb%d' % i) for i in range(2)]
        vsts = [P.tile(1024, BF16, 'rp_vst%d' % i) for i in range(2)]
        segs = [(0, CTX), (CTX, S)]
        cnt = [0]

        def do_tile(jt):
            zi = zin[jt % 2]
            zb = zbs[jt % 2]
            P.dma('sp' if jt % 2 == 0 else 'act', zi.ap, dr['ZR'][jt * 128:(jt + 1) * 128, :], reads=[sb['ZR']], writes=[zi])
            for (a, b) in segs:
                P.op('dve', lambda e, a=a, b=b: e.tensor_scalar(out=zs.ap[:, a:b], in0=zi.ap[:, a:b], scalar1=rv.ap[:, 34 + jt:35 + jt], scalar2=None, op0=ALU.mult), [zi, rv], [zs])
                P.op('dve', lambda e, a=a, b=b: e.scalar_tensor_tensor(out=zs.ap[:, a + 1:b], in0=zi.ap[:, a:b - 1], scalar=rv.ap[:, jt:jt + 1], in1=zs.ap[:, a + 1:b], op0=ALU.mult, op1=ALU.add), [zi, rv, zs], [zs])
                P.op('dve', lambda e, a=a, b=b: e.scalar_tensor_tensor(out=zs.ap[:, a:b - 1], in0=zi.ap[:, a + 1:b], scalar=rv.ap[:, 15 + jt:16 + jt], in1=zs.ap[:, a:b - 1], op0=ALU.mult, op1=ALU.add), [zi, rv, zs], [zs])
            if jt < 4:
                P.dma('sp', dr['RS'][jt * 128:(jt + 1) * 128, :], zs.ap, reads=[zs], writes=[sb['RS']])
            elif jt < 8:
                hp = jt - 4
                P.dma('sp', dr['KS'][hp * 128:(hp + 1) * 128, :], zs.ap, reads=[zs], writes=[sb['KS']])
                P.op('dve', lambda e: e.tensor_scalar(out=tmp.ap, in0=zs.ap, scalar1=rv.ap[:, 30 + hp:31 + hp], scalar2=None, op0=ALU.mult), [zs, rv], [tmp])
                P.op('act', lambda e: e.activation(out=zb.ap, in_=tmp.ap, func=AF.Square), [tmp], [zb])
                for blk in range(0, S, 512):
                    n = min(512, S - blk)
                    p2 = P.ps()
                    P.op('pe', lambda e, p2=p2, blk=blk, n=n: e.matmul(p2.ap[:, 0:n], self.blk64.ap, zb.ap[:, blk:blk + n], start=True, stop=True), [self.blk64, zb], [p2])
                    P.op('act', lambda e, p2=p2, blk=blk, n=n: e.activation(out=zs.ap[:, blk:blk + n], in_=p2.ap[:, 0:n], func=AF.Sqrt), [p2], [zs])
                P.op('dve', lambda e: e.tensor_scalar(out=zs.ap, in0=zs.ap, scalar1=1e-12, scalar2=None, op0=ALU.max), [zs], [zs])
                P.op('dve', lambda e: e.reciprocal(out=zs.ap, in_=zs.ap), [zs], [zs])
                P.op('dve', lambda e: e.tensor_tensor(out=tmp.ap, in0=tmp.ap, in1=zs.ap, op=ALU.mult), [tmp, zs], [tmp])
                P.dma('sp', dr['KK'][hp * 128:(hp + 1) * 128, :], tmp.ap, reads=[tmp], writes=[sb['KK']])
            elif jt < 12:
                hp = jt - 8
                P.op('act', lambda e: e.copy(out=zb.ap, in_=zs.ap), [zs], [zb])
                nt = S // 128
                for g0 in range(0, nt, 8):
                    ng = min(8, nt - g0)
                    pb = P.ps()
                    pbb = pb.ap.bitcast(BF16)
                    for i in range(ng):
                        P.op('pe', lambda e, pbb=pbb, i=i, g0=g0: e.transpose(pbb[:, i * 128:(i + 1) * 128], zb.ap[:, (g0 + i) * 128:(g0 + i + 1) * 128], self.identb.ap), [zb, self.identb], [pb])
                    vs = vsts[cnt[0] % 2]; cnt[0] += 1
                    if cnt[0] % 2 == 0:
                        P.op('act', lambda e, pbb=pbb, vs=vs, ng=ng: e.copy(out=vs.ap[:, 0:ng * 128], in_=pbb[:, 0:ng * 128]), [pb], [vs])
                    else:
                        P.op('dve', lambda e, pbb=pbb, vs=vs, ng=ng: e.tensor_copy(out=vs.ap[:, 0:ng * 128], in_=pbb[:, 0:ng * 128]), [pb], [vs])
                    P.dma('sp', dr['VT'][g0 * 128:(g0 + ng) * 128, hp * 128:(hp + 1) * 128].rearrange("(n p) c -> p n c", p=128), vs.ap[:, 0:ng * 128].rearrange("p (n c) -> p n c", c=128), reads=[vs], writes=[sb['VT']])
            else:
                nm = ('TW', 'AD', 'SG')[jt - 12]
                fn = (AF.Tanh, AF.Identity, AF.Sigmoid)[jt - 12]
                P.op('act', lambda e: e.activation(out=zb.ap, in_=zs.ap, func=fn), [zs], [zb])
                P.dma('sp', dr[nm][:, :], zb.ap, reads=[zb], writes=[sb[nm]])
        for jt in range(15):
            do_tile(jt)
        P.barrier()
        P.off = mark

    def rwkv_dir(self, l, d):
        P, dr, sb = self.P, self.dr, self.sb
        S, T = self.S, self.T
        mark = P.off
        LAM = float(np.exp(-0.5))
        lo, hi = d * 64, (d + 1) * 64
        wup = P.tile(512, BF16, 'r_wup'); aup = P.tile(512, BF16, 'r_aup')
        P.op('pool', lambda e: e.memset(wup.ap, 0.0), [], [wup])
        P.op('pool', lambda e: e.memset(aup.ap, 0.0), [], [aup])
        P.dma('pool', wup.ap[lo:hi, :], dr['rwkv_w_up'][l, d], writes=[wup])
        P.dma('pool', aup.ap[lo:hi, :], dr['rwkv_a_up'][l, d], writes=[aup])
        st = P.tile(128, F32, 'r_stage')
        P.dma('sp', st.ap[0:4, :], dr['rwkv_w0'][l, d].rearrange("(j p) -> j p", p=128), writes=[st])
        P.dma('sp', st.ap[4:8, :], dr['rwkv_a0'][l, d].rearrange("(j p) -> j p", p=128), writes=[st])
        P.dma('sp', st.ap[8:12, :], dr['rwkv_k_a'][l].rearrange("(j p) -> j p", p=128), writes=[st])
        P.dma('sp', st.ap[12:16, :], dr['rwkv_r_k'][l].rearrange("(j p) -> j p", p=128), writes=[st])
        ps0 = P.ps()
        P.op('pe', lambda e: e.transpose(ps0.ap[:, 0:16], st.ap[0:16, :], self.ident.ap[0:16, 0:16]), [st, self.ident], [ps0])
        vv = P.tile(24, F32, 'r_vv')
        P.op('dve', lambda e: e.tensor_copy(out=vv.ap[:, 0:16], in_=ps0.ap[:, 0:16]), [ps0], [vv])
        P.op('dve', lambda e: e.tensor_scalar(out=vv.ap[:, 16:20], in0=vv.ap[:, 8:12], scalar1=-1.0, scalar2=1.0, op0=ALU.mult, op1=ALU.add), [vv], [vv])
        if d == 1:
            gup = P.tile(512, BF16, 'r_gup')
            P.dma('pool', gup.ap, dr['rwkv_g_up'][l], writes=[gup])
            lng = P.tile(512, F32, 'r_lng'); lnb = P.tile(512, F32, 'r_lnb')
            P.dma('sp', lng.ap, dr['rwkv_ln_g'][l].partition_broadcast(128), writes=[lng])
            P.dma('sp', lnb.ap, dr['rwkv_ln_b'][l].partition_broadcast(128), writes=[lnb])
        A = P.tile(256, F32, 'r_A'); Ap = P.tile(256, BF16, 'r_Ap')
        P.op('dve', lambda e: e.memset(A.ap, 0.0), [], [A])
        BTP = [P.tile(1024, BF16, 'r_btp%d' % i) for i in range(2)]
        KTP = [P.tile(1024, BF16, 'r_ktp%d' % i) for i in range(2)]
        for t_ in BTP + KTP:
            P.op('pool', lambda e, t_=t_: e.memset(t_.ap, 0.0), [], [t_])
        def blkset(i):
            return dict(KR=P.tile(4096, BF16, 'r_KR%d' % i), BT=P.tile(2048, BF16, 'r_BT%d' % i), KT=P.tile(2048, BF16, 'r_KT%d' % i),
                        PR=P.tile(2048, F32, 'r_PR%d' % i), ecm=P.tile(16, F32, 'r_ecm%d' % i), efin=P.tile(16, F32, 'r_efin%d' % i),
                        wtot=P.tile(16, F32, 'r_wtot%d' % i), tw=P.tile(512, BF16, 'r_tw%d' % i), ad=P.tile(512, BF16, 'r_ad%d' % i),
                        sgd=P.tile(512, BF16, 'r_sgd%d' % i),
                        KRZ=P.tile(8192, BF16, 'r_KRZ%d' % i), BZ=P.tile(4096, BF16, 'r_BZ%d' % i), KZ=P.tile(4096, BF16, 'r_KZ%d' % i))
        bsets = [blkset(0)]
        bsets.append(bsets[0])
        for nm_ in ('KRZ', 'BZ', 'KZ'):
            P.op('pool', lambda e, nm_=nm_: e.memset(bsets[0][nm_].ap, 0.0), [], [bsets[0][nm_]])
        NT_ = 1
        tl = {nm: [P.tile(512, F32, 'r_%s%d' % (nm, i)) for i in range(NT_)] for nm in ('r', 'k', 'kk', 'sg', 'aa', 'kd', 'beta', 'cs', 'cse', 'dinc', 'e1', 'e2', 'e3')}
        G = [P.tile(8 * 512, BF16, 'r_G%d' % i) for i in range(2)]
        Xb = [P.tile(1024, BF16, 'r_X%d' % i) for i in range(2)]
        XTb = [P.tile(1024, BF16, 'r_XT%d' % i) for i in range(2)]
        PTb = [P.tile(1024, BF16, 'r_PT%d' % i) for i in range(2)]
        vms = [P.tile(512, BF16, 'r_vm%d' % i) for i in range(2)]
        rhs_t = P.tile(512, BF16, 'r_rhs'); u_t = P.tile(512, BF16, 'r_u')
        stmp = P.tile(256, F32, 'r_stmp')
        bon = P.tile(512, F32, 'r_bon'); oacc = [P.tile(512, F32, 'r_oacc%d' % i) for i in range(2)]
        if d == 1:
            osp = [P.tile(512, F32, 'r_osp%d' % i) for i in range(2)]
            gn = {nm: P.tile(n_, F32, 'r_gn_' + nm) for nm, n_ in (('mean', 8), ('var', 8), ('cen', 512), ('sq', 512))}
            yb = P.tile(512, BF16, 'r_yb'); yst = [P.tile(512, BF16, 'r_yst%d' % i) for i in range(2)]
        blocks = [(0, CTX)] + [(CTX + 512 * i, 512) for i in range(T // 512)]
        if d == 1:
            blocks = [blocks[0]] + blocks[1:][::-1]
        cc = [0]
        tc = [0]

        def prep_block(bi, c0, n):
            bs = bsets[bi % 2]
            nch = n // 128
            P.dma('sp', bs['tw'].ap[:, 0:n], dr['TW'][:, c0:c0 + n], reads=[sb['TW']], writes=[bs['tw']])
            P.dma('sp', bs['ad'].ap[:, 0:n], dr['AD'][:, c0:c0 + n], reads=[sb['AD']], writes=[bs['ad']])
            if d == 1:
                P.dma('sp', bs['sgd'].ap[:, 0:n], dr['SG'][:, c0:c0 + n], reads=[sb['SG']], writes=[bs['sgd']])
            KR5 = bs['KR'].ap.rearrange("p (h c x t) -> p h c x t", h=4, c=4, x=2)
            BT4 = bs['BT'].ap.rearrange("p (h c t) -> p h c t", h=4, c=4)
            KT4 = bs['KT'].ap.rearrange("p (h c t) -> p h c t", h=4, c=4)
            PR3 = v3(bs['PR'].ap, 4)
            ecm3 = v3(bs['ecm'].ap, 4); efin3 = v3(bs['efin'].ap, 4); wtot3 = v3(bs['wtot'].ap, 4)
            KRZ6 = bs['KRZ'].ap.rearrange("p (h c x q t) -> p h c x q t", h=4, c=4, x=2, q=2)
            BZ5 = bs['BZ'].ap.rearrange("p (h c q t) -> p h c q t", h=4, c=4, q=2)
            KZ5 = bs['KZ'].ap.rearrange("p (h c q t) -> p h c q t", h=4, c=4, q=2)
            def prep_hp(hp):
                b = tc[0] % NT_; tc[0] += 1
                t = {nm: tl[nm][b] for nm in tl}
                v_ = lambda nm: t[nm].ap[:, 0:n]
                c3 = lambda nm: t[nm].ap[:, 0:n].rearrange("p (c t) -> p c t", t=128)
                P.dma('sp', v_('r'), dr['RS'][hp * 128:(hp + 1) * 128, c0:c0 + n], reads=[sb['RS']], writes=[t['r']])
                P.dma('act', v_('k'), dr['KS'][hp * 128:(hp + 1) * 128, c0:c0 + n], reads=[sb['KS']], writes=[t['k']])
                P.dma('sp', v_('kk'), dr['KK'][hp * 128:(hp + 1) * 128, c0:c0 + n], reads=[sb['KK']], writes=[t['kk']])
                pw = P.ps(); pa = P.ps()
                P.op('pe', lambda e, pw=pw, hp=hp: e.matmul(pw.ap[:, 0:n], wup.ap[:, hp * 128:(hp + 1) * 128], bs['tw'].ap[:, 0:n], start=True, stop=True), [wup, bs['tw']], [pw])
                P.op('pe', lambda e, pa=pa, hp=hp: e.matmul(pa.ap[:, 0:n], aup.ap[:, hp * 128:(hp + 1) * 128], bs['ad'].ap[:, 0:n], start=True, stop=True), [aup, bs['ad']], [pa])
                P.op('act', lambda e, pw=pw, hp=hp, v_=v_: e.activation(out=v_('sg'), in_=pw.ap[:, 0:n], func=AF.Sigmoid, bias=vv.ap[:, hp:hp + 1], scale=1.0), [pw, vv], [t['sg']])
                P.op('act', lambda e, pa=pa, hp=hp, v_=v_: e.activation(out=v_('aa'), in_=pa.ap[:, 0:n], func=AF.Sigmoid, bias=vv.ap[:, 4 + hp:5 + hp], scale=1.0), [pa, vv], [t['aa']])
                P.op('dve', lambda e, hp=hp, v_=v_: e.tensor_scalar(out=v_('kd'), in0=v_('aa'), scalar1=vv.ap[:, 8 + hp:9 + hp], scalar2=vv.ap[:, 16 + hp:17 + hp], op0=ALU.mult, op1=ALU.add), [t['aa'], vv], [t['kd']])
                P.op('dve', lambda e, v_=v_: e.tensor_tensor(out=v_('kd'), in0=v_('kd'), in1=v_('k'), op=ALU.mult), [t['kd'], t['k']], [t['kd']])
                P.op('dve', lambda e, v_=v_: e.tensor_tensor(out=v_('beta'), in0=v_('aa'), in1=v_('kk'), op=ALU.mult), [t['aa'], t['kk']], [t['beta']])
                P.op('dve', lambda e, hp=hp, v_=v_: e.scalar_tensor_tensor(out=PR3[:, hp, 0:n], in0=v_('r'), scalar=vv.ap[:, 12 + hp:13 + hp], in1=v_('kd'), op0=ALU.mult, op1=ALU.mult), [t['r'], vv, t['kd']], [bs['PR']])
                if d == 0:
                    P.op('dve', lambda e, v_=v_: e.tensor_tensor_scan(v_('cs'), self.rst[0].ap[:, 0:n], v_('sg'), 0.0, ALU.mult, ALU.add), [self.rst[0], t['sg']], [t['cs']])
                else:
                    rv_ = lambda ap: bass.AP(ap.tensor, ap.offset + n - 1, [list(ap.ap[0]), [-1, n]])
                    P.op('dve', lambda e, v_=v_, rv_=rv_: e.tensor_tensor_scan(rv_(v_('cs')), rv_(self.rst[1].ap[:, 0:n]), rv_(v_('sg')), 0.0, ALU.mult, ALU.add), [self.rst[1], t['sg']], [t['cs']])
                P.op('dve', lambda e, v_=v_: e.tensor_tensor(out=v_('cse'), in0=v_('cs'), in1=v_('sg'), op=ALU.subtract), [t['cs'], t['sg']], [t['cse']])
                cmid = lambda c3=c3: c3('cs')[:, :, 63]
                P.op('dve', lambda e, c3=c3, cmid=cmid: e.tensor_tensor(out=c3('cse'), in0=c3('cse'), in1=bc(cmid(), [128, nch, 128], 2), op=ALU.subtract), [t['cse'], t['cs']], [t['cse']])
                P.op('dve', lambda e, c3=c3, cmid=cmid: e.tensor_tensor(out=c3('dinc'), in0=c3('cs'), in1=bc(cmid(), [128, nch, 128], 2), op=ALU.subtract), [t['cs']], [t['dinc']])
                P.op('act', lambda e, v_=v_: e.activation(out=v_('e1'), in_=v_('cse'), func=AF.Exp, scale=-LAM), [t['cse']], [t['e1']])
                P.op('act', lambda e, v_=v_: e.activation(out=v_('e2'), in_=v_('dinc'), func=AF.Exp, scale=-LAM), [t['dinc']], [t['e2']])
                P.op('act', lambda e, v_=v_: e.activation(out=v_('e3'), in_=v_('dinc'), func=AF.Exp, scale=LAM), [t['dinc']], [t['e3']])
                P.op('act', lambda e, hp=hp, cmid=cmid: e.activation(out=ecm3[:, hp, 0:nch], in_=cmid(), func=AF.Exp, scale=-LAM), [t['cs']], [bs['ecm']])
                ce = 127 if d == 0 else 0
                P.op('act', lambda e, hp=hp, c3=c3, ce=ce: e.activation(out=efin3[:, hp, 0:nch], in_=c3('dinc')[:, :, ce], func=AF.Exp, scale=-LAM), [t['dinc']], [bs['efin']])
                P.op('dve', lambda e, hp=hp: e.tensor_tensor(out=wtot3[:, hp, 0:nch], in0=efin3[:, hp, 0:nch], in1=ecm3[:, hp, 0:nch], op=ALU.mult), [bs['efin'], bs['ecm']], [bs['wtot']])
                P.op('dve', lambda e, hp=hp, c3=c3: e.scalar_tensor_tensor(out=KR5[:, hp, 0:nch, 0, :], in0=c3('kk'), scalar=-1.0, in1=c3('e1'), op0=ALU.mult, op1=ALU.mult), [t['kk'], t['e1']], [bs['KR']])
                P.op('dve', lambda e, hp=hp, c3=c3: e.tensor_tensor(out=KR5[:, hp, 0:nch, 1, :], in0=c3('r'), in1=c3('e2'), op=ALU.mult), [t['r'], t['e2']], [bs['KR']])
                P.op('dve', lambda e, hp=hp, c3=c3: e.tensor_tensor(out=BT4[:, hp, 0:nch, :], in0=c3('beta'), in1=c3('e3'), op=ALU.mult), [t['beta'], t['e3']], [bs['BT']])
                P.op('dve', lambda e, hp=hp, c3=c3: e.tensor_tensor(out=KT4[:, hp, 0:nch, :], in0=c3('kd'), in1=c3('e3'), op=ALU.mult), [t['kd'], t['e3']], [bs['KT']])
                for par in range(2):
                    psl = slice(par * 64, par * 64 + 64)
                    P.op('dve', lambda e, hp=hp, c3=c3, par=par, psl=psl: e.scalar_tensor_tensor(out=KRZ6[psl, hp, 0:nch, 0, par, :], in0=c3('kk')[psl], scalar=-1.0, in1=c3('e1')[psl], op0=ALU.mult, op1=ALU.mult), [t['kk'], t['e1']], [bs['KRZ']])
                    P.op('pool', lambda e, hp=hp, c3=c3, par=par, psl=psl: e.tensor_tensor(out=KRZ6[psl, hp, 0:nch, 1, par, :], in0=c3('r')[psl], in1=c3('e2')[psl], op=ALU.mult), [t['r'], t['e2']], [bs['KRZ']])
                    P.op('pool', lambda e, hp=hp, c3=c3, par=par, psl=psl: e.tensor_tensor(out=BZ5[psl, hp, 0:nch, par, :], in0=c3('beta')[psl], in1=c3('e3')[psl], op=ALU.mult), [t['beta'], t['e3']], [bs['BZ']])
                    P.op('pool', lambda e, hp=hp, c3=c3, par=par, psl=psl: e.tensor_tensor(out=KZ5[psl, hp, 0:nch, par, :], in0=c3('kd')[psl], in1=c3('e3')[psl], op=ALU.mult), [t['kd'], t['e3']], [bs['KZ']])
            for hp in range(4):
                prep_hp(hp)

        def do_chunk(bi, c0, ch):
            if RK_STOP == 0:
                return
            bs = bsets[bi % 2]
            ci = cc[0]; cc[0] += 1
            t0 = c0 + ch * 128
            KR5 = bs['KR'].ap.rearrange("p (h c x t) -> p h c x t", h=4, c=4, x=2)
            BT4 = bs['BT'].ap.rearrange("p (h c t) -> p h c t", h=4, c=4)
            KT4 = bs['KT'].ap.rearrange("p (h c t) -> p h c t", h=4, c=4)
            PR3 = v3(bs['PR'].ap, 4)
            ecm3 = v3(bs['ecm'].ap, 4); efin3 = v3(bs['efin'].ap, 4); wtot3 = v3(bs['wtot'].ap, 4)
            KRZ6 = bs['KRZ'].ap.rearrange("p (h c x q t) -> p h c x q t", h=4, c=4, x=2, q=2)
            BZ5 = bs['BZ'].ap.rearrange("p (h c q t) -> p h c q t", h=4, c=4, q=2)
            KZ5 = bs['KZ'].ap.rearrange("p (h c q t) -> p h c q t", h=4, c=4, q=2)
            vm = vms[ci % 2]
            P.dma('sp', vm.ap, dr['VT'][t0:t0 + 128, :], reads=[sb['VT']], writes=[vm])
            if d == 1:
                op_ = osp[ci % 2]
                P.dma('act', op_.ap, dr['OS'][t0:t0 + 128, :], reads=[sb['OS']], writes=[op_])
            Gt = G[ci % 2]
            G3 = v3(Gt.ap, 8)
            X = Xb; XT = XTb; PT = PTb
            x3 = [v3(X[i].ap, 8) for i in range(2)]
            xt3 = [v3(XT[i].ap, 8) for i in range(2)]
            pt3 = [v3(PT[i].ap, 8) for i in range(2)]
            hsl = lambda h: slice((h % 2) * 64, (h % 2) * 64 + 64)
            for h in range(8):
                hp = h // 2
                pg = P.ps()
                P.op('pe', lambda e, pg=pg, h=h, hp=hp: e.matmul(pg.ap[:, 0:256], BZ5[:, hp, ch, h % 2, :], KR5[:, hp, ch, :, :], start=True, stop=True), [bs['BZ'], bs['KR']], [pg])
                P.op('pe', lambda e, pg=pg, h=h, hp=hp: e.matmul(pg.ap[:, 256:512], KZ5[:, hp, ch, h % 2, :], KR5[:, hp, ch, :, :], start=True, stop=True), [bs['KZ'], bs['KR']], [pg])
                P.op('dve', lambda e, pg=pg, h=h: e.tensor_tensor(out=G3[:, h, :], in0=pg.ap, in1=self.mask4[d].ap, op=ALU.mult), [pg, self.mask4[d]], [Gt])
            if RK_STOP == 1:
                return
            for half in range(2):
                pn = P.ps()
                for hh in range(4):
                    h = half * 4 + hh
                    hp = h // 2
                    P.op('pe', lambda e, pn=pn, h=h, hp=hp, hh=hh: e.matmul(pn.ap[:, hh * 128:(hh + 1) * 128], KRZ6[:, hp, ch, 0, h % 2, :], BT4[:, hp, ch, :], start=True, stop=True), [bs['BT'], bs['KRZ']], [pn])
                P.op('dve', lambda e, pn=pn, half=half: e.tensor_tensor(out=X[0].ap[:, half * 512:(half + 1) * 512], in0=pn.ap, in1=self.maskN[d].ap, op=ALU.mult), [pn, self.maskN[d]], [X[0]])
            if RK_STOP == 2:
                return
            P.op('pool', lambda e: e.tensor_tensor(out=pt3[0], in0=G3[:, :, 0:128], in1=bc(self.identb.ap, [128, 8, 128], 1), op=ALU.add), [Gt, self.identb], [PT[0]])
            cur = 0
            for rnd in range(6):
                nxt = 1 - cur
                xt_prev = (lambda h: G3[:, h, 0:128]) if rnd == 0 else (lambda h, cur=cur: xt3[cur][:, h, :])
                xt_buf = Gt if rnd == 0 else XT[cur]
                for half in range(2):
                    px = P.ps()
                    for hh in range(4):
                        h = half * 4 + hh
                        P.op('pe', lambda e, px=px, h=h, hh=hh, xt_prev=xt_prev, cur=cur: e.matmul(px.ap[:, hh * 128:(hh + 1) * 128], xt_prev(h), x3[cur][:, h, :], start=True, stop=True), [xt_buf, X[cur]], [px])
                    P.op('act', lambda e, px=px, half=half, nxt=nxt: e.copy(out=X[nxt].ap[:, half * 512:(half + 1) * 512], in_=px.ap), [px], [X[nxt]])
                    if rnd < 5:
                        pxt = P.ps()
                        for hh in range(4):
                            h = half * 4 + hh
                            P.op('pe', lambda e, pxt=pxt, h=h, hh=hh, xt_prev=xt_prev, cur=cur: e.matmul(pxt.ap[:, hh * 128:(hh + 1) * 128], x3[cur][:, h, :], xt_prev(h), start=True, stop=True), [xt_buf, X[cur]], [pxt])
                        P.op('act', lambda e, pxt=pxt, half=half, nxt=nxt: e.copy(out=XT[nxt].ap[:, half * 512:(half + 1) * 512], in_=pxt.ap), [pxt], [XT[nxt]])
                    pp = P.ps()
                    for hh in range(4):
                        h = half * 4 + hh
                        P.op('pe', lambda e, pp=pp, h=h, hh=hh, nxt=nxt, cur=cur: e.matmul(pp.ap[:, hh * 128:(hh + 1) * 128], x3[nxt][:, h, :], pt3[cur][:, h, :], start=True, stop=True), [X[nxt], PT[cur]], [pp])
                    P.op('dve', lambda e, pp=pp, half=half, nxt=nxt, cur=cur: e.tensor_tensor(out=PT[nxt].ap[:, half * 512:(half + 1) * 512], in0=pp.ap, in1=PT[cur].ap[:, half * 512:(half + 1) * 512], op=ALU.add), [pp, PT[cur]], [PT[nxt]])
                cur = nxt
            PTf = pt3[cur]; PTbuf = PT[cur]
            if RK_STOP == 3:
                return
            btp = BTP[ci % 2]; ktp = KTP[ci % 2]
            ptp = P.ps()
            ptb = ptp.ap.bitcast(BF16)
            for hp in range(4):
                P.op('pe', lambda e, hp=hp: e.transpose(ptb[:, hp * 128:(hp + 1) * 128], BT4[:, hp, ch, :], self.identb.ap), [bs['BT'], self.identb], [ptp])
            for hp in range(4):
                P.op('pe', lambda e, hp=hp: e.transpose(ptb[:, 512 + hp * 128:512 + (hp + 1) * 128], KT4[:, hp, ch, :], self.identb.ap), [bs['KT'], self.identb], [ptp])
            for wh, dst in ((0, btp), (1, ktp)):
                src4 = ptb[:, wh * 512:(wh + 1) * 512].rearrange("p (h x k) -> p h x k", h=4, x=2)
                dst4 = dst.ap.rearrange("p (h x c) -> p h x c", h=4, x=2)
                for par in range(2):
                    P.op('act', lambda e, src4=src4, dst4=dst4, par=par: e.copy(out=dst4[:, :, par, par * 64:(par + 1) * 64], in_=src4[:, :, par, :]), [ptp], [dst])
            btp3 = v3(btp.ap, 8); ktp3 = v3(ktp.ap, 8)
            if RK_STOP == 4:
                return
            Ap3 = v3(Ap.ap, 4); A3 = v3(A.ap, 4)
            P.op('dve', lambda e: e.tensor_tensor(out=Ap3, in0=A3, in1=bc(ecm3[:, :, ch], [128, 4, 64], 2), op=ALU.mult), [A, bs['ecm']], [Ap])
            pr = P.ps()
            for h in range(8):
                hp = h // 2
                P.op('pe', lambda e, h=h, hp=hp: e.matmul(pr.ap[:, h * 64:(h + 1) * 64], KRZ6[:, hp, ch, 0, h % 2, :], Ap3[:, hp, :], start=True, stop=False), [bs['KRZ'], Ap], [pr])
                P.op('pe', lambda e, h=h: e.matmul(pr.ap[:, h * 64:(h + 1) * 64], G3[:, h, 256:384], vm.ap[:, h * 64:(h + 1) * 64], start=False, stop=True), [Gt, vm], [pr])
            P.op('act', lambda e: e.copy(out=rhs_t.ap, in_=pr.ap), [pr], [rhs_t])
            pu = P.ps()
            for h in range(8):
                P.op('pe', lambda e, h=h: e.matmul(pu.ap[:, h * 64:(h + 1) * 64], PTf[:, h, :], rhs_t.ap[:, h * 64:(h + 1) * 64], start=True, stop=True), [PTbuf, rhs_t], [pu])
            P.op('dve', lambda e: e.tensor_copy(out=u_t.ap, in_=pu.ap), [pu], [u_t])
            pst = P.ps()
            for hp in range(4):
                for par in range(2):
                    h = hp * 2 + par
                    P.op('pe', lambda e, h=h, hp=hp, par=par: e.matmul(pst.ap[:, hp * 64:(hp + 1) * 64], btp3[:, h, :], u_t.ap[:, h * 64:(h + 1) * 64], start=(par == 0), stop=False), [btp, u_t], [pst])
                    P.op('pe', lambda e, h=h, hp=hp, par=par: e.matmul(pst.ap[:, hp * 64:(hp + 1) * 64], ktp3[:, h, :], vm.ap[:, h * 64:(h + 1) * 64], start=False, stop=(par == 1)), [ktp, vm], [pst])
            if RK_STOP == 5:
                return
            po = P.ps()
            for h in range(8):
                hp = h // 2
                P.op('pe', lambda e, h=h, hp=hp: e.matmul(po.ap[:, h * 64:(h + 1) * 64], KRZ6[:, hp, ch, 1, h % 2, :], Ap3[:, hp, :], start=True, stop=False), [bs['KRZ'], Ap], [po])
                P.op('pe', lambda e, h=h: e.matmul(po.ap[:, h * 64:(h + 1) * 64], G3[:, h, 128:256], u_t.ap[:, h * 64:(h + 1) * 64], start=False, stop=False), [Gt, u_t], [po])
                P.op('pe', lambda e, h=h: e.matmul(po.ap[:, h * 64:(h + 1) * 64], G3[:, h, 384:512], vm.ap[:, h * 64:(h + 1) * 64], start=False, stop=True), [Gt, vm], [po])
            pbd = P.ps()
            for hp in range(4):
                P.op('pe', lambda e, hp=hp: e.matmul(pbd.ap[:, hp * 2:(hp + 1) * 2], PR3[:, hp, ch * 128:(ch + 1) * 128], self.sel.ap, start=True, stop=True), [bs['PR'], self.sel], [pbd])
            P.op('dve', lambda e: e.tensor_tensor(out=v3(stmp.ap, 4), in0=v3(pst.ap[:, 0:256], 4), in1=bc(efin3[:, :, ch], [128, 4, 64], 2), op=ALU.mult), [pst, bs['efin']], [stmp])
            P.op('dve', lambda e: e.tensor_tensor(out=A3, in0=A3, in1=bc(wtot3[:, :, ch], [128, 4, 64], 2), op=ALU.mult), [A, bs['wtot']], [A])
            P.op('dve', lambda e: e.tensor_tensor(out=A.ap, in0=A.ap, in1=stmp.ap, op=ALU.add), [A, stmp], [A])
            oa = oacc[ci % 2]
            P.op('dve', lambda e: e.tensor_tensor(out=v3(bon.ap, 8), in0=v3(vm.ap, 8), in1=bc(pbd.ap[:, 0:8], [128, 8, 64], 2), op=ALU.mult), [vm, pbd], [bon])
            P.op('dve', lambda e: e.tensor_tensor(out=oa.ap, in0=po.ap, in1=bon.ap, op=ALU.add), [po, bon], [oa])
            if d == 0:
                P.dma('sp', dr['OS'][t0:t0 + 128, :], oa.ap, reads=[oa], writes=[sb['OS']])
                return
            P.op('pool', lambda e: e.tensor_tensor(out=oa.ap, in0=oa.ap, in1=op_.ap, op=ALU.add), [oa, op_], [oa])
            oa3 = v3(oa.ap, 8)
            P.op('dve', lambda e: e.tensor_reduce(out=gn['mean'].ap, in_=oa3, axis=AX.X, op=ALU.add), [oa], [gn['mean']])
            P.op('dve', lambda e: e.tensor_scalar(out=gn['mean'].ap, in0=gn['mean'].ap, scalar1=1.0 / 64, scalar2=None, op0=ALU.mult), [gn['mean']], [gn['mean']])
            P.op('dve', lambda e: e.tensor_tensor(out=v3(gn['cen'].ap, 8), in0=oa3, in1=bc(gn['mean'].ap, [128, 8, 64], 2), op=ALU.subtract), [oa, gn['mean']], [gn['cen']])
            P.op('act', lambda e: e.activation(out=gn['sq'].ap, in_=gn['cen'].ap, func=AF.Square), [gn['cen']], [gn['sq']])
            P.op('dve', lambda e: e.tensor_reduce(out=gn['var'].ap, in_=v3(gn['sq'].ap, 8), axis=AX.X, op=ALU.add), [gn['sq']], [gn['var']])
            P.op('act', lambda e: e.activation(out=gn['var'].ap, in_=gn['var'].ap, func=AF.Sqrt, bias=self.lneps_ap, scale=1.0 / 64), [gn['var'], self.lneps_t], [gn['var']])
            P.op('dve', lambda e: e.reciprocal(out=gn['var'].ap, in_=gn['var'].ap), [gn['var']], [gn['var']])
            P.op('dve', lambda e: e.tensor_tensor(out=v3(gn['cen'].ap, 8), in0=v3(gn['cen'].ap, 8), in1=bc(gn['var'].ap, [128, 8, 64], 2), op=ALU.mult), [gn['cen'], gn['var']], [gn['cen']])
            P.op('pool', lambda e: e.tensor_tensor(out=gn['cen'].ap, in0=gn['cen'].ap, in1=lng.ap, op=ALU.mult), [gn['cen'], lng], [gn['cen']])
            P.op('pool', lambda e: e.tensor_tensor(out=gn['cen'].ap, in0=gn['cen'].ap, in1=lnb.ap, op=ALU.add), [gn['cen'], lnb], [gn['cen']])
            pgt = P.ps()
            P.op('pe', lambda e: e.matmul(pgt.ap, bs['sgd'].ap[:, ch * 128:(ch + 1) * 128], gup.ap, start=True, stop=True), [bs['sgd'], gup], [pgt])
            P.op('dve', lambda e: e.tensor_tensor(out=yb.ap, in0=pgt.ap, in1=gn['cen'].ap, op=ALU.mult), [pgt, gn['cen']], [yb])
            pyt = P.ps()
            pytb = pyt.ap.bitcast(BF16)
            for hp in range(4):
                P.op('pe', lambda e, hp=hp: e.transpose(pytb[:, hp * 128:(hp + 1) * 128], yb.ap[:, hp * 128:(hp + 1) * 128], self.identb.ap), [yb, self.identb], [pyt])
            ys = yst[ci % 2]
            P.op('act', lambda e: e.copy(out=ys.ap, in_=pytb[:, 0:512]), [pyt], [ys])
            P.dma('sp', dr['YR'][:, t0:t0 + 128].rearrange("(j p) t -> p j t", p=128), v3(ys.ap, 4), reads=[ys], writes=[sb['YR']])

        for bi, (c0, n) in enumerate(blocks):
            prep_block(bi, c0, n)
            chs = list(range(n // 128))
            if d == 1:
                chs = chs[::-1]
            for ch in chs:
                do_chunk(bi, c0, ch)
        P.barrier()
        P.off = mark

    def merge(self, l, mod):
        P, dr, sb = self.P, self.dr, self.sb
        mark = P.off
        n = CG
        wg = P.tile(8 * 3072, BF16, 'm_wg'); wg3 = v3(wg.ap, 8)
        for k in range(8):
            P.dma('pool', wg3[:, k, :], dr['w_in'][l, k * 128:(k + 1) * 128, 3712:6784], writes=[wg])
        wp = P.tile(12 * D, BF16, 'm_wp'); wp3 = v3(wp.ap, 12)
        for b in range(3):
            for kk in range(4):
                P.dma('pool', wp3[:, b * 4 + kk, :], dr['branch_proj'][l, b, kk * 128:(kk + 1) * 128, :], writes=[wp])
        wo = P.tile(8 * D, BF16, 'm_wo'); wo3 = v3(wo.ap, 8)
        for k in range(8):
            P.dma('pool', wo3[:, k, :], dr['w_out'][l, k * 128:(k + 1) * 128, :], writes=[wo])
        xts = [P.tile(8 * n, F32, 'xt%d' % b) for b in range(2)]
        xm = P.tile(8 * n, BF16, 'xm'); xn = P.tile(8 * n, F32, 'xn'); sq = P.tile(8 * n, BF16, 'sq'); rs = P.tile(n, F32, 'rs')
        xm3 = v3(xm.ap, 8)
        ys = [P.tile(12 * n, BF16, 'm_y%d' % b) for b in range(2)]
        sgs = [P.tile(n, F32, 'm_sg%d' % b) for b in range(2)]
        tts = [P.tile(n, F32, 'm_t%d' % b) for b in range(2)]
        accf = P.tile(n, F32, 'm_accf')
        accb = P.tile(8 * n, BF16, 'm_accb'); accb3 = v3(accb.ap, 8)
        chunks = self.chunks
        m1 = lambda v: mod[(1, v)]

        def load_y(ci):
            c0 = chunks[ci][0]
            y = ys[ci % 2]; y3 = v3(y.ap, 12)
            for b, nm in enumerate(('YA', 'YL', 'YR')):
                P.dma('act', y3[:, b * 4:(b + 1) * 4, :], dr[nm][:, c0:c0 + n].rearrange("(j p) t -> p j t", p=128), reads=[sb[nm]], writes=[y])
        self.load_norm(chunks[0][0], n, chunks[0][2], 1, mod, xts[0], xm, sq, rs, xn)
        load_y(0)
        cnt = [0]
        for ci, (c0, _, v) in enumerate(chunks):
            xt = xts[ci % 2]; xt3 = v3(xt.ap, 8)
            y = ys[ci % 2]; y3 = v3(y.ap, 12)
            m = m1(v)
            for dt in range(8):
                for b in range(3):
                    pgt = P.ps(); ppj = P.ps()
                    for k in range(8):
                        P.op('pe', lambda e, k=k, b=b, dt=dt, pgt=pgt: e.matmul(pgt.ap[:, 0:n], wg3[:, k, b * D + dt * 128:b * D + (dt + 1) * 128], xm3[:, k, :], start=(k == 0), stop=(k == 7)), [wg, xm], [pgt])
                    for kk in range(4):
                        P.op('pe', lambda e, kk=kk, b=b, dt=dt, ppj=ppj, y3=y3: e.matmul(ppj.ap[:, 0:n], wp3[:, b * 4 + kk, dt * 128:(dt + 1) * 128], y3[:, b * 4 + kk, :], start=(kk == 0), stop=(kk == 3)), [wp, y], [ppj])
                    sg_ = sgs[cnt[0] % 2]; tt = tts[cnt[0] % 2]; cnt[0] += 1
                    P.op('act', lambda e, pgt=pgt, sg_=sg_: e.activation(out=sg_.ap, in_=pgt.ap[:, 0:n], func=AF.Sigmoid), [pgt], [sg_])
                    if b == 0:
                        P.op('dve', lambda e, ppj=ppj, sg_=sg_: e.tensor_tensor(out=accf.ap, in0=ppj.ap[:, 0:n], in1=sg_.ap, op=ALU.mult), [ppj, sg_], [accf])
                    else:
                        P.op('dve', lambda e, ppj=ppj, sg_=sg_, tt=tt: e.tensor_tensor(out=tt.ap, in0=ppj.ap[:, 0:n], in1=sg_.ap, op=ALU.mult), [ppj, sg_], [tt])
                        if b == 1:
                            P.op('pool', lambda e, tt=tt: e.tensor_tensor(out=accf.ap, in0=accf.ap, in1=tt.ap, op=ALU.add), [accf, tt], [accf])
                        else:
                            P.op('pool', lambda e, tt=tt, dt=dt: e.tensor_tensor(out=accb3[:, dt, :], in0=accf.ap, in1=tt.ap, op=ALU.add), [accf, tt], [accb])
            if ci + 1 < len(chunks):
                self.load_norm(chunks[ci + 1][0], n, chunks[ci + 1][2], 1, mod, xts[(ci + 1) % 2], xm, sq, rs, xn)
                load_y(ci + 1)
            for d2 in range(8):
                po = P.ps()
                for dt in range(8):
                    P.op('pe', lambda e, dt=dt, d2=d2, po=po: e.matmul(po.ap[:, 0:n], wo3[:, dt, d2 * 128:(d2 + 1) * 128], accb3[:, dt, :], start=(dt == 0), stop=(dt == 7)), [wo, accb], [po])
                P.op('dve', lambda e, d2=d2, po=po, xt3=xt3, m=m: e.scalar_tensor_tensor(out=xt3[:, d2, :], in0=po.ap[:, 0:n], scalar=m['GT'][:, d2:d2 + 1], in1=xt3[:, d2, :], op0=ALU.mult, op1=ALU.add), [po, xt] + m['bufs'], [xt])
            P.dma('sp', dr['XT'][:, c0:c0 + n].rearrange("(k p) t -> p k t", p=128), xt3, reads=[xt], writes=[self.chunk_buf(c0)])
        P.barrier()
        P.off = mark

    def layer(self, l):
        P = self.P
        mark = P.off
        mod = self.adaln(l)
        if self.stop is not None and len(self.stop) > 2 and self.stop[2] == 'skip':
            self.rwkv_prep(l)
            self.rwkv_dir(l, 0)
            if self.stop[0] != 'r0':
                self.rwkv_dir(l, 1)
                self.merge(l, mod)
            self.dump_scratch(('OS',))
            P.off = mark
            return
        self.ffn(l, 0, mod)
        if self.stop == ('ffn1', l):
            P.off = mark
            return
        self.mix_in(l, mod)
        self.attn(l)
        self.lru(l)
        self.rwkv_prep(l)
        if self.stop == ('rp', l):
            self.dump_scratch(('RS', 'KK', 'VT', 'TW', 'SG', 'KS', 'AD'))
            P.off = mark
            return
        self.rwkv_dir(l, 0)
        if self.stop == ('r0', l):
            self.dump_scratch(('OS',))
            P.off = mark
            return
        self.rwkv_dir(l, 1)
        if self.stop == ('r1', l):
            self.dump_scratch(('YR', 'OS'))
            P.off = mark
            return
        self.merge(l, mod)
        if self.stop == ('mix1', l):
            self.dump_scratch()
            P.off = mark
            return
        self.ffn(l, 1, mod)
        P.barrier()
        P.off = mark


_NC_CACHE = {}


def get_nc(T, depth, stop=None, debug=False):
    key = (T, depth, stop, debug)
    if key not in _NC_CACHE:
        b = Builder(T, depth, stop)
        b.debug = debug
        _NC_CACHE[key] = b.build()
    return _NC_CACHE[key]


WNAMES = ('ada_w', 'ada_b', 'norm_g', 'ffn_w_gu', 'ffn_w_d', 'w_in', 'attn_q_gain', 'attn_k_gain', 'attn_sink',
          'lru_conv_w', 'lru_conv_b', 'lru_gate_w', 'lru_gate_b', 'lru_lambda', 'branch_proj', 'w_out',
          'rwkv_mu', 'rwkv_w_up', 'rwkv_w0', 'rwkv_a_up', 'rwkv_a0', 'rwkv_g_up', 'rwkv_k_k', 'rwkv_k_a', 'rwkv_r_k', 'rwkv_ln_g', 'rwkv_ln_b')


def host_consts(T):
    f32 = np.float32
    c = {'ident': np.eye(128, dtype=f32)}
    p = np.arange(128)
    dh = p % 64
    axis = dh // 32
    f = dh % 16
    half = (dh % 32) // 16
    t = np.arange(T)
    pos = np.where(axis[:, None] == 0, (t // 64)[None, :], (t % 64)[None, :]).astype(np.float64)
    inv = 10000.0 ** (-(f.astype(np.float64)) / 16.0)
    ang = pos * inv[:, None]
    c['rope_cos'] = np.cos(ang).astype(f32)
    c['rope_sin'] = (np.sin(ang) * np.where(half == 0, -1.0, 1.0)[:, None]).astype(f32)
    partner = np.where(half == 0, p + 16, p - 16)
    rotm = np.zeros((128, 128), f32)
    rotm[partner, p] = 1.0
    c['rotm'] = rotm
    c['blk64'] = (p[:, None] // 64 == p[None, :] // 64).astype(f32)
    a = np.arange(128)
    c['mlo'] = np.tile((a[None, :] <= a[:, None]).astype(f32), (1, 4))
    c['mhi'] = np.tile((a[:, None] <= a[None, :]).astype(f32), (1, 4))
    sel = np.zeros((128, 2), f32)
    sel[:64, 0] = 1.0
    sel[64:, 1] = 1.0
    c['sel'] = sel
    for d_ in range(2):
        if d_ == 0:
            strict = (a[:, None] < a[None, :]).astype(f32)
        else:
            strict = (a[:, None] > a[None, :]).astype(f32)
        incl = strict + np.eye(128, dtype=f32)
        c['mask4_%d' % d_] = np.concatenate([strict, incl, strict, incl], axis=1)
        c['maskN_%d' % d_] = np.tile(strict.T, (1, 4))
        col = np.arange(512) % 128
        rst = np.ones((128, 512), f32)
        rst[:, col == (0 if d_ == 0 else 127)] = 0.0
        c['rst_%d' % d_] = rst
    return c


def run(inputs, T, depth, stop=None, ncores=8, debug=False):
    nc = get_nc(T, depth, stop, debug)
    f32 = np.float32
    shared = {k: np.ascontiguousarray(np.asarray(inputs[k], dtype=f32)) for k in WNAMES}
    shared.update(host_consts(T))
    in_maps = []
    for b in range(ncores):
        m = dict(shared)
        m['x'] = np.ascontiguousarray(inputs['x'][b, :T].astype(f32))
        m['ctx'] = np.ascontiguousarray(inputs['ctx'][b].astype(f32))
        m['cvec'] = np.ascontiguousarray(np.stack([inputs['c'][b], inputs['c_ctx']]).astype(f32))
        in_maps.append(m)
    res = run_bass_kernel_spmd(nc, in_maps, core_ids=list(range(ncores)))
    if debug:
        return res.results
    return np.stack([r['out'] for r in res.results])


def kernel(**inputs):
    T = inputs['x'].shape[1]
    full = run(inputs, T, 4)
    return np.ascontiguousarray(full[:, CTX:, :]).astype(np.float32)
```

```python
import numpy as np
import ml_dtypes
from contextlib import ExitStack
import concourse.bass as bass
import concourse.mybir as mybir
from concourse.bass_utils import run_bass_kernel_spmd

F32 = mybir.dt.float32
BF16 = mybir.dt.bfloat16
AF = mybir.ActivationFunctionType
ALU = mybir.AluOpType
AX = mybir.AxisListType

D = 1024
CTX = 256
DFF = 2816
NADA = 9
EPS = 1e-6
ENGS = ('pe', 'act', 'dve', 'pool', 'sp')
ND = 12
SB_WORDS = 52736
RK_STOP = 99
CG = 256


class Buf:
    __slots__ = ('w', 'r', 'name')

    def __init__(self, name=''):
        self.w = {}
        self.r = {}
        self.name = name


class Tile:
    __slots__ = ('ap', 'buf')

    def __init__(self, ap, name=''):
        self.ap = ap
        self.buf = Buf(name)


class Prog:
    def __init__(self, nc, es):
        self.nc = nc
        self.sem = {e: es.enter_context(nc.semaphore('s_' + e)) for e in ENGS}
        self.dsem = {q: [es.enter_context(nc.semaphore('d_%s%d' % (q, i))) for i in range(ND)]
                     for q in ('sp', 'pool', 'act')}
        self.cnt = {e: 0 for e in ENGS}
        self.dcnt = {q: 0 for q in self.dsem}
        self.seen = {e: {} for e in ENGS}
        self.q = {e: [] for e in ENGS}
        self.big = es.enter_context(nc.sbuf_tensor('big', [128, SB_WORDS], F32))
        self.off = 0
        self.psum = []
        for i in range(8):
            t = es.enter_context(nc.psum_tensor('ps%d' % i, [128, 512], F32))
            self.psum.append(Tile(t[:, :], 'ps%d' % i))
        self.psi = 0
        self.ninstr = 0

    def tile(self, nfree, dtype=F32, name=''):
        nw = nfree if dtype == F32 else (nfree + 1) // 2
        nw = (nw + 7) // 8 * 8
        assert self.off + nw <= SB_WORDS, ('SBUF overflow', name, self.off, nw)
        ap = self.big[:, self.off:self.off + nw]
        self.off += nw
        if dtype != F32:
            ap = ap.bitcast(dtype)
        ap = ap[:, 0:nfree]
        return Tile(ap, name)

    def ps(self):
        t = self.psum[self.psi]
        self.psi = (self.psi + 1) % 8
        return t

    def handle(self, k):
        if isinstance(k, tuple):
            return self.dsem[k[1]][k[2]]
        return self.sem[k]

    def _deps(self, eng, reads, writes):
        need = {}
        for b in reads:
            for k, v in b.w.items():
                if need.get(k, 0) < v:
                    need[k] = v
        for b in writes:
            for k, v in b.w.items():
                if need.get(k, 0) < v:
                    need[k] = v
            for k, v in b.r.items():
                if need.get(k, 0) < v:
                    need[k] = v
        waits = []
        seen = self.seen[eng]
        for k, v in need.items():
            if k == 'pe' and eng == 'pe':
                continue
            if seen.get(k, 0) >= v:
                continue
            seen[k] = v
            waits.append((k, v))
        return waits

    @staticmethod
    def _bufs(xs):
        return [x.buf if isinstance(x, Tile) else x for x in xs]

    def op(self, eng, fn, reads=(), writes=()):
        reads = self._bufs(reads)
        writes = self._bufs(writes)
        waits = self._deps(eng, reads, writes)
        self.cnt[eng] += 1
        c = self.cnt[eng]
        for b in reads:
            if b.r.get(eng, 0) < c:
                b.r[eng] = c
        for b in writes:
            b.w[eng] = c
            b.r = {}
        self.q[eng].append((waits, fn, (eng, 1)))
        self.ninstr += 1 + len(waits)

    def dma(self, q, out, in_, reads=(), writes=(), **kw):
        reads = self._bufs(reads)
        writes = self._bufs(writes)
        waits = self._deps(q, reads, writes)
        i = self.dcnt[q]
        self.dcnt[q] += 1
        slot = i % ND
        val = 16 * (i // ND + 1)
        key = ('d', q, slot)
        if i >= ND and self.seen[q].get(key, 0) < val - 16:
            self.seen[q][key] = val - 16
            waits.append((key, val - 16))
        for b in reads:
            if b.r.get(key, 0) < val:
                b.r[key] = val
        for b in writes:
            b.w[key] = val
            b.r = {}
        self.q[q].append((waits, lambda e: e.dma_start(out=out, in_=in_, **kw), (key, 16)))
        self.ninstr += 1 + len(waits)

    def barrier(self):
        toks = [(e, self.cnt[e]) for e in ENGS if self.cnt[e] > 0]
        for q in self.dsem:
            n = self.dcnt[q]
            for slot in range(min(n, ND)):
                last = ((n - 1 - slot) // ND) + 1
                toks.append((('d', q, slot), 16 * last))
        for e in ENGS:
            waits = []
            for k, v in toks:
                if k == e and e == 'pe':
                    continue
                if self.seen[e].get(k, 0) >= v:
                    continue
                self.seen[e][k] = v
                waits.append((k, v))
            if waits:
                self.q[e].append((waits, None, None))
                self.ninstr += len(waits)

    def emit(self):
        self.barrier()
        with self.nc.Block() as block:
            def mk(e):
                def run(eng):
                    fuse = e in ('act', 'dve', 'pool')
                    for waits, fn, inc in self.q[e]:
                        if fn is None:
                            for k, v in waits:
                                eng.wait_ge(self.handle(k), v)
                            continue
                        is_dma = isinstance(inc[0], tuple)
                        if fuse and waits and not is_dma:
                            for k, v in waits[:-1]:
                                eng.wait_ge(self.handle(k), v)
                            k, v = waits[-1]
                            fn(eng)._wait_ge(self.handle(k), v).then_inc(self.handle(inc[0]), inc[1])
                        else:
                            for k, v in waits:
                                eng.wait_ge(self.handle(k), v)
                            fn(eng).then_inc(self.handle(inc[0]), inc[1])
                return run
            block.tensor(mk('pe'))
            block.scalar(mk('act'))
            block.vector(mk('dve'))
            block.gpsimd(mk('pool'))
            block.sync(mk('sp'))


def bc(ap, shape, axis):
    return ap.unsqueeze(axis).broadcast_to(list(shape))


def v3(ap, k):
    return ap.rearrange("p (k n) -> p k n", k=k)


class Builder:
    def __init__(self, T, depth, stop=None):
        self.T = T
        self.S = CTX + T
        self.depth = depth
        self.stop = stop
        self.debug = False
        self.dbg_names = []
        self.chunks = [(CG * i, CG, 1 if i == 0 else 0) for i in range(self.S // CG)]

    def build(self):
        nc = bass.Bass("TRN2", target_bir_lowering=False)
        self.nc = nc
        T, S = self.T, self.S
        L = 4
        dr = {}

        def inp(name, shape, dt=F32):
            dr[name] = nc.dram_tensor(name, list(shape), dt, kind="ExternalInput").ap()

        inp('x', [T, D]); inp('ctx', [CTX, D]); inp('cvec', [2, D])
        inp('ada_w', [L, D, NADA * D]); inp('ada_b', [L, NADA * D]); inp('norm_g', [L, 3, D])
        inp('ffn_w_gu', [L, 2, D, 2 * DFF]); inp('ffn_w_d', [L, 2, DFF, D])
        inp('ident', [128, 128])
        inp('w_in', [L, D, 6784]); inp('attn_q_gain', [L, 64]); inp('attn_k_gain', [L, 64]); inp('attn_sink', [L, 8])
        inp('lru_conv_w', [L, 4, 512]); inp('lru_conv_b', [L, 512]); inp('lru_gate_w', [L, 2, 2, 8, 64, 64])
        inp('lru_gate_b', [L, 2, 2, 512]); inp('lru_lambda', [L, 2, 512])
        inp('branch_proj', [L, 3, 512, D]); inp('w_out', [L, D, D])
        inp('rope_cos', [128, T]); inp('rope_sin', [128, T]); inp('rotm', [128, 128]); inp('blk64', [128, 128])
        inp('mlo', [128, 512]); inp('mhi', [128, 512])
        inp('rwkv_mu', [L, 2, 1920]); inp('rwkv_w_up', [L, 2, 64, 512]); inp('rwkv_w0', [L, 2, 512]); inp('rwkv_a_up', [L, 2, 64, 512])
        inp('rwkv_a0', [L, 2, 512]); inp('rwkv_g_up', [L, 128, 512]); inp('rwkv_k_k', [L, 512]); inp('rwkv_k_a', [L, 512])
        inp('rwkv_r_k', [L, 512]); inp('rwkv_ln_g', [L, 512]); inp('rwkv_ln_b', [L, 512])
        inp('sel', [128, 2])
        for d_ in range(2):
            inp('mask4_%d' % d_, [128, 512]); inp('maskN_%d' % d_, [128, 512]); inp('rst_%d' % d_, [128, 512])
        for nm, shp, dt in (('RS', [512, S], F32), ('KS', [512, S], F32), ('KK', [512, S], F32), ('VT', [S, 512], BF16),
                            ('TW', [128, S], BF16), ('AD', [128, S], BF16), ('SG', [128, S], BF16), ('OS', [S, 512], F32)):
            dr[nm] = nc.dram_tensor(nm, shp, dt, kind="Internal").ap()
        for nm, shp, dt in (('QT', [512, S], BF16), ('KT', [128, S], BF16), ('VA', [S, 128], BF16), ('ZL', [1024, S], F32),
                            ('ZR', [1920, S], F32), ('YA', [512, S], BF16), ('YL', [512, S], BF16), ('YR', [512, S], BF16)):
            dr[nm] = nc.dram_tensor(nm, shp, dt, kind="Internal").ap()
        self.sb = {nm: Buf(nm) for nm in ('QT', 'KT', 'VA', 'ZL', 'ZR', 'YA', 'YL', 'YR', 'RS', 'KS', 'KK', 'VT', 'TW', 'AD', 'SG', 'OS')}
        dr['out'] = nc.dram_tensor('out', [S, D], F32, kind="ExternalOutput").ap()
        dr['XT'] = nc.dram_tensor('XT', [D, S], F32, kind="Internal").ap()
        self.dr = dr
        with ExitStack() as es:
            P = Prog(nc, es)
            self.P = P
            self.xt_bufs = [Buf('xt%d' % i) for i in range(self.S // CG)]
            self.consts()
            self.phase_in()
            for l in range(self.depth):
                self.layer(l)
            self.phase_out()
            P.emit()
        return nc

    def consts(self):
        P, dr = self.P, self.dr
        self.ident = P.tile(128, F32, 'ident')
        P.dma('sp', self.ident.ap, dr['ident'][:, :], writes=[self.ident])
        self.identb = P.tile(128, BF16, 'identb')
        P.op('dve', lambda e: e.tensor_copy(out=self.identb.ap, in_=self.ident.ap), [self.ident], [self.identb])
        self.onesb = P.tile(128, BF16, 'onesb')
        P.op('dve', lambda e: e.memset(self.onesb.ap, 1.0), [], [self.onesb])
        st = P.tile(128, F32, 'cv_stage')
        P.dma('sp', st.ap[0:16, :], dr['cvec'].rearrange("v (k p) -> (v k) p", p=128), writes=[st])
        ps = P.ps()
        P.op('pe', lambda e: e.transpose(ps.ap[:, 0:16], st.ap[0:16, :], self.ident.ap[0:16, 0:16]), [st, self.ident], [ps])
        self.sc = P.tile(16, F32, 'silu_c')
        sc3 = self.sc.ap.rearrange("p (k v) -> p k v", v=2)
        P.op('act', lambda e: e.activation(out=sc3, in_=ps.ap[:, 0:16].rearrange("p (v k) -> p k v", v=2), func=AF.Silu), [ps], [self.sc])
        def cload(name, n, q='pool'):
            t = P.tile(n, BF16, name)
            P.dma(q, t.ap, dr[name][:, :], writes=[t])
            return t
        self.rotm = cload('rotm', 128)
        self.blk64 = cload('blk64', 128)
        self.mlo = cload('mlo', 512)
        self.mhi = cload('mhi', 512)
        self.mask4 = [cload('mask4_%d' % d_, 512) for d_ in range(2)]
        self.maskN = [cload('maskN_%d' % d_, 512) for d_ in range(2)]
        self.rst = []
        for d_ in range(2):
            t = P.tile(512, F32, 'rst%d' % d_)
            P.dma('sp', t.ap, dr['rst_%d' % d_][:, :], writes=[t])
            self.rst.append(t)
        self.sel = P.tile(2, F32, 'sel')
        P.dma('sp', self.sel.ap, dr['sel'][:, :], writes=[self.sel])
        self.eps_t = P.tile(1, F32, 'eps')
        self.eps_ap = self.eps_t.ap[:, 0:1]
        P.op('dve', lambda e: e.memset(self.eps_t.ap, EPS), [], [self.eps_t])
        self.lneps_t = P.tile(1, F32, 'lneps')
        self.lneps_ap = self.lneps_t.ap[:, 0:1]
        P.op('dve', lambda e: e.memset(self.lneps_t.ap, 64e-5), [], [self.lneps_t])
        self.one_t = P.tile(1, F32, 'one')
        self.one_ap = self.one_t.ap[:, 0:1]
        P.op('dve', lambda e: e.memset(self.one_t.ap, 1.0), [], [self.one_t])
        self.base_off = P.off

    def phase_in(self):
        P, dr = self.P, self.dr
        S = self.S
        stg = [P.tile(D, F32, 'in_stg%d' % i) for i in range(2)]
        outt = [P.tile(D, F32, 'in_out%d' % i) for i in range(2)]
        ntile = S // 128
        for it in range(ntile):
            t0 = it * 128
            a = stg[it % 2]
            o = outt[it % 2]
            src = dr['ctx'][t0:t0 + 128, :] if t0 < CTX else dr['x'][t0 - CTX:t0 - CTX + 128, :]
            P.dma('sp', a.ap, src, writes=[a])
            for hf in range(2):
                ps = P.ps()
                for j in range(4):
                    k = hf * 4 + j
                    P.op('pe', lambda e, ps=ps, j=j, k=k, a=a: e.transpose(ps.ap[:, j * 128:(j + 1) * 128], a.ap[:, k * 128:(k + 1) * 128], self.ident.ap), [a, self.ident], [ps])
                eng = 'act' if hf == 0 else 'dve'
                if eng == 'act':
                    P.op('act', lambda e, ps=ps, o=o, hf=hf: e.copy(out=o.ap[:, hf * 512:(hf + 1) * 512], in_=ps.ap), [ps], [o])
                else:
                    P.op('dve', lambda e, ps=ps, o=o, hf=hf: e.tensor_copy(out=o.ap[:, hf * 512:(hf + 1) * 512], in_=ps.ap), [ps], [o])
            cb = self.chunk_buf(t0)
            P.dma('sp', dr['XT'][:, t0:t0 + 128].rearrange("(k p) t -> p k t", p=128), v3(o.ap, 8), reads=[o], writes=[cb])
        P.barrier()
        P.off = self.base_off

    def dbg(self, name, tile, ap=None):
        if not self.debug:
            return
        ap = tile.ap if ap is None else ap
        shape = list(ap.shape)
        d = self.nc.dram_tensor('dbg_' + name, shape, ap.dtype, kind="ExternalOutput").ap()
        self.dbg_names.append('dbg_' + name)
        self.P.dma('sp', d, ap, reads=[tile])

    def dump_scratch(self, names=('YA', 'YL', 'YR')):
        P, dr, sb = self.P, self.dr, self.sb
        for nm in names:
            d = self.nc.dram_tensor('dbg_' + nm, list(dr[nm].shape), dr[nm].dtype, kind="ExternalOutput").ap()
            P.dma('sp', d, dr[nm], reads=[sb[nm]])

    def chunk_buf(self, t0):
        return self.xt_bufs[t0 // CG]

    def phase_out(self):
        P, dr = self.P, self.dr
        S = self.S
        stg = [P.tile(D, F32, 'o_stg%d' % i) for i in range(2)]
        outt = [P.tile(D, F32, 'o_out%d' % i) for i in range(2)]
        for it in range(S // 128):
            t0 = it * 128
            a = stg[it % 2]
            o = outt[it % 2]
            cb = self.chunk_buf(t0)
            P.dma('sp', v3(a.ap, 8), dr['XT'][:, t0:t0 + 128].rearrange("(k p) t -> p k t", p=128), reads=[cb], writes=[a])
            for hf in range(2):
                ps = P.ps()
                for j in range(4):
                    k = hf * 4 + j
                    P.op('pe', lambda e, ps=ps, j=j, k=k, a=a: e.transpose(ps.ap[:, j * 128:(j + 1) * 128], a.ap[:, k * 128:(k + 1) * 128], self.ident.ap), [a, self.ident], [ps])
                if hf == 0:
                    P.op('act', lambda e, ps=ps, o=o, hf=hf: e.copy(out=o.ap[:, hf * 512:(hf + 1) * 512], in_=ps.ap), [ps], [o])
                else:
                    P.op('dve', lambda e, ps=ps, o=o, hf=hf: e.tensor_copy(out=o.ap[:, hf * 512:(hf + 1) * 512], in_=ps.ap), [ps], [o])
            P.dma('sp', dr['out'][t0:t0 + 128, :], o.ap, reads=[o])
        P.barrier()
        P.off = self.base_off

    def adaln(self, l):
        P, dr = self.P, self.dr
        nt = NADA * 8
        vec = P.tile(96, F32, 'vecT')
        M = [P.tile(72, F32, 'M%d' % v) for v in range(2)]
        Gs = [P.tile(24, F32, 'G%d' % v) for v in range(2)]
        GTs = [P.tile(24, F32, 'GT%d' % v) for v in range(2)]
        mark = P.off
        CB = 1152
        wbuf = [P.tile(8 * CB, F32, 'adaw%d' % i) for i in range(2)]
        ps = P.ps()
        for cb in range(8):
            w = wbuf[cb % 2]
            w3 = v3(w.ap, 8)
            for k in range(8):
                P.dma('sp' if k % 2 == 0 else 'act', w3[:, k, :], dr['ada_w'][l, k * 128:(k + 1) * 128, cb * CB:(cb + 1) * CB], writes=[w])
            for f in range(9):
                ft = cb * 9 + f
                for k in range(8):
                    P.op('pe', lambda e, ps=ps, w3=w3, ft=ft, f=f, k=k: e.matmul(ps.ap[:, 2 * ft:2 * ft + 2], w3[:, k, f * 128:(f + 1) * 128], self.sc.ap[:, 2 * k:2 * k + 2], start=(k == 0), stop=(k == 7)), [w, self.sc], [ps])
        st = P.tile(128, F32, 'vec_stage')
        P.dma('sp', st.ap[0:72, :], dr['ada_b'][l].rearrange("(r p) -> r p", p=128), writes=[st])
        P.dma('sp', st.ap[72:96, :], dr['norm_g'][l].rearrange("i (k p) -> (i k) p", p=128), writes=[st])
        ps2 = P.ps()
        P.op('pe', lambda e: e.transpose(ps2.ap[:, 0:96], st.ap[0:96, :], self.ident.ap[0:96, 0:96]), [st, self.ident], [ps2])
        P.op('dve', lambda e: e.tensor_copy(out=vec.ap, in_=ps2.ap[:, 0:96]), [ps2], [vec])
        psv = ps.ap[:, 0:144].rearrange("p (t v) -> p t v", v=2)
        for v in range(2):
            P.op('dve', lambda e, v=v: e.tensor_tensor(out=M[v].ap, in0=psv[:, :, v], in1=vec.ap[:, 0:72], op=ALU.add), [ps, vec], [M[v]])
        P.barrier()
        P.off = mark
        mod = {}
        for v in range(2):
            G = Gs[v]
            GT = GTs[v]
            for i in range(3):
                P.op('dve', lambda e, v=v, i=i, G=G: e.scalar_tensor_tensor(out=G.ap[:, i * 8:(i + 1) * 8], in0=M[v].ap[:, (3 * i + 1) * 8:(3 * i + 2) * 8], scalar=1.0, in1=vec.ap[:, 72 + i * 8:72 + (i + 1) * 8], op0=ALU.add, op1=ALU.mult), [M[v], vec], [G])
                sc = 1.0 if i == 1 else 0.5
                P.op('dve', lambda e, v=v, i=i, GT=GT, sc=sc: e.tensor_scalar(out=GT.ap[:, i * 8:(i + 1) * 8], in0=M[v].ap[:, (3 * i + 2) * 8:(3 * i + 3) * 8], scalar1=sc, scalar2=None, op0=ALU.mult), [M[v]], [GT])
            for i in range(3):
                mod[(i, v)] = dict(G=G.ap[:, i * 8:(i + 1) * 8], SH=M[v].ap[:, 3 * i * 8:(3 * i + 1) * 8], GT=GT.ap[:, i * 8:(i + 1) * 8], bufs=[G, GT, M[v]])
        return mod

    def load_norm(self, c0, n, v, i, mod, xt, xm, sq, rs, xn):
        P, dr = self.P, self.dr
        m = mod[(i, v)]
        x3 = v3(xt.ap, 8)
        P.dma('sp', x3, dr['XT'][:, c0:c0 + n].rearrange("(k p) t -> p k t", p=128), reads=[self.chunk_buf(c0)], writes=[xt])
        P.op('act', lambda e: e.activation(out=sq.ap, in_=xt.ap, func=AF.Square), [xt], [sq])
        ps = P.ps()
        for k in range(8):
            P.op('pe', lambda e, k=k: e.matmul(ps.ap[:, 0:n], self.onesb.ap, v3(sq.ap, 8)[:, k, :], start=(k == 0), stop=(k == 7)), [sq, self.onesb], [ps])
        P.op('act', lambda e: e.activation(out=rs.ap, in_=ps.ap[:, 0:n], func=AF.Sqrt, bias=self.eps_ap, scale=1.0 / D), [ps, self.eps_t], [rs])
        P.op('dve', lambda e: e.reciprocal(out=rs.ap, in_=rs.ap), [rs], [rs])
        P.op('dve', lambda e: e.tensor_tensor(out=v3(xn.ap, 8), in0=x3, in1=bc(rs.ap, [128, 8, n], 1), op=ALU.mult), [xt, rs], [xn])
        for k in range(8):
            P.op('act', lambda e, k=k: e.activation(out=v3(xm.ap, 8)[:, k, :], in_=v3(xn.ap, 8)[:, k, :], func=AF.Identity, bias=m['SH'][:, k:k + 1], scale=m['G'][:, k:k + 1]), [xn] + m['bufs'], [xm])

    def ffn(self, l, which, mod):
        P, dr = self.P, self.dr
        i = 0 if which == 0 else 2
        mark = P.off
        n = CG
        wgu = P.tile(8 * 2 * DFF, BF16, 'wgu')
        wd = P.tile(22 * D, BF16, 'wd')
        wgu3 = v3(wgu.ap, 8)
        wd3 = v3(wd.ap, 22)
        for k in range(8):
            P.dma('pool', wgu3[:, k, :], dr['ffn_w_gu'][l, which, k * 128:(k + 1) * 128, :], writes=[wgu])
        for j in range(22):
            P.dma('pool', wd3[:, j, :], dr['ffn_w_d'][l, which, j * 128:(j + 1) * 128, :], writes=[wd])
        xts = [P.tile(8 * n, F32, 'xt%d' % b) for b in range(2)]
        xm = P.tile(8 * n, BF16, 'xm')
        xn = P.tile(8 * n, F32, 'xn')
        sq = P.tile(8 * n, BF16, 'sq')
        rs = P.tile(n, F32, 'rs')
        h = P.tile(22 * n, BF16, 'h')
        sg = [P.tile(n, F32, 'sg%d' % b) for b in range(2)]
        xm3 = v3(xm.ap, 8)
        h3 = v3(h.ap, 22)
        chunks = self.chunks
        self.load_norm(chunks[0][0], n, chunks[0][2], i, mod, xts[0], xm, sq, rs, xn)
        if which == 0 and l == 0:
            self.dbg('xm', xm); self.dbg('rs', rs); self.dbg('xn', xn); self.dbg('xt', xts[0])
            mm = mod[(0, 1)]
            self.dbg('G', mm['bufs'][0]); self.dbg('GT', mm['bufs'][1]); self.dbg('M', mm['bufs'][2])
        for ci, (c0, _, v) in enumerate(chunks):
            xt = xts[ci % 2]
            m = mod[(i, v)]
            for j in range(22):
                pg = P.ps()
                pu = P.ps()
                for k in range(8):
                    P.op('pe', lambda e, k=k, j=j, pg=pg: e.matmul(pg.ap[:, 0:n], wgu3[:, k, j * 128:(j + 1) * 128], xm3[:, k, :], start=(k == 0), stop=(k == 7)), [wgu, xm], [pg])
                for k in range(8):
                    P.op('pe', lambda e, k=k, j=j, pu=pu: e.matmul(pu.ap[:, 0:n], wgu3[:, k, DFF + j * 128:DFF + (j + 1) * 128], xm3[:, k, :], start=(k == 0), stop=(k == 7)), [wgu, xm], [pu])
                s_ = sg[j % 2]
                P.op('act', lambda e, pg=pg, s_=s_: e.activation(out=s_.ap, in_=pg.ap[:, 0:n], func=AF.Silu), [pg], [s_])
                P.op('dve', lambda e, pu=pu, s_=s_, j=j: e.tensor_tensor(out=h3[:, j, :], in0=pu.ap[:, 0:n], in1=s_.ap, op=ALU.mult), [pu, s_], [h])
            if which == 0 and l == 0 and ci == 0:
                self.dbg('h', h)
            if ci + 1 < len(chunks):
                self.load_norm(chunks[ci + 1][0], n, chunks[ci + 1][2], i, mod, xts[(ci + 1) % 2], xm, sq, rs, xn)
            xt3 = v3(xt.ap, 8)
            for dt in range(8):
                po = P.ps()
                for j in range(22):
                    P.op('pe', lambda e, j=j, dt=dt, po=po: e.matmul(po.ap[:, 0:n], wd3[:, j, dt * 128:(dt + 1) * 128], h3[:, j, :], start=(j == 0), stop=(j == 21)), [wd, h], [po])
                P.op('dve', lambda e, dt=dt, po=po, xt3=xt3, m=m: e.scalar_tensor_tensor(out=xt3[:, dt, :], in0=po.ap[:, 0:n], scalar=m['GT'][:, dt:dt + 1], in1=xt3[:, dt, :], op0=ALU.mult, op1=ALU.add), [po, xt] + m['bufs'], [xt])
            P.dma('sp', dr['XT'][:, c0:c0 + n].rearrange("(k p) t -> p k t", p=128), xt3, reads=[xt], writes=[self.chunk_buf(c0)])
        P.barrier()
        P.off = mark

    def mix_in(self, l, mod):
        P, dr, sb = self.P, self.dr, self.sb
        mark = P.off
        n = CG
        NZ = 3712
        win = P.tile(8 * NZ, BF16, 'win')
        win3 = v3(win.ap, 8)
        for k in range(8):
            P.dma('pool', win3[:, k, :], dr['w_in'][l, k * 128:(k + 1) * 128, 0:NZ], writes=[win])
        gq = P.tile(1, F32, 'gq'); gk = P.tile(1, F32, 'gk')
        for hh in range(2):
            P.dma('sp', gq.ap[hh * 64:(hh + 1) * 64, :], dr['attn_q_gain'][l].rearrange("(p o) -> p o", o=1), writes=[gq])
            P.dma('sp', gk.ap[hh * 64:(hh + 1) * 64, :], dr['attn_k_gain'][l].rearrange("(p o) -> p o", o=1), writes=[gk])
        xts = [P.tile(8 * n, F32, 'xt%d' % b) for b in range(2)]
        xm = P.tile(8 * n, BF16, 'xm'); xn = P.tile(8 * n, F32, 'xn'); sq = P.tile(8 * n, BF16, 'sq'); rs = P.tile(n, F32, 'rs')
        xm3 = v3(xm.ap, 8)
        zst = [P.tile(23 * n, F32, 'zst%d' % b) for b in range(2)]
        qst = [P.tile(5 * n, BF16, 'qst%d' % b) for b in range(2)]
        vst = [P.tile(128, BF16, 'vst%d' % b) for b in range(2)]
        sqz = [P.tile(n, BF16, 'sqz%d' % b) for b in range(2)]
        rq = [P.tile(n, F32, 'rq%d' % b) for b in range(2)]
        qn = [P.tile(n, BF16, 'qn%d' % b) for b in range(2)]
        t1 = [P.tile(n, F32, 't1%d' % b) for b in range(2)]
        t2 = [P.tile(n, F32, 't2%d' % b) for b in range(2)]
        cs = [P.tile(n, F32, 'cos%d' % b) for b in range(2)]
        sn = [P.tile(n, F32, 'sin%d' % b) for b in range(2)]
        chunks = self.chunks
        self.load_norm(chunks[0][0], n, chunks[0][2], 1, mod, xts[0], xm, sq, rs, xn)
        for ci, (c0, _, v) in enumerate(chunks):
            zs = zst[ci % 2]; qs = qst[ci % 2]
            zs3 = v3(zs.ap, 23); qs3 = v3(qs.ap, 5)
            if not v:
                P.dma('sp', cs[ci % 2].ap, dr['rope_cos'][:, c0 - CTX:c0 - CTX + n], writes=[cs[ci % 2]])
                P.dma('sp', sn[ci % 2].ap, dr['rope_sin'][:, c0 - CTX:c0 - CTX + n], writes=[sn[ci % 2]])
            for j in range(29):
                if j == 5:
                    continue
                pz = P.ps()
                for k in range(8):
                    P.op('pe', lambda e, k=k, j=j, pz=pz: e.matmul(pz.ap[:, 0:n], win3[:, k, j * 128:(j + 1) * 128], xm3[:, k, :], start=(k == 0), stop=(k == 7)), [win, xm], [pz])
                if j < 5:
                    b = j % 2
                    g_ = gq if j < 4 else gk
                    P.op('act', lambda e, pz=pz, b=b: e.activation(out=sqz[b].ap, in_=pz.ap[:, 0:n], func=AF.Square), [pz], [sqz[b]])
                    p2 = P.ps()
                    P.op('pe', lambda e, p2=p2, b=b: e.matmul(p2.ap[:, 0:n], self.blk64.ap, sqz[b].ap, start=True, stop=True), [self.blk64, sqz[b]], [p2])
                    P.op('act', lambda e, p2=p2, b=b: e.activation(out=rq[b].ap, in_=p2.ap[:, 0:n], func=AF.Sqrt, bias=self.eps_ap, scale=1.0 / 64), [p2, self.eps_t], [rq[b]])
                    P.op('dve', lambda e, b=b: e.reciprocal(out=rq[b].ap, in_=rq[b].ap), [rq[b]], [rq[b]])
                    if v:
                        P.op('dve', lambda e, pz=pz, b=b, g_=g_, j=j, qs3=qs3: e.scalar_tensor_tensor(out=qs3[:, j, :], in0=pz.ap[:, 0:n], scalar=g_.ap[:, 0:1], in1=rq[b].ap, op0=ALU.mult, op1=ALU.mult), [pz, g_, rq[b]], [qs])
                    else:
                        P.op('dve', lambda e, pz=pz, b=b, g_=g_: e.scalar_tensor_tensor(out=qn[b].ap, in0=pz.ap[:, 0:n], scalar=g_.ap[:, 0:1], in1=rq[b].ap, op0=ALU.mult, op1=ALU.mult), [pz, g_, rq[b]], [qn[b]])
                        p3 = P.ps()
                        P.op('pe', lambda e, p3=p3, b=b: e.matmul(p3.ap[:, 0:n], self.rotm.ap, qn[b].ap, start=True, stop=True), [self.rotm, qn[b]], [p3])
                        P.op('dve', lambda e, b=b, ci=ci: e.tensor_tensor(out=t1[b].ap, in0=qn[b].ap, in1=cs[ci % 2].ap, op=ALU.mult), [qn[b], cs[ci % 2]], [t1[b]])
                        P.op('dve', lambda e, b=b, ci=ci, p3=p3: e.tensor_tensor(out=t2[b].ap, in0=p3.ap[:, 0:n], in1=sn[ci % 2].ap, op=ALU.mult), [p3, sn[ci % 2]], [t2[b]])
                        P.op('pool', lambda e, b=b, j=j, qs3=qs3: e.tensor_tensor(out=qs3[:, j, :], in0=t1[b].ap, in1=t2[b].ap, op=ALU.add), [t1[b], t2[b]], [qs])
                else:
                    jj = j - 6
                    if jj % 2 == 0:
                        P.op('act', lambda e, pz=pz, jj=jj, zs3=zs3: e.copy(out=zs3[:, jj, :], in_=pz.ap[:, 0:n]), [pz], [zs])
                    else:
                        P.op('dve', lambda e, pz=pz, jj=jj, zs3=zs3: e.tensor_copy(out=zs3[:, jj, :], in_=pz.ap[:, 0:n]), [pz], [zs])
            for tt in range(n // 128):
                pv = P.ps()
                for k in range(8):
                    P.op('pe', lambda e, k=k, tt=tt, pv=pv: e.matmul(pv.ap[:, 0:128], xm3[:, k, tt * 128:(tt + 1) * 128], win3[:, k, 640:768], start=(k == 0), stop=(k == 7)), [win, xm], [pv])
                vs = vst[tt % 2]
                P.op('act', lambda e, pv=pv, vs=vs: e.copy(out=vs.ap, in_=pv.ap[:, 0:128]), [pv], [vs])
                P.dma('sp', dr['VA'][c0 + tt * 128:c0 + (tt + 1) * 128, :], vs.ap, reads=[vs], writes=[sb['VA']])
            P.dma('sp', dr['QT'][:, c0:c0 + n].rearrange("(j p) t -> p j t", p=128), qs3[:, 0:4, :], reads=[qs], writes=[sb['QT']])
            P.dma('sp', dr['KT'][:, c0:c0 + n], qs3[:, 4, :], reads=[qs], writes=[sb['KT']])
            P.dma('act', dr['ZL'][:, c0:c0 + n].rearrange("(j p) t -> p j t", p=128), zs3[:, 0:8, :], reads=[zs], writes=[sb['ZL']])
            P.dma('act', dr['ZR'][:, c0:c0 + n].rearrange("(j p) t -> p j t", p=128), zs3[:, 8:23, :], reads=[zs], writes=[sb['ZR']])
            if ci + 1 < len(chunks):
                self.load_norm(chunks[ci + 1][0], n, chunks[ci + 1][2], 1, mod, xts[(ci + 1) % 2], xm, sq, rs, xn)
        P.barrier()
        P.off = mark

    def attn(self, l):
        P, dr, sb = self.P, self.dr, self.sb
        S, T = self.S, self.T
        mark = P.off
        kt = P.tile(2 * S, BF16, 'kt')
        kt3 = v3(kt.ap, 2)
        P.dma('sp', kt3[0:64, :, :], dr['KT'].rearrange("(g d) t -> d g t", d=64), reads=[sb['KT']], writes=[kt])
        va = P.tile(S, BF16, 'va')
        va3 = va.ap.rearrange("p (n c) -> p n c", c=128)
        P.dma('sp', va3, dr['VA'].rearrange("(n p) c -> p n c", p=128), reads=[sb['VA']], writes=[va])
        sk = P.tile(8, F32, 'sk')
        P.dma('sp', sk.ap[0:64, :], dr['attn_sink'][l].partition_broadcast(64), writes=[sk])
        P.op('act', lambda e: e.activation(out=sk.ap[0:64, :], in_=sk.ap[0:64, :], func=AF.Exp), [sk], [sk])
        se = [P.tile(512, F32, 'se%d' % g) for g in range(2)]
        for g in range(2):
            P.op('dve', lambda e, g=g: e.tensor_copy(out=v3(se[g].ap, 4)[0:64], in_=bc(sk.ap[0:64, 4 * g:4 * g + 4], [64, 4, 128], 2)), [sk], [se[g]])
        qts = [P.tile(8 * 128, BF16, 'qt%d' % b) for b in range(2)]
        pts = [P.tile(512, BF16, 'pt%d' % b) for b in range(6)]
        dens = [P.tile(512, F32, 'den%d' % b) for b in range(2)]
        yos = [P.tile(512, BF16, 'yo%d' % b) for b in range(2)]
        nb = T // 128
        pti = 0
        for qb in range(S // 128):
            t0 = qb * 128
            qt = qts[qb % 2]
            qt3 = v3(qt.ap, 8)
            P.dma('sp', qt3[0:64], dr['QT'].rearrange("(h d) t -> d h t", d=64)[:, :, t0:t0 + 128], reads=[sb['QT']], writes=[qt])
            keys = [(0, None), (1, None)]
            if t0 >= CTX:
                i = qb - 2
                if i > 0:
                    keys.append((qb - 1, self.mlo))
                keys.append((qb, None))
                if i < nb - 1:
                    keys.append((qb + 1, self.mhi))
            for g in range(2):
                ptl = []
                for (kb, msk) in keys:
                    ps = P.ps()
                    P.op('pe', lambda e, ps=ps, kb=kb, g=g, qt3=qt3: e.matmul(ps.ap, kt3[0:64, g, kb * 128:(kb + 1) * 128], qt3[0:64, 4 * g:4 * g + 4, :], start=True, stop=True), [kt, qt], [ps])
                    pt = pts[pti % 6]; pti += 1
                    P.op('act', lambda e, ps=ps, pt=pt: e.activation(out=pt.ap, in_=ps.ap, func=AF.Exp, scale=0.125), [ps], [pt])
                    if msk is not None:
                        P.op('pool', lambda e, pt=pt, msk=msk: e.tensor_tensor(out=pt.ap, in0=pt.ap, in1=msk.ap, op=ALU.mult), [pt, msk], [pt])
                    ptl.append((kb, pt))
                po = P.ps(); pd = P.ps()
                for idx, (kb, pt) in enumerate(ptl):
                    P.op('pe', lambda e, po=po, kb=kb, pt=pt, g=g, idx=idx, nk=len(ptl): e.matmul(po.ap[0:64, :], va3[:, kb, g * 64:(g + 1) * 64], pt.ap, start=(idx == 0), stop=(idx == nk - 1)), [va, pt], [po])
                for idx, (kb, pt) in enumerate(ptl):
                    P.op('pe', lambda e, pd=pd, pt=pt, idx=idx, nk=len(ptl): e.matmul(pd.ap[0:64, :], self.onesb.ap[:, 0:64], pt.ap, start=(idx == 0), stop=(idx == nk - 1)), [self.onesb, pt], [pd])
                den = dens[g]; yo = yos[g]
                P.op('dve', lambda e, pd=pd, den=den, g=g: e.tensor_tensor(out=den.ap[0:64], in0=pd.ap[0:64, :], in1=se[g].ap[0:64], op=ALU.add), [pd, se[g]], [den])
                P.op('dve', lambda e, den=den: e.reciprocal(out=den.ap[0:64], in_=den.ap[0:64]), [den], [den])
                P.op('dve', lambda e, po=po, den=den, yo=yo: e.tensor_tensor(out=yo.ap[0:64], in0=po.ap[0:64, :], in1=den.ap[0:64], op=ALU.mult), [po, den], [yo])
                P.dma('sp', dr['YA'].rearrange("(h d) t -> d h t", d=64)[:, 4 * g:4 * g + 4, t0:t0 + 128], v3(yo.ap, 4)[0:64], reads=[yo], writes=[sb['YA']])
        P.barrier()
        P.off = mark

    def lru(self, l):
        P, dr, sb = self.P, self.dr, self.sb
        S, T = self.S, self.T
        mark = P.off
        CH = 512
        st = P.tile(128, F32, 'lst')
        P.dma('sp', st.ap[0:16, :], dr['lru_conv_w'][l].rearrange("c (j p) -> (c j) p", p=128), writes=[st])
        P.dma('sp', st.ap[16:20, :], dr['lru_conv_b'][l].rearrange("(j p) -> j p", p=128), writes=[st])
        P.dma('sp', st.ap[20:36, :], dr['lru_gate_b'][l].rearrange("d g (j p) -> (d g j) p", p=128), writes=[st])
        P.dma('sp', st.ap[36:44, :], dr['lru_lambda'][l].rearrange("d (j p) -> (d j) p", p=128), writes=[st])
        ps = P.ps()
        P.op('pe', lambda e: e.transpose(ps.ap[:, 0:44], st.ap[0:44, :], self.ident.ap[0:44, 0:44]), [st, self.ident], [ps])
        lv = P.tile(44, F32, 'lv')
        P.op('dve', lambda e: e.tensor_copy(out=lv.ap, in_=ps.ap[:, 0:44]), [ps], [lv])
        cc = P.tile(8, F32, 'lcc'); cc2 = P.tile(8, F32, 'lcc2')
        P.op('act', lambda e: e.activation(out=cc.ap, in_=lv.ap[:, 36:44], func=AF.Exp, scale=-1.0), [lv], [cc])
        P.op('act', lambda e: e.activation(out=cc.ap, in_=cc.ap, func=AF.Ln, bias=self.one_ap, scale=1.0), [cc, self.one_t], [cc])
        P.op('dve', lambda e: e.tensor_scalar(out=cc2.ap, in0=cc.ap, scalar1=-16.0, scalar2=None, op0=ALU.mult), [cc], [cc2])
        P.op('dve', lambda e: e.tensor_scalar(out=cc.ap, in0=cc.ap, scalar1=-8.0, scalar2=None, op0=ALU.mult), [cc], [cc])
        gw = P.tile(16 * 128, BF16, 'lgw')
        gw3 = v3(gw.ap, 16)
        P.op('dve', lambda e: e.memset(gw.ap, 0.0), [], [gw])
        for j in range(4):
            for d in range(2):
                for gt in range(2):
                    idx = (j * 2 + d) * 2 + gt
                    for hb in range(2):
                        P.dma('pool', gw3[hb * 64:(hb + 1) * 64, idx, hb * 64:(hb + 1) * 64], dr['lru_gate_w'][l, d, gt, 2 * j + hb], writes=[gw])
        ux = P.tile(S, F32, 'ux')
        ug = P.tile(S, F32, 'ug')
        xl = P.tile(S, F32, 'xl')
        xlb = P.tile(S, BF16, 'xlb')
        NB = 2
        rr = [P.tile(CH, F32, 'lr%d' % b) for b in range(NB)]
        ii = [P.tile(CH, F32, 'li%d' % b) for b in range(NB)]
        aa = [P.tile(CH, F32, 'la%d' % b) for b in range(NB)]
        a2 = [P.tile(CH, F32, 'la2%d' % b) for b in range(NB)]
        hh = [P.tile(CH, F32, 'lh%d' % b) for b in range(NB)]
        yb = P.tile(S, BF16, 'lyb')
        segs = [(0, CTX), (CTX, S)]
        itc = [0]

        def do_tile(j):
            P.dma('sp', ux.ap, dr['ZL'][j * 128:(j + 1) * 128, :], reads=[sb['ZL']], writes=[ux])
            P.dma('act', ug.ap, dr['ZL'][512 + j * 128:512 + (j + 1) * 128, :], reads=[sb['ZL']], writes=[ug])
            w = lambda c, j=j: lv.ap[:, c * 4 + j:c * 4 + j + 1]
            for (a, b) in segs:
                P.op('dve', lambda e, a=a, b=b: e.tensor_scalar(out=xl.ap[:, a:b], in0=ux.ap[:, a:b], scalar1=w(2), scalar2=lv.ap[:, 16 + j:17 + j], op0=ALU.mult, op1=ALU.add), [ux, lv], [xl])
                P.op('dve', lambda e, a=a, b=b: e.scalar_tensor_tensor(out=xl.ap[:, a + 1:b], in0=ux.ap[:, a:b - 1], scalar=w(1), in1=xl.ap[:, a + 1:b], op0=ALU.mult, op1=ALU.add), [ux, lv, xl], [xl])
                P.op('dve', lambda e, a=a, b=b: e.scalar_tensor_tensor(out=xl.ap[:, a + 2:b], in0=ux.ap[:, a:b - 2], scalar=w(0), in1=xl.ap[:, a + 2:b], op0=ALU.mult, op1=ALU.add), [ux, lv, xl], [xl])
                P.op('dve', lambda e, a=a, b=b: e.scalar_tensor_tensor(out=xl.ap[:, a:b - 1], in0=ux.ap[:, a + 1:b], scalar=w(3), in1=xl.ap[:, a:b - 1], op0=ALU.mult, op1=ALU.add), [ux, lv, xl], [xl])
            P.op('act', lambda e: e.copy(out=xlb.ap, in_=xl.ap), [xl], [xlb])
            hs = ux
            for d in range(2):
                chs = [(c, min(CH, (CTX if c < CTX else S) - c)) for c in list(range(0, CTX, CH)) + list(range(CTX, S, CH))]
                if d == 1:
                    chs = [c for c in chs if c[0] < CTX][::-1] + [c for c in chs if c[0] >= CTX][::-1]
                prev = None
                for (c0, n) in chs:
                    b = itc[0] % NB; itc[0] += 1
                    pr = P.ps(); pi = P.ps()
                    ir = (j * 2 + d) * 2
                    P.op('pe', lambda e, pr=pr, ir=ir, c0=c0, n=n: e.matmul(pr.ap[:, 0:n], gw3[:, ir, :], xlb.ap[:, c0:c0 + n], start=True, stop=True), [gw, xlb], [pr])
                    P.op('pe', lambda e, pi=pi, ir=ir, c0=c0, n=n: e.matmul(pi.ap[:, 0:n], gw3[:, ir + 1, :], xlb.ap[:, c0:c0 + n], start=True, stop=True), [gw, xlb], [pi])
                    cb = 20 + (d * 2) * 4 + j
                    P.op('act', lambda e, pr=pr, b=b, n=n, cb=cb: e.activation(out=rr[b].ap[:, 0:n], in_=pr.ap[:, 0:n], func=AF.Sigmoid, bias=lv.ap[:, cb:cb + 1], scale=1.0), [pr, lv], [rr[b]])
                    P.op('act', lambda e, pi=pi, b=b, n=n, cb=cb: e.activation(out=ii[b].ap[:, 0:n], in_=pi.ap[:, 0:n], func=AF.Sigmoid, bias=lv.ap[:, cb + 4:cb + 5], scale=1.0), [pi, lv], [ii[b]])
                    P.op('act', lambda e, b=b, n=n, d=d: e.activation(out=aa[b].ap[:, 0:n], in_=rr[b].ap[:, 0:n], func=AF.Exp, scale=cc.ap[:, d * 4 + j:d * 4 + j + 1]), [rr[b], cc], [aa[b]])
                    P.op('act', lambda e, b=b, n=n, d=d: e.activation(out=a2[b].ap[:, 0:n], in_=rr[b].ap[:, 0:n], func=AF.Exp, scale=cc2.ap[:, d * 4 + j:d * 4 + j + 1]), [rr[b], cc2], [a2[b]])
                    P.op('act', lambda e, b=b, n=n: e.activation(out=a2[b].ap[:, 0:n], in_=a2[b].ap[:, 0:n], func=AF.Sqrt, bias=self.one_ap, scale=-1.0), [a2[b], self.one_t], [a2[b]])
                    P.op('dve', lambda e, b=b, n=n, c0=c0: e.tensor_tensor(out=ii[b].ap[:, 0:n], in0=ii[b].ap[:, 0:n], in1=xl.ap[:, c0:c0 + n], op=ALU.mult), [ii[b], xl], [ii[b]])
                    P.op('dve', lambda e, b=b, n=n: e.tensor_tensor(out=ii[b].ap[:, 0:n], in0=ii[b].ap[:, 0:n], in1=a2[b].ap[:, 0:n], op=ALU.mult), [ii[b], a2[b]], [ii[b]])
                    if d == 0:
                        init = 0.0 if prev is None else hs.ap[:, c0 - 1:c0]
                        P.op('dve', lambda e, b=b, n=n, c0=c0, init=init: e.tensor_tensor_scan(hs.ap[:, c0:c0 + n], aa[b].ap[:, 0:n], ii[b].ap[:, 0:n], init, ALU.mult, ALU.add), [aa[b], ii[b], hs], [hs])
                    else:
                        if prev is None:
                            init = 0.0
                        elif c0 + n == S:
                            init = hh[(itc[0] - 2) % NB].ap[:, 0:1]
                        else:
                            init = hh[(itc[0] - 2) % NB].ap[:, 0:1]
                        rv = lambda ap, n=n: bass.AP(ap.tensor, ap.offset + n - 1, [list(ap.ap[0]), [-1, n]])
                        P.op('dve', lambda e, b=b, n=n, init=init, rv=rv: e.tensor_tensor_scan(rv(hh[b].ap[:, 0:n]), rv(aa[b].ap[:, 0:n]), rv(ii[b].ap[:, 0:n]), init, ALU.mult, ALU.add), [aa[b], ii[b], hh[(itc[0] - 2) % NB]], [hh[b]])
                        P.op('pool', lambda e, b=b, n=n, c0=c0: e.tensor_tensor(out=hs.ap[:, c0:c0 + n], in0=hs.ap[:, c0:c0 + n], in1=hh[b].ap[:, 0:n], op=ALU.add), [hs, hh[b]], [hs])
                    prev = (c0, n)
            P.op('act', lambda e: e.activation(out=xl.ap, in_=ug.ap, func=AF.Square), [ug], [xl])
            P.op('dve', lambda e: e.tensor_scalar(out=xl.ap, in0=xl.ap, scalar1=0.044715, scalar2=1.0, op0=ALU.mult, op1=ALU.add), [xl], [xl])
            P.op('dve', lambda e: e.tensor_tensor(out=xl.ap, in0=xl.ap, in1=ug.ap, op=ALU.mult), [xl, ug], [xl])
            P.op('act', lambda e: e.activation(out=xl.ap, in_=xl.ap, func=AF.Sigmoid, scale=1.5957691216057308), [xl], [xl])
            P.op('dve', lambda e: e.tensor_tensor(out=xl.ap, in0=xl.ap, in1=ug.ap, op=ALU.mult), [xl, ug], [xl])
            P.op('dve', lambda e: e.tensor_tensor(out=yb.ap, in0=xl.ap, in1=hs.ap, op=ALU.mult), [xl, hs], [yb])
            P.dma('sp', dr['YL'][j * 128:(j + 1) * 128, :], yb.ap, reads=[yb], writes=[sb['YL']])
        for j in range(4):
            do_tile(j)
        P.barrier()
        P.off = mark

    def rwkv_prep(self, l):
        P, dr, sb = self.P, self.dr, self.sb
        S = self.S
        mark = P.off
        st = P.tile(128, F32, 'rp_stage')
        P.dma('sp', st.ap[0:30, :], dr['rwkv_mu'][l].rearrange("m (j p) -> (m j) p", p=128), writes=[st])
        P.dma('sp', st.ap[30:34, :], dr['rwkv_k_k'][l].rearrange("(j p) -> j p", p=128), writes=[st])
        ps = P.ps()
        P.op('pe', lambda e: e.transpose(ps.ap[:, 0:34], st.ap[0:34, :], self.ident.ap[0:34, 0:34]), [st, self.ident], [ps])
        rv = P.tile(64, F32, 'rp_vec')
        P.op('dve', lambda e: e.tensor_copy(out=rv.ap[:, 0:34], in_=ps.ap[:, 0:34]), [ps], [rv])
        P.op('dve', lambda e: e.tensor_tensor(out=rv.ap[:, 34:49], in0=rv.ap[:, 0:15], in1=rv.ap[:, 15:30], op=ALU.add), [rv], [rv])
        P.op('dve', lambda e: e.tensor_scalar(out=rv.ap[:, 34:49], in0=rv.ap[:, 34:49], scalar1=-1.0, scalar2=1.0, op0=ALU.mult, op1=ALU.add), [rv], [rv])
        zin = [P.tile(S, F32, 'rp_zin%d' % i) for i in range(2)]
        zs = P.tile(S, F32, 'rp_zs')
        tmp = P.tile(S, F32, 'rp_tmp')
        zbs = [P.tile(S, BF16, 'rp_zb%d' % i) for i in range(2)]
        vsts = [P.tile(1024, BF16, 'rp_vst%d' % i) for i in range(2)]
        segs = [(0, CTX), (CTX, S)]
        cnt = [0]

        def do_tile(jt):
            zi = zin[jt % 2]
            zb = zbs[jt % 2]
            P.dma('sp' if jt % 2 == 0 else 'act', zi.ap, dr['ZR'][jt * 128:(jt + 1) * 128, :], reads=[sb['ZR']], writes=[zi])
            for (a, b) in segs:
                P.op('dve', lambda e, a=a, b=b: e.tensor_scalar(out=zs.ap[:, a:b], in0=zi.ap[:, a:b], scalar1=rv.ap[:, 34 + jt:35 + jt], scalar2=None, op0=ALU.mult), [zi, rv], [zs])
                P.op('dve', lambda e, a=a, b=b: e.scalar_tensor_tensor(out=zs.ap[:, a + 1:b], in0=zi.ap[:, a:b - 1], scalar=rv.ap[:, jt:jt + 1], in1=zs.ap[:, a + 1:b], op0=ALU.mult, op1=ALU.add), [zi, rv, zs], [zs])
                P.op('dve', lambda e, a=a, b=b: e.scalar_tensor_tensor(out=zs.ap[:, a:b - 1], in0=zi.ap[:, a + 1:b], scalar=rv.ap[:, 15 + jt:16 + jt], in1=zs.ap[:, a:b - 1], op0=ALU.mult, op1=ALU.add), [zi, rv, zs], [zs])
            if jt < 4:
                P.dma('sp', dr['RS'][jt * 128:(jt + 1) * 128, :], zs.ap, reads=[zs], writes=[sb['RS']])
            elif jt < 8:
                hp = jt - 4
                P.dma('sp', dr['KS'][hp * 128:(hp + 1) * 128, :], zs.ap, reads=[zs], writes=[sb['KS']])
                P.op('dve', lambda e: e.tensor_scalar(out=tmp.ap, in0=zs.ap, scalar1=rv.ap[:, 30 + hp:31 + hp], scalar2=None, op0=ALU.mult), [zs, rv], [tmp])
                P.op('act', lambda e: e.activation(out=zb.ap, in_=tmp.ap, func=AF.Square), [tmp], [zb])
                for blk in range(0, S, 512):
                    n = min(512, S - blk)
                    p2 = P.ps()
                    P.op('pe', lambda e, p2=p2, blk=blk, n=n: e.matmul(p2.ap[:, 0:n], self.blk64.ap, zb.ap[:, blk:blk + n], start=True, stop=True), [self.blk64, zb], [p2])
                    P.op('act', lambda e, p2=p2, blk=blk, n=n: e.activation(out=zs.ap[:, blk:blk + n], in_=p2.ap[:, 0:n], func=AF.Sqrt), [p2], [zs])
                P.op('dve', lambda e: e.tensor_scalar(out=zs.ap, in0=zs.ap, scalar1=1e-12, scalar2=None, op0=ALU.max), [zs], [zs])
                P.op('dve', lambda e: e.reciprocal(out=zs.ap, in_=zs.ap), [zs], [zs])
                P.op('dve', lambda e: e.tensor_tensor(out=tmp.ap, in0=tmp.ap, in1=zs.ap, op=ALU.mult), [tmp, zs], [tmp])
                P.dma('sp', dr['KK'][hp * 128:(hp + 1) * 128, :], tmp.ap, reads=[tmp], writes=[sb['KK']])
            elif jt < 12:
                hp = jt - 8
                P.op('act', lambda e: e.copy(out=zb.ap, in_=zs.ap), [zs], [zb])
                nt = S // 128
                for g0 in range(0, nt, 8):
                    ng = min(8, nt - g0)
                    pb = P.ps()
                    pbb = pb.ap.bitcast(BF16)
                    for i in range(ng):
                        P.op('pe', lambda e, pbb=pbb, i=i, g0=g0: e.transpose(pbb[:, i * 128:(i + 1) * 128], zb.ap[:, (g0 + i) * 128:(g0 + i + 1) * 128], self.identb.ap), [zb, self.identb], [pb])
                    vs = vsts[cnt[0] % 2]; cnt[0] += 1
                    if cnt[0] % 2 == 0:
                        P.op('act', lambda e, pbb=pbb, vs=vs, ng=ng: e.copy(out=vs.ap[:, 0:ng * 128], in_=pbb[:, 0:ng * 128]), [pb], [vs])
                    else:
                        P.op('dve', lambda e, pbb=pbb, vs=vs, ng=ng: e.tensor_copy(out=vs.ap[:, 0:ng * 128], in_=pbb[:, 0:ng * 128]), [pb], [vs])
                    P.dma('sp', dr['VT'][g0 * 128:(g0 + ng) * 128, hp * 128:(hp + 1) * 128].rearrange("(n p) c -> p n c", p=128), vs.ap[:, 0:ng * 128].rearrange("p (n c) -> p n c", c=128), reads=[vs], writes=[sb['VT']])
            else:
                nm = ('TW', 'AD', 'SG')[jt - 12]
                fn = (AF.Tanh, AF.Identity, AF.Sigmoid)[jt - 12]
                P.op('act', lambda e: e.activation(out=zb.ap, in_=zs.ap, func=fn), [zs], [zb])
                P.dma('sp', dr[nm][:, :], zb.ap, reads=[zb], writes=[sb[nm]])
        for jt in range(15):
            do_tile(jt)
        P.barrier()
        P.off = mark

    def rwkv_dir(self, l, d):
        P, dr, sb = self.P, self.dr, self.sb
        S, T = self.S, self.T
        mark = P.off
        LAM = float(np.exp(-0.5))
        lo, hi = d * 64, (d + 1) * 64
        wup = P.tile(512, BF16, 'r_wup'); aup = P.tile(512, BF16, 'r_aup')
        P.op('pool', lambda e: e.memset(wup.ap, 0.0), [], [wup])
        P.op('pool', lambda e: e.memset(aup.ap, 0.0), [], [aup])
        P.dma('pool', wup.ap[lo:hi, :], dr['rwkv_w_up'][l, d], writes=[wup])
        P.dma('pool', aup.ap[lo:hi, :], dr['rwkv_a_up'][l, d], writes=[aup])
        st = P.tile(128, F32, 'r_stage')
        P.dma('sp', st.ap[0:4, :], dr['rwkv_w0'][l, d].rearrange("(j p) -> j p", p=128), writes=[st])
        P.dma('sp', st.ap[4:8, :], dr['rwkv_a0'][l, d].rearrange("(j p) -> j p", p=128), writes=[st])
        P.dma('sp', st.ap[8:12, :], dr['rwkv_k_a'][l].rearrange("(j p) -> j p", p=128), writes=[st])
        P.dma('sp', st.ap[12:16, :], dr['rwkv_r_k'][l].rearrange("(j p) -> j p", p=128), writes=[st])
        ps0 = P.ps()
        P.op('pe', lambda e: e.transpose(ps0.ap[:, 0:16], st.ap[0:16, :], self.ident.ap[0:16, 0:16]), [st, self.ident], [ps0])
        vv = P.tile(24, F32, 'r_vv')
        P.op('dve', lambda e: e.tensor_copy(out=vv.ap[:, 0:16], in_=ps0.ap[:, 0:16]), [ps0], [vv])
        P.op('dve', lambda e: e.tensor_scalar(out=vv.ap[:, 16:20], in0=vv.ap[:, 8:12], scalar1=-1.0, scalar2=1.0, op0=ALU.mult, op1=ALU.add), [vv], [vv])
        if d == 1:
            gup = P.tile(512, BF16, 'r_gup')
            P.dma('pool', gup.ap, dr['rwkv_g_up'][l], writes=[gup])
            lng = P.tile(512, F32, 'r_lng'); lnb = P.tile(512, F32, 'r_lnb')
            P.dma('sp', lng.ap, dr['rwkv_ln_g'][l].partition_broadcast(128), writes=[lng])
            P.dma('sp', lnb.ap, dr['rwkv_ln_b'][l].partition_broadcast(128), writes=[lnb])
        A = P.tile(256, F32, 'r_A'); Ap = P.tile(256, BF16, 'r_Ap')
        P.op('dve', lambda e: e.memset(A.ap, 0.0), [], [A])
        BTP = [P.tile(1024, BF16, 'r_btp%d' % i) for i in range(2)]
        KTP = [P.tile(1024, BF16, 'r_ktp%d' % i) for i in range(2)]
        for t_ in BTP + KTP:
            P.op('pool', lambda e, t_=t_: e.memset(t_.ap, 0.0), [], [t_])
        def blkset(i):
            return dict(KR=P.tile(4096, BF16, 'r_KR%d' % i), BT=P.tile(2048, BF16, 'r_BT%d' % i), KT=P.tile(2048, BF16, 'r_KT%d' % i),
                        PR=P.tile(2048, F32, 'r_PR%d' % i), ecm=P.tile(16, F32, 'r_ecm%d' % i), efin=P.tile(16, F32, 'r_efin%d' % i),
                        wtot=P.tile(16, F32, 'r_wtot%d' % i), tw=P.tile(512, BF16, 'r_tw%d' % i), ad=P.tile(512, BF16, 'r_ad%d' % i),
                        sgd=P.tile(512, BF16, 'r_sgd%d' % i),
                        KRZ=P.tile(8192, BF16, 'r_KRZ%d' % i), BZ=P.tile(4096, BF16, 'r_BZ%d' % i), KZ=P.tile(4096, BF16, 'r_KZ%d' % i))
        bsets = [blkset(0)]
        bsets.append(bsets[0])
        for nm_ in ('KRZ', 'BZ', 'KZ'):
            P.op('pool', lambda e, nm_=nm_: e.memset(bsets[0][nm_].ap, 0.0), [], [bsets[0][nm_]])
        NT_ = 1
        tl = {nm: [P.tile(512, F32, 'r_%s%d' % (nm, i)) for i in range(NT_)] for nm in ('r', 'k', 'kk', 'sg', 'aa', 'kd', 'beta', 'cs', 'cse', 'dinc', 'e1', 'e2', 'e3')}
        G = [P.tile(8 * 512, BF16, 'r_G%d' % i) for i in range(2)]
        Xb = [P.tile(1024, BF16, 'r_X%d' % i) for i in range(2)]
        XTb = [P.tile(1024, BF16, 'r_XT%d' % i) for i in range(2)]
        PTb = [P.tile(1024, BF16, 'r_PT%d' % i) for i in range(2)]
        vms = [P.tile(512, BF16, 'r_vm%d' % i) for i in range(2)]
        rhs_t = P.tile(512, BF16, 'r_rhs'); u_t = P.tile(512, BF16, 'r_u')
        stmp = P.tile(256, F32, 'r_stmp')
        bon = P.tile(512, F32, 'r_bon'); oacc = [P.tile(512, F32, 'r_oacc%d' % i) for i in range(2)]
        if d == 1:
            osp = [P.tile(512, F32, 'r_osp%d' % i) for i in range(2)]
            gn = {nm: P.tile(n_, F32, 'r_gn_' + nm) for nm, n_ in (('mean', 8), ('var', 8), ('cen', 512), ('sq', 512))}
            yb = P.tile(512, BF16, 'r_yb'); yst = [P.tile(512, BF16, 'r_yst%d' % i) for i in range(2)]
        blocks = [(0, CTX)] + [(CTX + 512 * i, 512) for i in range(T // 512)]
        if d == 1:
            blocks = [blocks[0]] + blocks[1:][::-1]
        cc = [0]
        tc = [0]

        def prep_block(bi, c0, n):
            bs = bsets[bi % 2]
            nch = n // 128
            P.dma('sp', bs['tw'].ap[:, 0:n], dr['TW'][:, c0:c0 + n], reads=[sb['TW']], writes=[bs['tw']])
            P.dma('sp', bs['ad'].ap[:, 0:n], dr['AD'][:, c0:c0 + n], reads=[sb['AD']], writes=[bs['ad']])
            if d == 1:
                P.dma('sp', bs['sgd'].ap[:, 0:n], dr['SG'][:, c0:c0 + n], reads=[sb['SG']], writes=[bs['sgd']])
            KR5 = bs['KR'].ap.rearrange("p (h c x t) -> p h c x t", h=4, c=4, x=2)
            BT4 = bs['BT'].ap.rearrange("p (h c t) -> p h c t", h=4, c=4)
            KT4 = bs['KT'].ap.rearrange("p (h c t) -> p h c t", h=4, c=4)
            PR3 = v3(bs['PR'].ap, 4)
            ecm3 = v3(bs['ecm'].ap, 4); efin3 = v3(bs['efin'].ap, 4); wtot3 = v3(bs['wtot'].ap, 4)
            KRZ6 = bs['KRZ'].ap.rearrange("p (h c x q t) -> p h c x q t", h=4, c=4, x=2, q=2)
            BZ5 = bs['BZ'].ap.rearrange("p (h c q t) -> p h c q t", h=4, c=4, q=2)
            KZ5 = bs['KZ'].ap.rearrange("p (h c q t) -> p h c q t", h=4, c=4, q=2)
            def prep_hp(hp):
                b = tc[0] % NT_; tc[0] += 1
                t = {nm: tl[nm][b] for nm in tl}
                v_ = lambda nm: t[nm].ap[:, 0:n]
                c3 = lambda nm: t[nm].ap[:, 0:n].rearrange("p (c t) -> p c t", t=128)
                P.dma('sp', v_('r'), dr['RS'][hp * 128:(hp + 1) * 128, c0:c0 + n], reads=[sb['RS']], writes=[t['r']])
                P.dma('act', v_('k'), dr['KS'][hp * 128:(hp + 1) * 128, c0:c0 + n], reads=[sb['KS']], writes=[t['k']])
                P.dma('sp', v_('kk'), dr['KK'][hp * 128:(hp + 1) * 128, c0:c0 + n], reads=[sb['KK']], writes=[t['kk']])
                pw = P.ps(); pa = P.ps()
                P.op('pe', lambda e, pw=pw, hp=hp: e.matmul(pw.ap[:, 0:n], wup.ap[:, hp * 128:(hp + 1) * 128], bs['tw'].ap[:, 0:n], start=True, stop=True), [wup, bs['tw']], [pw])
                P.op('pe', lambda e, pa=pa, hp=hp: e.matmul(pa.ap[:, 0:n], aup.ap[:, hp * 128:(hp + 1) * 128], bs['ad'].ap[:, 0:n], start=True, stop=True), [aup, bs['ad']], [pa])
                P.op('act', lambda e, pw=pw, hp=hp, v_=v_: e.activation(out=v_('sg'), in_=pw.ap[:, 0:n], func=AF.Sigmoid, bias=vv.ap[:, hp:hp + 1], scale=1.0), [pw, vv], [t['sg']])
                P.op('act', lambda e, pa=pa, hp=hp, v_=v_: e.activation(out=v_('aa'), in_=pa.ap[:, 0:n], func=AF.Sigmoid, bias=vv.ap[:, 4 + hp:5 + hp], scale=1.0), [pa, vv], [t['aa']])
                P.op('dve', lambda e, hp=hp, v_=v_: e.tensor_scalar(out=v_('kd'), in0=v_('aa'), scalar1=vv.ap[:, 8 + hp:9 + hp], scalar2=vv.ap[:, 16 + hp:17 + hp], op0=ALU.mult, op1=ALU.add), [t['aa'], vv], [t['kd']])
                P.op('dve', lambda e, v_=v_: e.tensor_tensor(out=v_('kd'), in0=v_('kd'), in1=v_('k'), op=ALU.mult), [t['kd'], t['k']], [t['kd']])
                P.op('dve', lambda e, v_=v_: e.tensor_tensor(out=v_('beta'), in0=v_('aa'), in1=v_('kk'), op=ALU.mult), [t['aa'], t['kk']], [t['beta']])
                P.op('dve', lambda e, hp=hp, v_=v_: e.scalar_tensor_tensor(out=PR3[:, hp, 0:n], in0=v_('r'), scalar=vv.ap[:, 12 + hp:13 + hp], in1=v_('kd'), op0=ALU.mult, op1=ALU.mult), [t['r'], vv, t['kd']], [bs['PR']])
                if d == 0:
                    P.op('dve', lambda e, v_=v_: e.tensor_tensor_scan(v_('cs'), self.rst[0].ap[:, 0:n], v_('sg'), 0.0, ALU.mult, ALU.add), [self.rst[0], t['sg']], [t['cs']])
                else:
                    rv_ = lambda ap: bass.AP(ap.tensor, ap.offset + n - 1, [list(ap.ap[0]), [-1, n]])
                    P.op('dve', lambda e, v_=v_, rv_=rv_: e.tensor_tensor_scan(rv_(v_('cs')), rv_(self.rst[1].ap[:, 0:n]), rv_(v_('sg')), 0.0, ALU.mult, ALU.add), [self.rst[1], t['sg']], [t['cs']])
                P.op('dve', lambda e, v_=v_: e.tensor_tensor(out=v_('cse'), in0=v_('cs'), in1=v_('sg'), op=ALU.subtract), [t['cs'], t['sg']], [t['cse']])
                cmid = lambda c3=c3: c3('cs')[:, :, 63]
                P.op('dve', lambda e, c3=c3, cmid=cmid: e.tensor_tensor(out=c3('cse'), in0=c3('cse'), in1=bc(cmid(), [128, nch, 128], 2), op=ALU.subtract), [t['cse'], t['cs']], [t['cse']])
                P.op('dve', lambda e, c3=c3, cmid=cmid: e.tensor_tensor(out=c3('dinc'), in0=c3('cs'), in1=bc(cmid(), [128, nch, 128], 2), op=ALU.subtract), [t['cs']], [t['dinc']])
                P.op('act', lambda e, v_=v_: e.activation(out=v_('e1'), in_=v_('cse'), func=AF.Exp, scale=-LAM), [t['cse']], [t['e1']])
                P.op('act', lambda e, v_=v_: e.activation(out=v_('e2'), in_=v_('dinc'), func=AF.Exp, scale=-LAM), [t['dinc']], [t['e2']])
                P.op('act', lambda e, v_=v_: e.activation(out=v_('e3'), in_=v_('dinc'), func=AF.Exp, scale=LAM), [t['dinc']], [t['e3']])
                P.op('act', lambda e, hp=hp, cmid=cmid: e.activation(out=ecm3[:, hp, 0:nch], in_=cmid(), func=AF.Exp, scale=-LAM), [t['cs']], [bs['ecm']])
                ce = 127 if d == 0 else 0
                P.op('act', lambda e, hp=hp, c3=c3, ce=ce: e.activation(out=efin3[:, hp, 0:nch], in_=c3('dinc')[:, :, ce], func=AF.Exp, scale=-LAM), [t['dinc']], [bs['efin']])
                P.op('dve', lambda e, hp=hp: e.tensor_tensor(out=wtot3[:, hp, 0:nch], in0=efin3[:, hp, 0:nch], in1=ecm3[:, hp, 0:nch], op=ALU.mult), [bs['efin'], bs['ecm']], [bs['wtot']])
                P.op('dve', lambda e, hp=hp, c3=c3: e.scalar_tensor_tensor(out=KR5[:, hp, 0:nch, 0, :], in0=c3('kk'), scalar=-1.0, in1=c3('e1'), op0=ALU.mult, op1=ALU.mult), [t['kk'], t['e1']], [bs['KR']])
                P.op('dve', lambda e, hp=hp, c3=c3: e.tensor_tensor(out=KR5[:, hp, 0:nch, 1, :], in0=c3('r'), in1=c3('e2'), op=ALU.mult), [t['r'], t['e2']], [bs['KR']])
                P.op('dve', lambda e, hp=hp, c3=c3: e.tensor_tensor(out=BT4[:, hp, 0:nch, :], in0=c3('beta'), in1=c3('e3'), op=ALU.mult), [t['beta'], t['e3']], [bs['BT']])
                P.op('dve', lambda e, hp=hp, c3=c3: e.tensor_tensor(out=KT4[:, hp, 0:nch, :], in0=c3('kd'), in1=c3('e3'), op=ALU.mult), [t['kd'], t['e3']], [bs['KT']])
                for par in range(2):
                    psl = slice(par * 64, par * 64 + 64)
                    P.op('dve', lambda e, hp=hp, c3=c3, par=par, psl=psl: e.scalar_tensor_tensor(out=KRZ6[psl, hp, 0:nch, 0, par, :], in0=c3('kk')[psl], scalar=-1.0, in1=c3('e1')[psl], op0=ALU.mult, op1=ALU.mult), [t['kk'], t['e1']], [bs['KRZ']])
                    P.op('pool', lambda e, hp=hp, c3=c3, par=par, psl=psl: e.tensor_tensor(out=KRZ6[psl, hp, 0:nch, 1, par, :], in0=c3('r')[psl], in1=c3('e2')[psl], op=ALU.mult), [t['r'], t['e2']], [bs['KRZ']])
                    P.op('pool', lambda e, hp=hp, c3=c3, par=par, psl=psl: e.tensor_tensor(out=BZ5[psl, hp, 0:nch, par, :], in0=c3('beta')[psl], in1=c3('e3')[psl], op=ALU.mult), [t['beta'], t['e3']], [bs['BZ']])
                    P.op('pool', lambda e, hp=hp, c3=c3, par=par, psl=psl: e.tensor_tensor(out=KZ5[psl, hp, 0:nch, par, :], in0=c3('kd')[psl], in1=c3('e3')[psl], op=ALU.mult), [t['kd'], t['e3']], [bs['KZ']])
            for hp in range(4):
                prep_hp(hp)

        def do_chunk(bi, c0, ch):
            if RK_STOP == 0:
                return
            bs = bsets[bi % 2]
            ci = cc[0]; cc[0] += 1
            t0 = c0 + ch * 128
            KR5 = bs['KR'].ap.rearrange("p (h c x t) -> p h c x t", h=4, c=4, x=2)
            BT4 = bs['BT'].ap.rearrange("p (h c t) -> p h c t", h=4, c=4)
            KT4 = bs['KT'].ap.rearrange("p (h c t) -> p h c t", h=4, c=4)
            PR3 = v3(bs['PR'].ap, 4)
            ecm3 = v3(bs['ecm'].ap, 4); efin3 = v3(bs['efin'].ap, 4); wtot3 = v3(bs['wtot'].ap, 4)
            KRZ6 = bs['KRZ'].ap.rearrange("p (h c x q t) -> p h c x q t", h=4, c=4, x=2, q=2)
            BZ5 = bs['BZ'].ap.rearrange("p (h c q t) -> p h c q t", h=4, c=4, q=2)
            KZ5 = bs['KZ'].ap.rearrange("p (h c q t) -> p h c q t", h=4, c=4, q=2)
            vm = vms[ci % 2]
            P.dma('sp', vm.ap, dr['VT'][t0:t0 + 128, :], reads=[sb['VT']], writes=[vm])
            if d == 1:
                op_ = osp[ci % 2]
                P.dma('act', op_.ap, dr['OS'][t0:t0 + 128, :], reads=[sb['OS']], writes=[op_])
            Gt = G[ci % 2]
            G3 = v3(Gt.ap, 8)
            X = Xb; XT = XTb; PT = PTb
            x3 = [v3(X[i].ap, 8) for i in range(2)]
            xt3 = [v3(XT[i].ap, 8) for i in range(2)]
            pt3 = [v3(PT[i].ap, 8) for i in range(2)]
            hsl = lambda h: slice((h % 2) * 64, (h % 2) * 64 + 64)
            for h in range(8):
                hp = h // 2
                pg = P.ps()
                P.op('pe', lambda e, pg=pg, h=h, hp=hp: e.matmul(pg.ap[:, 0:256], BZ5[:, hp, ch, h % 2, :], KR5[:, hp, ch, :, :], start=True, stop=True), [bs['BZ'], bs['KR']], [pg])
                P.op('pe', lambda e, pg=pg, h=h, hp=hp: e.matmul(pg.ap[:, 256:512], KZ5[:, hp, ch, h % 2, :], KR5[:, hp, ch, :, :], start=True, stop=True), [bs['KZ'], bs['KR']], [pg])
                P.op('dve', lambda e, pg=pg, h=h: e.tensor_tensor(out=G3[:, h, :], in0=pg.ap, in1=self.mask4[d].ap, op=ALU.mult), [pg, self.mask4[d]], [Gt])
            if RK_STOP == 1:
                return
            for half in range(2):
                pn = P.ps()
                for hh in range(4):
                    h = half * 4 + hh
                    hp = h // 2
                    P.op('pe', lambda e, pn=pn, h=h, hp=hp, hh=hh: e.matmul(pn.ap[:, hh * 128:(hh + 1) * 128], KRZ6[:, hp, ch, 0, h % 2, :], BT4[:, hp, ch, :], start=True, stop=True), [bs['BT'], bs['KRZ']], [pn])
                P.op('dve', lambda e, pn=pn, half=half: e.tensor_tensor(out=X[0].ap[:, half * 512:(half + 1) * 512], in0=pn.ap, in1=self.maskN[d].ap, op=ALU.mult), [pn, self.maskN[d]], [X[0]])
            if RK_STOP == 2:
                return
            P.op('pool', lambda e: e.tensor_tensor(out=pt3[0], in0=G3[:, :, 0:128], in1=bc(self.identb.ap, [128, 8, 128], 1), op=ALU.add), [Gt, self.identb], [PT[0]])
            cur = 0
            for rnd in range(6):
                nxt = 1 - cur
                xt_prev = (lambda h: G3[:, h, 0:128]) if rnd == 0 else (lambda h, cur=cur: xt3[cur][:, h, :])
                xt_buf = Gt if rnd == 0 else XT[cur]
                for half in range(2):
                    px = P.ps()
                    for hh in range(4):
                        h = half * 4 + hh
                        P.op('pe', lambda e, px=px, h=h, hh=hh, xt_prev=xt_prev, cur=cur: e.matmul(px.ap[:, hh * 128:(hh + 1) * 128], xt_prev(h), x3[cur][:, h, :], start=True, stop=True), [xt_buf, X[cur]], [px])
                    P.op('act', lambda e, px=px, half=half, nxt=nxt: e.copy(out=X[nxt].ap[:, half * 512:(half + 1) * 512], in_=px.ap), [px], [X[nxt]])
                    if rnd < 5:
                        pxt = P.ps()
                        for hh in range(4):
                            h = half * 4 + hh
                            P.op('pe', lambda e, pxt=pxt, h=h, hh=hh, xt_prev=xt_prev, cur=cur: e.matmul(pxt.ap[:, hh * 128:(hh + 1) * 128], x3[cur][:, h, :], xt_prev(h), start=True, stop=True), [xt_buf, X[cur]], [pxt])
                        P.op('act', lambda e, pxt=pxt, half=half, nxt=nxt: e.copy(out=XT[nxt].ap[:, half * 512:(half + 1) * 512], in_=pxt.ap), [pxt], [XT[nxt]])
                    pp = P.ps()
                    for hh in range(4):
                        h = half * 4 + hh
                        P.op('pe', lambda e, pp=pp, h=h, hh=hh, nxt=nxt, cur=cur: e.matmul(pp.ap[:, hh * 128:(hh + 1) * 128], x3[nxt][:, h, :], pt3[cur][:, h, :], start=True, stop=True), [X[nxt], PT[cur]], [pp])
                    P.op('dve', lambda e, pp=pp, half=half, nxt=nxt, cur=cur: e.tensor_tensor(out=PT[nxt].ap[:, half * 512:(half + 1) * 512], in0=pp.ap, in1=PT[cur].ap[:, half * 512:(half + 1) * 512], op=ALU.add), [pp, PT[cur]], [PT[nxt]])
                cur = nxt
            PTf = pt3[cur]; PTbuf = PT[cur]
            if RK_STOP == 3:
                return
            btp = BTP[ci % 2]; ktp = KTP[ci % 2]
            ptp = P.ps()
            ptb = ptp.ap.bitcast(BF16)
            for hp in range(4):
                P.op('pe', lambda e, hp=hp: e.transpose(ptb[:, hp * 128:(hp + 1) * 128], BT4[:, hp, ch, :], self.identb.ap), [bs['BT'], self.identb], [ptp])
            for hp in range(4):
                P.op('pe', lambda e, hp=hp: e.transpose(ptb[:, 512 + hp * 128:512 + (hp + 1) * 128], KT4[:, hp, ch, :], self.identb.ap), [bs['KT'], self.identb], [ptp])
            for wh, dst in ((0, btp), (1, ktp)):
                src4 = ptb[:, wh * 512:(wh + 1) * 512].rearrange("p (h x k) -> p h x k", h=4, x=2)
                dst4 = dst.ap.rearrange("p (h x c) -> p h x c", h=4, x=2)
                for par in range(2):
                    P.op('act', lambda e, src4=src4, dst4=dst4, par=par: e.copy(out=dst4[:, :, par, par * 64:(par + 1) * 64], in_=src4[:, :, par, :]), [ptp], [dst])
            btp3 = v3(btp.ap, 8); ktp3 = v3(ktp.ap, 8)
            if RK_STOP == 4:
                return
            Ap3 = v3(Ap.ap, 4); A3 = v3(A.ap, 4)
            P.op('dve', lambda e: e.tensor_tensor(out=Ap3, in0=A3, in1=bc(ecm3[:, :, ch], [128, 4, 64], 2), op=ALU.mult), [A, bs['ecm']], [Ap])
            pr = P.ps()
            for h in range(8):
                hp = h // 2
                P.op('pe', lambda e, h=h, hp=hp: e.matmul(pr.ap[:, h * 64:(h + 1) * 64], KRZ6[:, hp, ch, 0, h % 2, :], Ap3[:, hp, :], start=True, stop=False), [bs['KRZ'], Ap], [pr])
                P.op('pe', lambda e, h=h: e.matmul(pr.ap[:, h * 64:(h + 1) * 64], G3[:, h, 256:384], vm.ap[:, h * 64:(h + 1) * 64], start=False, stop=True), [Gt, vm], [pr])
            P.op('act', lambda e: e.copy(out=rhs_t.ap, in_=pr.ap), [pr], [rhs_t])
            pu = P.ps()
            for h in range(8):
                P.op('pe', lambda e, h=h: e.matmul(pu.ap[:, h * 64:(h + 1) * 64], PTf[:, h, :], rhs_t.ap[:, h * 64:(h + 1) * 64], start=True, stop=True), [PTbuf, rhs_t], [pu])
            P.op('dve', lambda e: e.tensor_copy(out=u_t.ap, in_=pu.ap), [pu], [u_t])
            pst = P.ps()
            for hp in range(4):
                for par in range(2):
                    h = hp * 2 + par
                    P.op('pe', lambda e, h=h, hp=hp, par=par: e.matmul(pst.ap[:, hp * 64:(hp + 1) * 64], btp3[:, h, :], u_t.ap[:, h * 64:(h + 1) * 64], start=(par == 0), stop=False), [btp, u_t], [pst])
                    P.op('pe', lambda e, h=h, hp=hp, par=par: e.matmul(pst.ap[:, hp * 64:(hp + 1) * 64], ktp3[:, h, :], vm.ap[:, h * 64:(h + 1) * 64], start=False, stop=(par == 1)), [ktp, vm], [pst])
            if RK_STOP == 5:
                return
            po = P.ps()
            for h in range(8):
                hp = h // 2
                P.op('pe', lambda e, h=h, hp=hp: e.matmul(po.ap[:, h * 64:(h + 1) * 64], KRZ6[:, hp, ch, 1, h % 2, :], Ap3[:, hp, :], start=True, stop=False), [bs['KRZ'], Ap], [po])
                P.op('pe', lambda e, h=h: e.matmul(po.ap[:, h * 64:(h + 1) * 64], G3[:, h, 128:256], u_t.ap[:, h * 64:(h + 1) * 64], start=False, stop=False), [Gt, u_t], [po])
                P.op('pe', lambda e, h=h: e.matmul(po.ap[:, h * 64:(h + 1) * 64], G3[:, h, 384:512], vm.ap[:, h * 64:(h + 1) * 64], start=False, stop=True), [Gt, vm], [po])
            pbd = P.ps()
            for hp in range(4):
                P.op('pe', lambda e, hp=hp: e.matmul(pbd.ap[:, hp * 2:(hp + 1) * 2], PR3[:, hp, ch * 128:(ch + 1) * 128], self.sel.ap, start=True, stop=True), [bs['PR'], self.sel], [pbd])
            P.op('dve', lambda e: e.tensor_tensor(out=v3(stmp.ap, 4), in0=v3(pst.ap[:, 0:256], 4), in1=bc(efin3[:, :, ch], [128, 4, 64], 2), op=ALU.mult), [pst, bs['efin']], [stmp])
            P.op('dve', lambda e: e.tensor_tensor(out=A3, in0=A3, in1=bc(wtot3[:, :, ch], [128, 4, 64], 2), op=ALU.mult), [A, bs['wtot']], [A])
            P.op('dve', lambda e: e.tensor_tensor(out=A.ap, in0=A.ap, in1=stmp.ap, op=ALU.add), [A, stmp], [A])
            oa = oacc[ci % 2]
            P.op('dve', lambda e: e.tensor_tensor(out=v3(bon.ap, 8), in0=v3(vm.ap, 8), in1=bc(pbd.ap[:, 0:8], [128, 8, 64], 2), op=ALU.mult), [vm, pbd], [bon])
            P.op('dve', lambda e: e.tensor_tensor(out=oa.ap, in0=po.ap, in1=bon.ap, op=ALU.add), [po, bon], [oa])
            if d == 0:
                P.dma('sp', dr['OS'][t0:t0 + 128, :], oa.ap, reads=[oa], writes=[sb['OS']])
                return
            P.op('pool', lambda e: e.tensor_tensor(out=oa.ap, in0=oa.ap, in1=op_.ap, op=ALU.add), [oa, op_], [oa])
            oa3 = v3(oa.ap, 8)
            P.op('dve', lambda e: e.tensor_reduce(out=gn['mean'].ap, in_=oa3, axis=AX.X, op=ALU.add), [oa], [gn['mean']])
            P.op('dve', lambda e: e.tensor_scalar(out=gn['mean'].ap, in0=gn['mean'].ap, scalar1=1.0 / 64, scalar2=None, op0=ALU.mult), [gn['mean']], [gn['mean']])
            P.op('dve', lambda e: e.tensor_tensor(out=v3(gn['cen'].ap, 8), in0=oa3, in1=bc(gn['mean'].ap, [128, 8, 64], 2), op=ALU.subtract), [oa, gn['mean']], [gn['cen']])
            P.op('act', lambda e: e.activation(out=gn['sq'].ap, in_=gn['cen'].ap, func=AF.Square), [gn['cen']], [gn['sq']])
            P.op('dve', lambda e: e.tensor_reduce(out=gn['var'].ap, in_=v3(gn['sq'].ap, 8), axis=AX.X, op=ALU.add), [gn['sq']], [gn['var']])
            P.op('act', lambda e: e.activation(out=gn['var'].ap, in_=gn['var'].ap, func=AF.Sqrt, bias=self.lneps_ap, scale=1.0 / 64), [gn['var'], self.lneps_t], [gn['var']])
            P.op('dve', lambda e: e.reciprocal(out=gn['var'].ap, in_=gn['var'].ap), [gn['var']], [gn['var']])
            P.op('dve', lambda e: e.tensor_tensor(out=v3(gn['cen'].ap, 8), in0=v3(gn['cen'].ap, 8), in1=bc(gn['var'].ap, [128, 8, 64], 2), op=ALU.mult), [gn['cen'], gn['var']], [gn['cen']])
            P.op('pool', lambda e: e.tensor_tensor(out=gn['cen'].ap, in0=gn['cen'].ap, in1=lng.ap, op=ALU.mult), [gn['cen'], lng], [gn['cen']])
            P.op('pool', lambda e: e.tensor_tensor(out=gn['cen'].ap, in0=gn['cen'].ap, in1=lnb.ap, op=ALU.add), [gn['cen'], lnb], [gn['cen']])
            pgt = P.ps()
            P.op('pe', lambda e: e.matmul(pgt.ap, bs['sgd'].ap[:, ch * 128:(ch + 1) * 128], gup.ap, start=True, stop=True), [bs['sgd'], gup], [pgt])
            P.op('dve', lambda e: e.tensor_tensor(out=yb.ap, in0=pgt.ap, in1=gn['cen'].ap, op=ALU.mult), [pgt, gn['cen']], [yb])
            pyt = P.ps()
            pytb = pyt.ap.bitcast(BF16)
            for hp in range(4):
                P.op('pe', lambda e, hp=hp: e.transpose(pytb[:, hp * 128:(hp + 1) * 128], yb.ap[:, hp * 128:(hp + 1) * 128], self.identb.ap), [yb, self.identb], [pyt])
            ys = yst[ci % 2]
            P.op('act', lambda e: e.copy(out=ys.ap, in_=pytb[:, 0:512]), [pyt], [ys])
            P.dma('sp', dr['YR'][:, t0:t0 + 128].rearrange("(j p) t -> p j t", p=128), v3(ys.ap, 4), reads=[ys], writes=[sb['YR']])

        for bi, (c0, n) in enumerate(blocks):
            prep_block(bi, c0, n)
            chs = list(range(n // 128))
            if d == 1:
                chs = chs[::-1]
            for ch in chs:
                do_chunk(bi, c0, ch)
        P.barrier()
        P.off = mark

    def merge(self, l, mod):
        P, dr, sb = self.P, self.dr, self.sb
        mark = P.off
        n = CG
        wg = P.tile(8 * 3072, BF16, 'm_wg'); wg3 = v3(wg.ap, 8)
        for k in range(8):
            P.dma('pool', wg3[:, k, :], dr['w_in'][l, k * 128:(k + 1) * 128, 3712:6784], writes=[wg])
        wp = P.tile(12 * D, BF16, 'm_wp'); wp3 = v3(wp.ap, 12)
        for b in range(3):
            for kk in range(4):
                P.dma('pool', wp3[:, b * 4 + kk, :], dr['branch_proj'][l, b, kk * 128:(kk + 1) * 128, :], writes=[wp])
        wo = P.tile(8 * D, BF16, 'm_wo'); wo3 = v3(wo.ap, 8)
        for k in range(8):
            P.dma('pool', wo3[:, k, :], dr['w_out'][l, k * 128:(k + 1) * 128, :], writes=[wo])
        xts = [P.tile(8 * n, F32, 'xt%d' % b) for b in range(2)]
        xm = P.tile(8 * n, BF16, 'xm'); xn = P.tile(8 * n, F32, 'xn'); sq = P.tile(8 * n, BF16, 'sq'); rs = P.tile(n, F32, 'rs')
        xm3 = v3(xm.ap, 8)
        ys = [P.tile(12 * n, BF16, 'm_y%d' % b) for b in range(2)]
        sgs = [P.tile(n, F32, 'm_sg%d' % b) for b in range(2)]
        tts = [P.tile(n, F32, 'm_t%d' % b) for b in range(2)]
        accf = P.tile(n, F32, 'm_accf')
        accb = P.tile(8 * n, BF16, 'm_accb'); accb3 = v3(accb.ap, 8)
        chunks = self.chunks
        m1 = lambda v: mod[(1, v)]

        def load_y(ci):
            c0 = chunks[ci][0]
            y = ys[ci % 2]; y3 = v3(y.ap, 12)
            for b, nm in enumerate(('YA', 'YL', 'YR')):
                P.dma('act', y3[:, b * 4:(b + 1) * 4, :], dr[nm][:, c0:c0 + n].rearrange("(j p) t -> p j t", p=128), reads=[sb[nm]], writes=[y])
        self.load_norm(chunks[0][0], n, chunks[0][2], 1, mod, xts[0], xm, sq, rs, xn)
        load_y(0)
        cnt = [0]
        for ci, (c0, _, v) in enumerate(chunks):
            xt = xts[ci % 2]; xt3 = v3(xt.ap, 8)
            y = ys[ci % 2]; y3 = v3(y.ap, 12)
            m = m1(v)
            for dt in range(8):
                for b in range(3):
                    pgt = P.ps(); ppj = P.ps()
                    for k in range(8):
                        P.op('pe', lambda e, k=k, b=b, dt=dt, pgt=pgt: e.matmul(pgt.ap[:, 0:n], wg3[:, k, b * D + dt * 128:b * D + (dt + 1) * 128], xm3[:, k, :], start=(k == 0), stop=(k == 7)), [wg, xm], [pgt])
                    for kk in range(4):
                        P.op('pe', lambda e, kk=kk, b=b, dt=dt, ppj=ppj, y3=y3: e.matmul(ppj.ap[:, 0:n], wp3[:, b * 4 + kk, dt * 128:(dt + 1) * 128], y3[:, b * 4 + kk, :], start=(kk == 0), stop=(kk == 3)), [wp, y], [ppj])
                    sg_ = sgs[cnt[0] % 2]; tt = tts[cnt[0] % 2]; cnt[0] += 1
                    P.op('act', lambda e, pgt=pgt, sg_=sg_: e.activation(out=sg_.ap, in_=pgt.ap[:, 0:n], func=AF.Sigmoid), [pgt], [sg_])
                    if b == 0:
                        P.op('dve', lambda e, ppj=ppj, sg_=sg_: e.tensor_tensor(out=accf.ap, in0=ppj.ap[:, 0:n], in1=sg_.ap, op=ALU.mult), [ppj, sg_], [accf])
                    else:
                        P.op('dve', lambda e, ppj=ppj, sg_=sg_, tt=tt: e.tensor_tensor(out=tt.ap, in0=ppj.ap[:, 0:n], in1=sg_.ap, op=ALU.mult), [ppj, sg_], [tt])
                        if b == 1:
                            P.op('pool', lambda e, tt=tt: e.tensor_tensor(out=accf.ap, in0=accf.ap, in1=tt.ap, op=ALU.add), [accf, tt], [accf])
                        else:
                            P.op('pool', lambda e, tt=tt, dt=dt: e.tensor_tensor(out=accb3[:, dt, :], in0=accf.ap, in1=tt.ap, op=ALU.add), [accf, tt], [accb])
            if ci + 1 < len(chunks):
                self.load_norm(chunks[ci + 1][0], n, chunks[ci + 1][2], 1, mod, xts[(ci + 1) % 2], xm, sq, rs, xn)
                load_y(ci + 1)
            for d2 in range(8):
                po = P.ps()
                for dt in range(8):
                    P.op('pe', lambda e, dt=dt, d2=d2, po=po: e.matmul(po.ap[:, 0:n], wo3[:, dt, d2 * 128:(d2 + 1) * 128], accb3[:, dt, :], start=(dt == 0), stop=(dt == 7)), [wo, accb], [po])
                P.op('dve', lambda e, d2=d2, po=po, xt3=xt3, m=m: e.scalar_tensor_tensor(out=xt3[:, d2, :], in0=po.ap[:, 0:n], scalar=m['GT'][:, d2:d2 + 1], in1=xt3[:, d2, :], op0=ALU.mult, op1=ALU.add), [po, xt] + m['bufs'], [xt])
            P.dma('sp', dr['XT'][:, c0:c0 + n].rearrange("(k p) t -> p k t", p=128), xt3, reads=[xt], writes=[self.chunk_buf(c0)])
        P.barrier()
        P.off = mark

    def layer(self, l):
        P = self.P
        mark = P.off
        mod = self.adaln(l)
        if self.stop is not None and len(self.stop) > 2 and self.stop[2] == 'skip':
            self.rwkv_prep(l)
            self.rwkv_dir(l, 0)
            if self.stop[0] != 'r0':
                self.rwkv_dir(l, 1)
                self.merge(l, mod)
            self.dump_scratch(('OS',))
            P.off = mark
            return
        self.ffn(l, 0, mod)
        if self.stop == ('ffn1', l):
            P.off = mark
            return
        self.mix_in(l, mod)
        self.attn(l)
        self.lru(l)
        self.rwkv_prep(l)
        if self.stop == ('rp', l):
            self.dump_scratch(('RS', 'KK', 'VT', 'TW', 'SG', 'KS', 'AD'))
            P.off = mark
            return
        self.rwkv_dir(l, 0)
        if self.stop == ('r0', l):
            self.dump_scratch(('OS',))
            P.off = mark
            return
        self.rwkv_dir(l, 1)
        if self.stop == ('r1', l):
            self.dump_scratch(('YR', 'OS'))
            P.off = mark
            return
        self.merge(l, mod)
        if self.stop == ('mix1', l):
            self.dump_scratch()
            P.off = mark
            return
        self.ffn(l, 1, mod)
        P.barrier()
        P.off = mark


_NC_CACHE = {}


def get_nc(T, depth, stop=None, debug=False):
    key = (T, depth, stop, debug)
    if key not in _NC_CACHE:
        b = Builder(T, depth, stop)
        b.debug = debug
        _NC_CACHE[key] = b.build()
    return _NC_CACHE[key]


WNAMES = ('ada_w', 'ada_b', 'norm_g', 'ffn_w_gu', 'ffn_w_d', 'w_in', 'attn_q_gain', 'attn_k_gain', 'attn_sink',
          'lru_conv_w', 'lru_conv_b', 'lru_gate_w', 'lru_gate_b', 'lru_lambda', 'branch_proj', 'w_out',
          'rwkv_mu', 'rwkv_w_up', 'rwkv_w0', 'rwkv_a_up', 'rwkv_a0', 'rwkv_g_up', 'rwkv_k_k', 'rwkv_k_a', 'rwkv_r_k', 'rwkv_ln_g', 'rwkv_ln_b')


def host_consts(T):
    f32 = np.float32
    c = {'ident': np.eye(128, dtype=f32)}
    p = np.arange(128)
    dh = p % 64
    axis = dh // 32
    f = dh % 16
    half = (dh % 32) // 16
    t = np.arange(T)
    pos = np.where(axis[:, None] == 0, (t // 64)[None, :], (t % 64)[None, :]).astype(np.float64)
    inv = 10000.0 ** (-(f.astype(np.float64)) / 16.0)
    ang = pos * inv[:, None]
    c['rope_cos'] = np.cos(ang).astype(f32)
    c['rope_sin'] = (np.sin(ang) * np.where(half == 0, -1.0, 1.0)[:, None]).astype(f32)
    partner = np.where(half == 0, p + 16, p - 16)
    rotm = np.zeros((128, 128), f32)
    rotm[partner, p] = 1.0
    c['rotm'] = rotm
    c['blk64'] = (p[:, None] // 64 == p[None, :] // 64).astype(f32)
    a = np.arange(128)
    c['mlo'] = np.tile((a[None, :] <= a[:, None]).astype(f32), (1, 4))
    c['mhi'] = np.tile((a[:, None] <= a[None, :]).astype(f32), (1, 4))
    sel = np.zeros((128, 2), f32)
    sel[:64, 0] = 1.0
    sel[64:, 1] = 1.0
    c['sel'] = sel
    for d_ in range(2):
        if d_ == 0:
            strict = (a[:, None] < a[None, :]).astype(f32)
        else:
            strict = (a[:, None] > a[None, :]).astype(f32)
        incl = strict + np.eye(128, dtype=f32)
        c['mask4_%d' % d_] = np.concatenate([strict, incl, strict, incl], axis=1)
        c['maskN_%d' % d_] = np.tile(strict.T, (1, 4))
        col = np.arange(512) % 128
        rst = np.ones((128, 512), f32)
        rst[:, col == (0 if d_ == 0 else 127)] = 0.0
        c['rst_%d' % d_] = rst
    return c


def run(inputs, T, depth, stop=None, ncores=8, debug=False):
    nc = get_nc(T, depth, stop, debug)
    f32 = np.float32
    shared = {k: np.ascontiguousarray(np.asarray(inputs[k], dtype=f32)) for k in WNAMES}
    shared.update(host_consts(T))
    in_maps = []
    for b in range(ncores):
        m = dict(shared)
        m['x'] = np.ascontiguousarray(inputs['x'][b, :T].astype(f32))
        m['ctx'] = np.ascontiguousarray(inputs['ctx'][b].astype(f32))
        m['cvec'] = np.ascontiguousarray(np.stack([inputs['c'][b], inputs['c_ctx']]).astype(f32))
        in_maps.append(m)
    res = run_bass_kernel_spmd(nc, in_maps, core_ids=list(range(ncores)))
    if debug:
        return res.results
    return np.stack([r['out'] for r in res.results])


def kernel(**inputs):
    T = inputs['x'].shape[1]
    full = run(inputs, T, 4)
    return np.ascontiguousarray(full[:, CTX:, :]).astype(np.float32)
```
